# Optimizing a Trainium2 kernel written in Bass

```python
import math, functools
import jax, jax.numpy as jnp
from jax import lax
import numpy as np

D_MODEL = 1024
BATCH = 2
SEQ = 8192
DEPTH = 2
DEC_BATCH = 32
DEC_SEQ = 8
PAST_LEN = 16384
PAGE_SIZE = 128

SSD_EXPAND = 2
D_INNER = SSD_EXPAND * D_MODEL
SSD_HEAD_DIM = 64
SSD_HEADS = D_INNER // SSD_HEAD_DIM
SSD_GROUPS = 4
SSD_HPG = SSD_HEADS // SSD_GROUPS
D_STATE = 128
CONV_WIDTH = 4
CONV_DIM = D_INNER + 2 * SSD_GROUPS * D_STATE
SSD_CHUNK = 128
N_HEADS = 16
KV_HEADS = 4
HEAD_DIM = 64
Q_PER_KV = N_HEADS // KV_HEADS
IDX_HEADS = 8
IDX_DIM = 64
TOPK_MAX = 256
Q_BLOCK = 128
ATT_SCALE = HEAD_DIM ** -0.5
IDX_W_SCALE = (IDX_HEADS ** -0.5) * (IDX_DIM ** -0.5)
D_FF = 2816
EPS = 1e-6
PROJ_SIZES = (D_INNER, CONV_DIM, SSD_HEADS, N_HEADS * HEAD_DIM, KV_HEADS * HEAD_DIM,
              KV_HEADS * HEAD_DIM, IDX_HEADS * IDX_DIM, IDX_DIM, IDX_HEADS, D_MODEL, D_MODEL)
PROJ_SPLITS = tuple(sum(PROJ_SIZES[:i + 1]) for i in range(len(PROJ_SIZES) - 1))
D_PROJ = sum(PROJ_SIZES)

kernel_name = 'hybrid_ssd_dsa_macaron_step'

F32 = jnp.float32


def rms_norm(x, g):
    x32 = x.astype(F32)
    y = x32 * lax.rsqrt(jnp.mean(x32 * x32, axis=-1, keepdims=True) + EPS)
    return (y * g.astype(F32)).astype(x.dtype)


def half_swiglu(x, g, w1, w2):
    a, b = jnp.split(rms_norm(x, g) @ w1, 2, axis=-1)
    return x + 0.5 * ((jax.nn.silu(a) * b) @ w2)


def gather_rows(rows, idx):
    return jax.vmap(lambda r, i: r[i])(rows, idx)


def causal_conv(xbc, buf, w, b):
    l = xbc.shape[1]
    xp = jnp.concatenate([buf.astype(xbc.dtype), xbc], axis=1)
    out = b
    for i in range(CONV_WIDTH):
        out = out + xp[:, i:i + l] * w[i]
    return jax.nn.silu(out), xp[:, -(CONV_WIDTH - 1):]


def ssd_scan(xs, dt, a, bm, cm, h0):
    b_, l = xs.shape[:2]
    q = math.gcd(l, SSD_CHUNK)
    c = l // q
    x = (xs.astype(F32) * dt[..., None]).reshape(b_, c, q, SSD_GROUPS, SSD_HPG, SSD_HEAD_DIM)
    la = (dt * a).reshape(b_, c, q, SSD_GROUPS, SSD_HPG).transpose(0, 1, 3, 4, 2)
    a_cs = jnp.cumsum(la, axis=-1)
    tril = jnp.tril(jnp.ones((q, q), bool))
    lmat = jnp.exp(jnp.where(tril, a_cs[..., :, None] - a_cs[..., None, :], -jnp.inf))
    bc = bm.astype(F32).reshape(b_, c, q, SSD_GROUPS, D_STATE)
    cc = cm.astype(F32).reshape(b_, c, q, SSD_GROUPS, D_STATE)
    cb = jnp.einsum('bclgn,bcsgn->bcgls', cc, bc)
    y_diag = jnp.einsum('bcgjls,bcsgjp->bclgjp', cb[:, :, :, None] * lmat, x)
    decay = jnp.exp(a_cs[..., -1:] - a_cs).transpose(0, 1, 4, 2, 3)
    states = jnp.einsum('bclgn,bclgjp->bcgjpn', bc, x * decay[..., None])
    chunk_decay = jnp.exp(a_cs[..., -1])

    def step(h, inp):
        st, dec = inp
        return dec[..., None, None] * h + st, h

    h_init = h0.astype(F32).reshape(b_, SSD_GROUPS, SSD_HPG, SSD_HEAD_DIM, D_STATE)
    h_fin, h_in = lax.scan(step, h_init, (jnp.moveaxis(states, 1, 0), jnp.moveaxis(chunk_decay, 1, 0)))
    h_in = jnp.moveaxis(h_in, 0, 1)
    y_off = jnp.einsum('bclgn,bcgjpn->bclgjp', cc, h_in) * jnp.exp(a_cs).transpose(0, 1, 4, 2, 3)[..., None]
    y = (y_diag + y_off).reshape(b_, l, SSD_HEADS, SSD_HEAD_DIM)
    return y, h_fin.reshape(b_, SSD_HEADS, SSD_HEAD_DIM, D_STATE)


def ssd_branch(z, xbc, dt_raw, buf, h0, conv_w, conv_b, dt_bias, a_log, d_skip, norm_g):
    xbc, new_buf = causal_conv(xbc, buf, conv_w, conv_b)
    xs, bm, cm = jnp.split(xbc, [D_INNER, D_INNER + SSD_GROUPS * D_STATE], axis=-1)
    b_, l = xs.shape[:2]
    xs = xs.reshape(b_, l, SSD_HEADS, SSD_HEAD_DIM)
    bm = bm.reshape(b_, l, SSD_GROUPS, D_STATE)
    cm = cm.reshape(b_, l, SSD_GROUPS, D_STATE)
    dt = jax.nn.softplus(dt_raw.astype(F32) + dt_bias.astype(F32))
    a = -jnp.exp(a_log.astype(F32))
    y, h_fin = ssd_scan(xs, dt, a, bm, cm, h0)
    y = y + d_skip.astype(F32)[:, None] * xs.astype(F32)
    y = y.reshape(b_, l, D_INNER) * jax.nn.silu(z.astype(F32))
    yg = y.reshape(b_, l, SSD_GROUPS, D_INNER // SSD_GROUPS)
    yg = yg * lax.rsqrt(jnp.mean(yg * yg, axis=-1, keepdims=True) + EPS)
    y = yg.reshape(b_, l, D_INNER) * norm_g.astype(F32)
    return y.astype(z.dtype), new_buf, h_fin


def indexer_scores(qi, wi, ki_all):
    dots = jnp.einsum('bthd,bsd->bths', qi.astype(F32), ki_all.astype(F32))
    return jnp.einsum('bths,bth->bts', jax.nn.relu(dots), wi.astype(F32) * IDX_W_SCALE)


def sparse_attend(q, kg, vg, valid):
    s = jnp.einsum('btkgd,btskd->btkgs', q.astype(F32), kg.astype(F32)) * ATT_SCALE
    s = jnp.where(valid[:, :, None, None, :], s, -jnp.inf)
    p = jax.nn.softmax(s, axis=-1)
    return jnp.einsum('btkgs,btskd->btkgd', p, vg.astype(F32)).astype(q.dtype)


def dsa_prompt(q, k, v, qi, ki, wi):
    b_, l = q.shape[:2]
    topk = min(TOPK_MAX, l // 4)
    nb = l // Q_BLOCK

    def to_blocks(t):
        return jnp.moveaxis(t.reshape((b_, nb, Q_BLOCK) + t.shape[2:]), 1, 0)

    key_pos = jnp.arange(l)
    q_pos = key_pos.reshape(nb, Q_BLOCK)

    def block(args):
        qb, qib, wib, tpos = args
        sc = indexer_scores(qib, wib, ki)
        sc = jnp.where((key_pos[None, :] <= tpos[:, None])[None], sc, -jnp.inf)
        _, idx = lax.top_k(sc, topk)
        valid = idx <= tpos[None, :, None]
        return sparse_attend(qb, gather_rows(k, idx), gather_rows(v, idx), valid)

    out = lax.map(block, (to_blocks(q), to_blocks(qi), to_blocks(wi), q_pos))
    return jnp.moveaxis(out, 0, 1).reshape(b_, l, N_HEADS * HEAD_DIM)


def dsa_sample(q, k, v, qi, ki, wi, ck, cv, ci, page_table):
    db, t = q.shape[:2]
    ps = ck.shape[1]
    past = page_table.shape[1] * ps
    l = past + t
    topk = min(TOPK_MAX, l // 4)
    ki_past = ci[page_table].reshape(db, past, IDX_DIM).astype(ki.dtype)
    ki_all = jnp.concatenate([ki_past, ki], axis=1)
    sc = indexer_scores(qi, wi, ki_all)
    q_pos = past + jnp.arange(t)
    key_pos = jnp.arange(l)
    sc = jnp.where(key_pos[None, None, :] <= q_pos[None, :, None], sc, -jnp.inf)
    _, idx = lax.top_k(sc, topk)
    in_past = (idx < past)[..., None, None]
    pidx = jnp.minimum(idx, past - 1)
    phys = gather_rows(page_table, pidx // ps)
    off = pidx % ps
    nidx = jnp.clip(idx - past, 0, t - 1)
    kg = jnp.where(in_past, ck[phys, off].astype(k.dtype), gather_rows(k, nidx))
    vg = jnp.where(in_past, cv[phys, off].astype(v.dtype), gather_rows(v, nidx))
    valid = idx <= q_pos[None, :, None]
    return sparse_attend(q, kg, vg, valid).reshape(db, t, N_HEADS * HEAD_DIM)


def token_mixer(h, w_in, conv_w, conv_b, dt_bias, a_log, d_skip, ssd_norm,
                w_br_ssd, w_br_attn, w_out, conv_buf, h0, attend):
    b_, l = h.shape[:2]
    z, xbc, dt_raw, q, k, v, qi, ki, wi, g_s, g_a = jnp.split(h @ w_in, PROJ_SPLITS, axis=-1)
    y_ssd, new_buf, h_fin = ssd_branch(z, xbc, dt_raw, conv_buf, h0, conv_w, conv_b,
                                       dt_bias, a_log, d_skip, ssd_norm)
    q = q.reshape(b_, l, KV_HEADS, Q_PER_KV, HEAD_DIM)
    k = k.reshape(b_, l, KV_HEADS, HEAD_DIM)
    v = v.reshape(b_, l, KV_HEADS, HEAD_DIM)
    qi = qi.reshape(b_, l, IDX_HEADS, IDX_DIM)
    y_attn = attend(q, k, v, qi, ki, wi)
    merged = jax.nn.sigmoid(g_s) * (y_ssd @ w_br_ssd) + jax.nn.sigmoid(g_a) * (y_attn @ w_br_attn)
    return merged @ w_out, new_buf, h_fin, k, v, ki


def setup_inputs(seed: int = 0) -> dict:
    key = jax.random.key(seed)
    ks = jax.random.split(key, 32)
    n_pages = PAST_LEN // PAGE_SIZE
    n_used = DEC_BATCH * n_pages
    n_pool = n_used + n_used // 4

    def nrm(k, shape, scale):
        return jax.random.normal(k, shape, F32) * scale

    def gain(k, shape):
        return 1.0 + 0.02 * jax.random.normal(k, shape, F32)

    dt0 = jnp.exp(jax.random.uniform(ks[14], (DEPTH, SSD_HEADS), F32, math.log(1e-3), math.log(1e-1)))
    return {
        'x_prompt': nrm(ks[0], (BATCH, SEQ, D_MODEL), 1.0),
        'x_sample': nrm(ks[1], (DEC_BATCH, DEC_SEQ, D_MODEL), 1.0),
        'cache_k': nrm(ks[2], (DEPTH, n_pool, PAGE_SIZE, KV_HEADS, HEAD_DIM), 1.0),
        'cache_v': nrm(ks[3], (DEPTH, n_pool, PAGE_SIZE, KV_HEADS, HEAD_DIM), 1.0),
        'cache_idx_k': nrm(ks[4], (DEPTH, n_pool, PAGE_SIZE, IDX_DIM), 1.0),
        'state_ssm': nrm(ks[5], (DEPTH, DEC_BATCH, SSD_HEADS, SSD_HEAD_DIM, D_STATE), 0.1),
        'state_conv': nrm(ks[6], (DEPTH, DEC_BATCH, CONV_WIDTH - 1, CONV_DIM), 1.0),
        'page_table': jax.random.permutation(ks[7], n_pool)[:n_used].reshape(DEC_BATCH, n_pages).astype(jnp.int32),
        'ffn1_norm': gain(ks[8], (DEPTH, D_MODEL)),
        'ffn1_w1': nrm(ks[9], (DEPTH, D_MODEL, 2 * D_FF), D_MODEL ** -0.5),
        'ffn1_w2': nrm(ks[10], (DEPTH, D_FF, D_MODEL), D_FF ** -0.5),
        'mix_norm': gain(ks[11], (DEPTH, D_MODEL)),
        'w_in': nrm(ks[12], (DEPTH, D_MODEL, D_PROJ), D_MODEL ** -0.5),
        'conv_w': nrm(ks[13], (DEPTH, CONV_WIDTH, CONV_DIM), CONV_WIDTH ** -0.5),
        'conv_b': nrm(ks[15], (DEPTH, CONV_DIM), 0.01),
        'dt_bias': dt0 + jnp.log(-jnp.expm1(-dt0)),
        'a_log': jnp.log(jax.random.uniform(ks[16], (DEPTH, SSD_HEADS), F32, 1.0, 16.0)),
        'd_skip': 1.0 + 0.1 * jax.random.normal(ks[17], (DEPTH, SSD_HEADS), F32),
        'ssd_norm': gain(ks[18], (DEPTH, D_INNER)),
        'w_branch_ssd': nrm(ks[19], (DEPTH, D_INNER, D_MODEL), D_INNER ** -0.5),
        'w_branch_attn': nrm(ks[20], (DEPTH, N_HEADS * HEAD_DIM, D_MODEL), (N_HEADS * HEAD_DIM) ** -0.5),
        'w_out': nrm(ks[21], (DEPTH, D_MODEL, D_MODEL), D_MODEL ** -0.5),
        'ffn2_norm': gain(ks[22], (DEPTH, D_MODEL)),
        'ffn2_w1': nrm(ks[23], (DEPTH, D_MODEL, 2 * D_FF), D_MODEL ** -0.5),
        'ffn2_w2': nrm(ks[24], (DEPTH, D_FF, D_MODEL), D_FF ** -0.5),
        'final_norm': gain(ks[25], (D_MODEL,)),
    }


def reference(x_prompt, x_sample, cache_k, cache_v, cache_idx_k, state_ssm, state_conv, page_table,
              ffn1_norm, ffn1_w1, ffn1_w2, mix_norm, w_in, conv_w, conv_b, dt_bias, a_log, d_skip,
              ssd_norm, w_branch_ssd, w_branch_attn, w_out, ffn2_norm, ffn2_w1, ffn2_w2, final_norm):
    yp, ys = x_prompt, x_sample
    bp = x_prompt.shape[0]
    kp_l, vp_l, ip_l, sp_l, cp_l = [], [], [], [], []
    ks_l, vs_l, is_l, ss_l, cs_l = [], [], [], [], []
    for l in range(DEPTH):
        yp = half_swiglu(yp, ffn1_norm[l], ffn1_w1[l], ffn1_w2[l])
        ys = half_swiglu(ys, ffn1_norm[l], ffn1_w1[l], ffn1_w2[l])
        mix_w = (w_in[l], conv_w[l], conv_b[l], dt_bias[l], a_log[l], d_skip[l], ssd_norm[l],
                 w_branch_ssd[l], w_branch_attn[l], w_out[l])
        buf0 = jnp.zeros((bp, CONV_WIDTH - 1, CONV_DIM), x_prompt.dtype)
        h00 = jnp.zeros((bp, SSD_HEADS, SSD_HEAD_DIM, D_STATE), F32)
        mp, cbp, hfp, kp, vp, kip = token_mixer(rms_norm(yp, mix_norm[l]), *mix_w, buf0, h00, dsa_prompt)
        att_s = functools.partial(dsa_sample, ck=cache_k[l], cv=cache_v[l], ci=cache_idx_k[l], page_table=page_table)
        ms, cbs, hfs, kss, vss, kis = token_mixer(rms_norm(ys, mix_norm[l]), *mix_w, state_conv[l], state_ssm[l], att_s)
        yp = yp + mp
        ys = ys + ms
        yp = half_swiglu(yp, ffn2_norm[l], ffn2_w1[l], ffn2_w2[l])
        ys = half_swiglu(ys, ffn2_norm[l], ffn2_w1[l], ffn2_w2[l])
        kp_l.append(kp); vp_l.append(vp); ip_l.append(kip); sp_l.append(hfp); cp_l.append(cbp)
        ks_l.append(kss); vs_l.append(vss); is_l.append(kis); ss_l.append(hfs); cs_l.append(cbs)
    y_prompt = rms_norm(yp, final_norm)
    y_sample = rms_norm(ys, final_norm)
    return (y_prompt, y_sample,
            jnp.stack(kp_l), jnp.stack(vp_l), jnp.stack(ip_l), jnp.stack(sp_l), jnp.stack(cp_l),
            jnp.stack(ks_l), jnp.stack(vs_l), jnp.stack(is_l), jnp.stack(ss_l), jnp.stack(cs_l))
```

```python
import numpy as np
from contextlib import ExitStack
import concourse.bass as bass
import concourse.mybir as mybir
from concourse.bass_utils import run_bass_kernel_spmd

F32 = mybir.dt.float32
BF16 = mybir.dt.bfloat16
I32 = mybir.dt.int32
AF = mybir.ActivationFunctionType
ALU = mybir.AluOpType
AX = mybir.AxisListType

NDMA_SEM = 8
D = 1024
DFF = 2816
EPS = 1e-6


class Sched:
    ENGS = ("pe", "act", "dve", "pool", "sp")

    def __init__(self, nc, stack):
        self.nc = nc
        self.ops = {e: [] for e in self.ENGS}
        self.cnt = {e: 0 for e in self.ENGS}
        self.esem = {e: stack.enter_context(nc.semaphore("es_" + e)) for e in self.ENGS if e != "sp"}
        self.dsem = {q: [stack.enter_context(nc.semaphore(f"ds_{q}{i}")) for i in range(NDMA_SEM)]
                     for q in ("sp", "act", "pool")}
        self.dcnt = {q: 0 for q in self.dsem}
        self.dval = {q: [0] * NDMA_SEM for q in self.dsem}
        self.lastw = {}
        self.readers = {}
        self.seen = {e: {} for e in self.ENGS}
        self.out_tokens = []

    def _sem(self, name):
        if name[0] == "E":
            return self.esem[name[1:]]
        q, i = name[1:].split(":")
        return self.dsem[q][int(i)]

    def _deps(self, eng, reads, writes):
        toks = set()
        for k in reads:
            t = self.lastw.get(k)
            if t is not None:
                toks.add(t)
        for k in writes:
            t = self.lastw.get(k)
            if t is not None:
                toks.add(t)
            for r in self.readers.get(k, ()):
                toks.add(r)
        best = {}
        for (s, v, pe) in toks:
            if pe == "pe" and eng == "pe":
                continue
            if best.get(s, 0) < v:
                best[s] = v
        waits = []
        for s, v in best.items():
            if self.seen[eng].get(s, 0) >= v:
                continue
            self.seen[eng][s] = v
            waits.append((s, v))
        return waits

    def _commit(self, tok, reads, writes):
        for k in reads:
            self.readers.setdefault(k, []).append(tok)
        for k in writes:
            self.lastw[k] = tok
            self.readers[k] = []

    lim = None
    nrec = 0

    def _skip(self):
        if self.lim is None:
            return False
        self.nrec += 1
        return self.nrec > self.lim

    def op(self, eng, fn, reads=(), writes=()):
        if self._skip():
            return None
        waits = self._deps(eng, reads, writes)
        self.cnt[eng] += 1
        tok = ("E" + eng, self.cnt[eng], eng)
        self.ops[eng].append((waits, fn, (tok[0], 1)))
        self._commit(tok, reads, writes)
        return tok

    def dma(self, q, fn, reads=(), writes=(), is_output=False):
        if self._skip():
            return None
        waits = self._deps(q, reads, writes)
        i = self.dcnt[q] % NDMA_SEM
        self.dcnt[q] += 1
        sname = f"D{q}:{i}"
        prev = self.dval[q][i]
        if prev > 0 and self.seen[q].get(sname, 0) < prev:
            self.seen[q][sname] = prev
            waits.append((sname, prev))
        self.dval[q][i] = prev + 16
        tok = (sname, prev + 16, "dma")
        self.ops[q].append((waits, fn, (sname, 16)))
        self._commit(tok, reads, writes)
        if is_output:
            self.out_tokens.append(tok)
        return tok

    def barrier(self):
        allw = {}
        for e in self.ENGS:
            if e != "sp" and self.cnt[e] > 0:
                allw["E" + e] = self.cnt[e]
        for q in self.dsem:
            for i in range(NDMA_SEM):
                if self.dval[q][i] > 0:
                    allw[f"D{q}:{i}"] = self.dval[q][i]
        for e in self.ENGS:
            waits = []
            for s_, v in allw.items():
                if self.seen[e].get(s_, 0) < v:
                    self.seen[e][s_] = v
                    waits.append((s_, v))
            if waits:
                self.ops[e].append((waits, None, None))

    def finish(self):
        best = {}
        for q in self.dsem:
            for i in range(NDMA_SEM):
                if self.dval[q][i] > 0:
                    best[f"D{q}:{i}"] = self.dval[q][i]
        self.ops["sp"].append((list(best.items()), None, None))

    def emit(self):
        nc = self.nc
        with nc.Block() as block:
            def run(engname):
                def body(eng):
                    for waits, fn, inc in self.ops[engname]:
                        for (s, v) in waits:
                            eng.wait_ge(self._sem(s), v)
                        if fn is None:
                            continue
                        ins = fn(eng)
                        ins.then_inc(self._sem(inc[0]), inc[1])
                return body
            block.tensor(run("pe"))
            block.scalar(run("act"))
            block.vector(run("dve"))
            block.gpsimd(run("pool"))
            block.sync(run("sp"))


class Ctx:
    pass


_UID = [0]


def sb(nc, st, name, shape, dt):
    _UID[0] += 1
    return st.enter_context(nc.sbuf_tensor(f"{name}_{_UID[0]}", list(shape), dt))


def ps(nc, st, name, shape, dt=F32):
    _UID[0] += 1
    return st.enter_context(nc.psum_tensor(f"{name}_{_UID[0]}", list(shape), dt))


def phase_to_fm(c, src, t_src0, ntok, t_dst0):
    nc, S = c.nc, c.S
    S.barrier()
    with ExitStack() as st:
        xin = [sb(nc, st, f"p0x{i}", [128, D], F32) for i in range(2)]
        xo = [sb(nc, st, f"p0o{i}", [128, 8, 128], F32) for i in range(2)]
        pt = [ps(nc, st, f"p0p{i}", [128, 8, 128], F32) for i in range(2)]
        nt = (ntok + 127) // 128
        for i in range(nt):
            n = min(128, ntok - i * 128)
            b = i % 2
            r0 = t_src0 + i * 128
            S.dma("sp", lambda e, b=b, r0=r0, n=n: e.dma_start(out=xin[b][0:n, :], in_=src[r0:r0 + n, :]),
                  writes=[("p0x", b)])
            for dc in range(8):
                S.op("pe", lambda e, b=b, dc=dc, n=n: e.transpose(pt[b][:, dc, 0:n], xin[b][0:n, dc * 128:(dc + 1) * 128],
                                                                 c.ident_f[0:n, 0:n]),
                     reads=[("p0x", b), "ident_f"], writes=[("p0p", b)])
            S.op("act", lambda e, b=b, n=n: e.copy(out=xo[b][:, :, 0:n], in_=pt[b][:, :, 0:n]),
                 reads=[("p0p", b)], writes=[("p0o", b)])
            d0 = t_dst0 + i * 128
            S.dma("sp", lambda e, b=b, d0=d0, n=n: e.dma_start(
                out=c.xT[:, d0:d0 + n].rearrange("(dc p) t -> p dc t", p=128), in_=xo[b][:, :, 0:n]),
                reads=[("p0o", b)], writes=[("xT", d0 // 128)])


def rmsnorm_fm(c, st, xs, xs_key, T, gain_ap, hnT, hn_key, tag):
    nc, S = c.nc, c.S
    sq = [sb(nc, st, f"{tag}sq{i}", [128, T], F32) for i in range(2)]
    rstd = sb(nc, st, f"{tag}rstd", [128, T], F32)
    nh = (T + 511) // 512
    pss = ps(nc, st, f"{tag}pss", [128, nh, 512], F32)
    for dc in range(8):
        b = dc % 2
        S.op("act", lambda e, b=b, dc=dc: e.activation(out=sq[b][:, :], in_=xs[:, dc, :], func=AF.Square),
             reads=[xs_key], writes=[(tag + "sq", b)])
        for h in range(nh):
            w = min(512, T - h * 512)
            S.op("pe", lambda e, b=b, dc=dc, h=h, w=w: e.matmul(pss[:, h, 0:w], lhsT=c.ones_f[:, :], rhs=sq[b][:, h * 512:h * 512 + w],
                                                              start=(dc == 0), stop=(dc == 7)),
                 reads=[(tag + "sq", b), "ones_f"], writes=[(tag + "pss")])
    for h in range(nh):
        w = min(512, T - h * 512)
        S.op("act", lambda e, h=h, w=w: e.activation(out=rstd[:, h * 512:h * 512 + w], in_=pss[:, h, 0:w], func=AF.Sqrt,
                                                    scale=1.0 / D, bias=c.eps_t[:, 0:1]),
             reads=[(tag + "pss"), "consts"], writes=[(tag + "rstd")])
    S.op("dve", lambda e: e.reciprocal(out=rstd[:, :], in_=rstd[:, :]), reads=[(tag + "rstd")], writes=[(tag + "rstd")])
    for dc in range(8):
        S.op("dve", lambda e, dc=dc: e.scalar_tensor_tensor(out=hnT[:, dc, :], in0=xs[:, dc, :], scalar=gain_ap[:, dc:dc + 1],
                                                          in1=rstd[:, :], op0=ALU.mult, op1=ALU.mult),
             reads=[xs_key, (tag + "rstd"), "gains"], writes=[hn_key])


def phase_ffn(c, layer, which, tiles):
    nc, S = c.nc, c.S
    S.barrier()
    w1 = c.w1[which][layer]
    w2 = c.w2[which][layer]
    gain = c.gains[:, (layer * 3 + (0 if which == 0 else 2)) * 8:(layer * 3 + (0 if which == 0 else 2)) * 8 + 8]
    TS = c.TS
    w1v = w1.rearrange("(ko ki) n -> ki ko n", ki=128)
    w2v = w2.rearrange("(fo fi) n -> fi fo n", fi=128)
    NF = DFF // 128
    with ExitStack() as st:
        xs = sb(nc, st, "f_xs", [128, 8, TS], F32)
        hnT = sb(nc, st, "f_hnT", [128, 8, TS], BF16)
        actT = sb(nc, st, "f_actT", [128, NF, TS], BF16)
        w1s = [sb(nc, st, f"f_w1s{i}", [128, 2, 8, 128], F32) for i in range(2)]
        w1b = [sb(nc, st, f"f_w1b{i}", [128, 2, 8, 128], BF16) for i in range(2)]
        w2s = [sb(nc, st, f"f_w2s{i}", [128, NF, 128], F32) for i in range(2)]
        w2b = [sb(nc, st, f"f_w2b{i}", [128, NF, 128], BF16) for i in range(2)]
        sa = [sb(nc, st, f"f_sa{i}", [128, 512], F32) for i in range(2)]
        xo = [sb(nc, st, f"f_xo{i}", [128, 512], F32) for i in range(2)]
        pa = [ps(nc, st, f"f_pa{i}", [128, 512], F32) for i in range(2)]
        pb = [ps(nc, st, f"f_pb{i}", [128, 512], F32) for i in range(2)]
        po = [ps(nc, st, f"f_po{i}", [128, 512], F32) for i in range(2)]
        wi = 0
        w2i = 0
        it = 0
        for (t0, T) in tiles:
            S.dma("sp", lambda e, t0=t0, T=T: e.dma_start(out=xs[:, :, 0:T],
                                                         in_=c.xT[:, t0:t0 + T].rearrange("(dc p) t -> p dc t", p=128)),
                  reads=[("xT", k) for k in range(t0 // 128, (t0 + T + 127) // 128)], writes=["f_xs"])
            with ExitStack() as st2:
                rmsnorm_fm(c, st2, xs[:, :, 0:T], "f_xs", T, gain, hnT[:, :, 0:T], "f_hnT", "fn")
            nh = (T + 511) // 512
            for j in range(NF):
                b = wi % 2
                wi += 1
                S.dma("sp", lambda e, b=b, j=j: e.dma_start(out=w1s[b][:, 0, :, :], in_=w1v[:, :, j * 128:(j + 1) * 128]),
                      writes=[("f_w1s", b)])
                S.dma("sp", lambda e, b=b, j=j: e.dma_start(out=w1s[b][:, 1, :, :], in_=w1v[:, :, DFF + j * 128:DFF + (j + 1) * 128]),
                      writes=[("f_w1s", b)])
                S.op("pool", lambda e, b=b: e.tensor_copy(out=w1b[b][:, :, :, :], in_=w1s[b][:, :, :, :]),
                     reads=[("f_w1s", b)], writes=[("f_w1b", b)])
                for h in range(nh):
                    w = min(512, T - h * 512)
                    pbuf = it % 2
                    it += 1
                    for ko in range(8):
                        S.op("pe", lambda e, b=b, ko=ko, h=h, w=w, pbuf=pbuf: e.matmul(
                            pa[pbuf][:, 0:w], lhsT=w1b[b][:, 0, ko, :], rhs=hnT[:, ko, h * 512:h * 512 + w],
                            start=(ko == 0), stop=(ko == 7)),
                            reads=[("f_w1b", b), "f_hnT"], writes=[("f_pa", pbuf)])
                    for ko in range(8):
                        S.op("pe", lambda e, b=b, ko=ko, h=h, w=w, pbuf=pbuf: e.matmul(
                            pb[pbuf][:, 0:w], lhsT=w1b[b][:, 1, ko, :], rhs=hnT[:, ko, h * 512:h * 512 + w],
                            start=(ko == 0), stop=(ko == 7)),
                            reads=[("f_w1b", b), "f_hnT"], writes=[("f_pb", pbuf)])
                    S.op("act", lambda e, w=w, pbuf=pbuf: e.activation(out=sa[pbuf][:, 0:w], in_=pa[pbuf][:, 0:w], func=AF.Silu),
                         reads=[("f_pa", pbuf)], writes=[("f_sa", pbuf)])
                    S.op("dve", lambda e, j=j, h=h, w=w, pbuf=pbuf: e.tensor_tensor(
                        out=actT[:, j, h * 512:h * 512 + w], in0=sa[pbuf][:, 0:w], in1=pb[pbuf][:, 0:w], op=ALU.mult),
                        reads=[("f_sa", pbuf), ("f_pb", pbuf)], writes=[("f_actT", j)])
            for cc in range(8):
                b = w2i % 2
                w2i += 1
                S.dma("sp", lambda e, b=b, cc=cc: e.dma_start(out=w2s[b][:, :, :], in_=w2v[:, :, cc * 128:(cc + 1) * 128]),
                      writes=[("f_w2s", b)])
                S.op("pool", lambda e, b=b: e.tensor_copy(out=w2b[b][:, :, :], in_=w2s[b][:, :, :]),
                     reads=[("f_w2s", b)], writes=[("f_w2b", b)])
                for h in range(nh):
                    w = min(512, T - h * 512)
                    pbuf = it % 2
                    it += 1
                    for fo in range(NF):
                        S.op("pe", lambda e, b=b, fo=fo, h=h, w=w, pbuf=pbuf: e.matmul(
                            po[pbuf][:, 0:w], lhsT=w2b[b][:, fo, :], rhs=actT[:, fo, h * 512:h * 512 + w],
                            start=(fo == 0), stop=(fo == NF - 1)),
                            reads=[("f_w2b", b), ("f_actT", fo)], writes=[("f_po", pbuf)])
                    S.op("dve", lambda e, cc=cc, h=h, w=w, pbuf=pbuf: e.scalar_tensor_tensor(
                        out=xo[pbuf][:, 0:w], in0=po[pbuf][:, 0:w], scalar=0.5, in1=xs[:, cc, h * 512:h * 512 + w],
                        op0=ALU.mult, op1=ALU.add),
                        reads=[("f_po", pbuf), "f_xs"], writes=[("f_xo", pbuf)])
                    a0 = t0 + h * 512
                    S.dma("sp", lambda e, cc=cc, a0=a0, w=w, pbuf=pbuf: e.dma_start(
                        out=c.xT[cc * 128:(cc + 1) * 128, a0:a0 + w], in_=xo[pbuf][:, 0:w]),
                        reads=[("f_xo", pbuf)], writes=[("xT", k) for k in range(a0 // 128, (a0 + w + 127) // 128)])


C_Z, C_XBC, C_DT, C_Q, C_K, C_V, C_QI, C_KI, C_WI, C_GS, C_GA = 0, 2048, 5120, 5152, 6176, 6432, 6688, 7200, 7264, 7272, 8296
DPROJ = 9320
IDX_W_SCALE = (8 ** -0.5) * (64 ** -0.5)


def inproj_slabs():
    sl = []
    for i in range(4):
        sl.append((C_Z + 512 * i, 512, [("tm", 0, 512, "z", 512 * i)]))
    for i in range(6):
        sl.append((C_XBC + 512 * i, 512, [("fm", 128 * j, 128, "xbc", 512 * i + 128 * j) for j in range(4)]))
    sl.append((C_DT, 32, [("tm", 0, 32, "dt", 0)]))
    for i in range(2):
        sl.append((C_Q + 512 * i, 512, [("fm", 64 * j, 64, "q", 8 * i + j) for j in range(8)]))
    sl.append((C_K, 512, [("tm", 0, 512, "kv", 0)] + [("fm", 64 * j, 64, "k", j) for j in range(4)]))
    sl.append((C_QI, 512, [("fm", 64 * j, 64, "qi", j) for j in range(8)]))
    sl.append((C_KI, 72, [("tm", 0, 72, "kiw", 0), ("fm", 0, 64, "ki", 0)]))
    for i in range(2):
        sl.append((C_GS + 512 * i, 512, [("fm", 128 * j, 128, "gs", 512 * i + 128 * j) for j in range(4)]))
    for i in range(2):
        sl.append((C_GA + 512 * i, 512, [("fm", 128 * j, 128, "ga", 512 * i + 128 * j) for j in range(4)]))
    return sl


def tkeys(name, a, b):
    return [(name, k) for k in range(a // 128, (b + 127) // 128)]


def phase_inproj(c, layer, tiles):
    nc, S = c.nc, c.S
    S.barrier()
    SEQ = c.SEQ
    wv = c.w_in[layer].rearrange("(ko ki) n -> ki ko n", ki=128)
    gain = c.gains[:, (layer * 3 + 1) * 8:(layer * 3 + 1) * 8 + 8]
    TS = c.TS
    slabs = inproj_slabs()
    with ExitStack() as st:
        xs = sb(nc, st, "a_xs", [128, 8, TS], F32)
        hnT = sb(nc, st, "a_hnT", [128, 8, TS], BF16)
        wS = [sb(nc, st, f"a_wS{i}", [128, 8, 512], F32) for i in range(2)]
        wB = [sb(nc, st, f"a_wB{i}", [128, 8, 512], BF16) for i in range(2)]
        sf = [sb(nc, st, f"a_sf{i}", [128, 512], F32) for i in range(3)]
        sh = [sb(nc, st, f"a_sh{i}", [128, 512], BF16) for i in range(3)]
        pf = [ps(nc, st, f"a_pf{i}", [128, 512], F32) for i in range(3)]
        wi_ = 0
        it = 0
        for (t0, T) in tiles:
            is_s = t0 >= SEQ
            S.dma("sp", lambda e, t0=t0, T=T: e.dma_start(out=xs[:, :, 0:T],
                                                         in_=c.xT[:, t0:t0 + T].rearrange("(dc p) t -> p dc t", p=128)),
                  reads=tkeys("xT", t0, t0 + T), writes=["a_xs"])
            with ExitStack() as st2:
                rmsnorm_fm(c, st2, xs[:, :, 0:T], "a_xs", T, gain, hnT[:, :, 0:T], "a_hnT", "an")
            for (c0, ncol, jobs) in slabs:
                b = wi_ % 2
                wi_ += 1
                S.dma("sp", lambda e, b=b, c0=c0, ncol=ncol: e.dma_start(out=wS[b][:, :, 0:ncol], in_=wv[:, :, c0:c0 + ncol]),
                      writes=[("a_wS", b)])
                S.op("pool", lambda e, b=b, ncol=ncol: e.tensor_copy(out=wB[b][:, :, 0:ncol], in_=wS[b][:, :, 0:ncol]),
                     reads=[("a_wS", b)], writes=[("a_wB", b)])
                for (kind, off, n, name, idx) in jobs:
                    if kind == "fm":
                        for h in range((T + 511) // 512):
                            w = min(512, T - h * 512)
                            a0 = t0 + h * 512
                            pb_ = it % 3
                            it += 1
                            for ko in range(8):
                                S.op("pe", lambda e, b=b, ko=ko, off=off, n=n, h=h, w=w, pb_=pb_: e.matmul(
                                    pf[pb_][0:n, 0:w], lhsT=wB[b][:, ko, off:off + n], rhs=hnT[:, ko, h * 512:h * 512 + w],
                                    start=(ko == 0), stop=(ko == 7)),
                                    reads=[("a_wB", b), "a_hnT"], writes=[("a_pf", pb_)])
                            if name == "xbc":
                                S.op("act", lambda e, n=n, w=w, pb_=pb_: e.copy(out=sf[pb_][0:n, 0:w], in_=pf[pb_][0:n, 0:w]),
                                     reads=[("a_pf", pb_)], writes=[("a_sf", pb_)])
                                S.dma("sp", lambda e, idx=idx, a0=a0, w=w, pb_=pb_: e.dma_start(
                                    out=c.xbcT[idx:idx + 128, a0:a0 + w], in_=sf[pb_][:, 0:w]),
                                    reads=[("a_sf", pb_)], writes=tkeys("xbcT", a0, a0 + w))
                            elif name in ("gs", "ga"):
                                dst = c.gsT if name == "gs" else c.gaT
                                S.op("act", lambda e, n=n, w=w, pb_=pb_: e.activation(out=sf[pb_][0:n, 0:w], in_=pf[pb_][0:n, 0:w],
                                                                                    func=AF.Sigmoid),
                                     reads=[("a_pf", pb_)], writes=[("a_sf", pb_)])
                                S.dma("sp", lambda e, dst=dst, idx=idx, a0=a0, w=w, pb_=pb_: e.dma_start(
                                    out=dst[idx:idx + 128, a0:a0 + w], in_=sf[pb_][:, 0:w]),
                                    reads=[("a_sf", pb_)], writes=tkeys(name + "T", a0, a0 + w))
                            else:
                                dst = {"q": c.qT, "k": c.kT, "qi": c.qiT, "ki": c.kiT}[name]
                                scl = 0.125 if name == "q" else 1.0
                                S.op("act", lambda e, n=n, w=w, pb_=pb_, scl=scl: e.mul(out=sh[pb_][0:n, 0:w], in_=pf[pb_][0:n, 0:w],
                                                                                      mul=scl),
                                     reads=[("a_pf", pb_)], writes=[("a_sh", pb_)])
                                S.dma("sp", lambda e, dst=dst, idx=idx, a0=a0, w=w, pb_=pb_: e.dma_start(
                                    out=dst[idx, :, a0:a0 + w], in_=sh[pb_][0:64, 0:w]),
                                    reads=[("a_sh", pb_)], writes=tkeys(name + "T", a0, a0 + w))
                    else:
                        for i in range((T + 127) // 128):
                            nt_ = min(128, T - i * 128)
                            a0 = t0 + i * 128
                            pb_ = it % 3
                            it += 1
                            for ko in range(8):
                                S.op("pe", lambda e, b=b, ko=ko, off=off, n=n, i=i, nt_=nt_, pb_=pb_: e.matmul(
                                    pf[pb_][0:nt_, 0:n], lhsT=hnT[:, ko, i * 128:i * 128 + nt_], rhs=wB[b][:, ko, off:off + n],
                                    start=(ko == 0), stop=(ko == 7)),
                                    reads=[("a_wB", b), "a_hnT"], writes=[("a_pf", pb_)])
                            S.op("act", lambda e, n=n, nt_=nt_, pb_=pb_: e.copy(out=sf[pb_][0:nt_, 0:n], in_=pf[pb_][0:nt_, 0:n]),
                                 reads=[("a_pf", pb_)], writes=[("a_sf", pb_)])
                            if name == "z":
                                S.dma("sp", lambda e, idx=idx, a0=a0, nt_=nt_, pb_=pb_: e.dma_start(
                                    out=c.zS[a0:a0 + nt_, idx:idx + 512], in_=sf[pb_][0:nt_, 0:512]),
                                    reads=[("a_sf", pb_)], writes=tkeys("zS", a0, a0 + nt_))
                            elif name == "dt":
                                S.dma("sp", lambda e, a0=a0, nt_=nt_, pb_=pb_: e.dma_start(
                                    out=c.dtS[a0:a0 + nt_, :], in_=sf[pb_][0:nt_, 0:32]),
                                    reads=[("a_sf", pb_)], writes=tkeys("dtS", a0, a0 + nt_))
                            elif name == "kv":
                                ko_, vo_ = (c.ok_s, c.ov_s) if is_s else (c.ok_p, c.ov_p)
                                r0 = a0 - SEQ if is_s else a0
                                S.dma("sp", lambda e, ko_=ko_, r0=r0, nt_=nt_, pb_=pb_: e.dma_start(
                                    out=ko_[layer, r0:r0 + nt_, :], in_=sf[pb_][0:nt_, 0:256]),
                                    reads=[("a_sf", pb_)], writes=[("ok", layer, a0)])
                                S.dma("sp", lambda e, vo_=vo_, r0=r0, nt_=nt_, pb_=pb_: e.dma_start(
                                    out=vo_[layer, r0:r0 + nt_, :], in_=sf[pb_][0:nt_, 256:512]),
                                    reads=[("a_sf", pb_)], writes=[("ov", layer, a0)])
                                S.op("dve", lambda e, nt_=nt_, pb_=pb_: e.tensor_copy(out=sh[pb_][0:nt_, 0:256], in_=sf[pb_][0:nt_, 256:512]),
                                     reads=[("a_sf", pb_)], writes=[("a_sh", pb_)])
                                S.dma("sp", lambda e, a0=a0, nt_=nt_, pb_=pb_: e.dma_start(
                                    out=c.vB[a0:a0 + nt_, :], in_=sh[pb_][0:nt_, 0:256]),
                                    reads=[("a_sh", pb_)], writes=tkeys("vB", a0, a0 + nt_))
                            elif name == "kiw":
                                io_ = c.oi_s if is_s else c.oi_p
                                r0 = a0 - SEQ if is_s else a0
                                S.dma("sp", lambda e, io_=io_, r0=r0, nt_=nt_, pb_=pb_: e.dma_start(
                                    out=io_[layer, r0:r0 + nt_, :], in_=sf[pb_][0:nt_, 0:64]),
                                    reads=[("a_sf", pb_)], writes=[("oi", layer, a0)])
                                S.op("dve", lambda e, nt_=nt_, pb_=pb_: e.tensor_scalar(
                                    out=sf[pb_][0:nt_, 64:72], in0=sf[pb_][0:nt_, 64:72], scalar1=IDX_W_SCALE, scalar2=None, op0=ALU.mult),
                                    reads=[("a_sf", pb_)], writes=[("a_sf", pb_)])
                                S.dma("sp", lambda e, a0=a0, nt_=nt_, pb_=pb_: e.dma_start(
                                    out=c.wiS[a0:a0 + nt_, :], in_=sf[pb_][0:nt_, 64:72]),
                                    reads=[("a_sf", pb_)], writes=tkeys("wiS", a0, a0 + nt_))


def load_layer_params(c, layer):
    S = c.S
    for i in range(4):
        S.dma("sp", lambda e, i=i: e.dma_start(out=c.convw[:, :, i], in_=c.conv_w_d[layer][i].rearrange("(k p) -> p k", p=128),
                                               allow_slow_non_contiguous=True), writes=["lay_conv"])
    S.dma("sp", lambda e: e.dma_start(out=c.convb[:, :], in_=c.conv_b_d[layer].rearrange("(k p) -> p k", p=128),
                                      allow_slow_non_contiguous=True), writes=["lay_conv"])
    S.dma("sp", lambda e: e.dma_start(out=c.dtb_rep[:, :], in_=c.dt_bias_d[layer].partition_broadcast(128)), writes=["lay_ssd"])
    S.dma("sp", lambda e: e.dma_start(out=c.a_rep[:, :], in_=c.a_log_d[layer].partition_broadcast(128)), writes=["lay_ssd"])
    S.dma("sp", lambda e: e.dma_start(out=c.dsk_rep[:, :], in_=c.d_skip_d[layer].partition_broadcast(128)), writes=["lay_ssd"])
    S.dma("sp", lambda e: e.dma_start(out=c.normg_rep[:, :], in_=c.ssd_norm_d[layer].partition_broadcast(128)), writes=["lay_ssd"])
    S.op("act", lambda e: e.activation(out=c.a_rep[:, :], in_=c.a_rep[:, :], func=AF.Exp), reads=["lay_ssd"], writes=["lay_ssd"])
    S.op("dve", lambda e: e.tensor_scalar(out=c.a_rep[:, :], in0=c.a_rep[:, :], scalar1=-1.0, scalar2=None, op0=ALU.mult),
         reads=["lay_ssd"], writes=["lay_ssd"])


def phase_conv(c, layer, wins):
    nc, S = c.nc, c.S
    S.barrier()
    SEQ = c.SEQ
    WMAX = max(3 + c.TS, c.NSEQ * 11)
    with ExitStack() as st:
        xw = [sb(nc, st, f"c_xw{i}", [128, 4, WMAX], F32) for i in range(2)]
        acc = sb(nc, st, "c_acc", [128, WMAX], F32)
        xc = sb(nc, st, "c_xc", [128, 4, c.TS], F32)
        xcb = sb(nc, st, "c_xcb", [128, 4, c.TS], BF16)
        of = [sb(nc, st, f"c_of{i}", [128, 512], F32) for i in range(2)]
        ob = [sb(nc, st, f"c_ob{i}", [128, 512], BF16) for i in range(2)]
        tail = sb(nc, st, "c_tail", [128, 24, c.NSEQ, 3], F32)
        ptf = [ps(nc, st, f"c_pf{i}", [128, 512], F32) for i in range(2)]
        ptb = [ps(nc, st, f"c_pb{i}", [128, 512], BF16) for i in range(2)]
        gi = 0
        it = 0
        for (t0, nseg, L) in wins:
            is_s = t0 >= SEQ
            T = nseg * L
            W = 3 + L
            for cg in range(6):
                b = gi % 2
                gi += 1
                rows = slice(cg * 512, (cg + 1) * 512)
                xv = xw[b][:, :, 0:nseg * W].rearrange("p k (s w) -> p k s w", w=W)
                for k in range(4):
                    r0 = cg * 512 + k * 128
                    for s_ in range(nseg):
                        S.dma("sp", lambda e, xv=xv, r0=r0, k=k, t0=t0, L=L, s_=s_: e.dma_start(
                            out=xv[:, k, s_, 3:3 + L], in_=c.xbcT[r0:r0 + 128, t0 + s_ * L:t0 + (s_ + 1) * L]),
                            reads=tkeys("xbcT", t0, t0 + T), writes=[("c_xw", b)])
                        if is_s:
                            S.dma("sp", lambda e, xv=xv, r0=r0, k=k, s_=s_: e.dma_start(
                                out=xv[:, k, s_, 0:3], in_=c.sconv[layer, s_, :, r0:r0 + 128].rearrange("i p -> p i"),
                                allow_slow_non_contiguous=True), writes=[("c_xw", b)])
                if is_s:
                    pass
                elif t0 == 0:
                    S.op("pool", lambda e, xv=xv: e.memset(xv[:, :, :, 0:3], 0.0), writes=[("c_xw", b)])
                else:
                    S.dma("sp", lambda e, xv=xv, rows=rows, t0=t0: e.dma_start(
                        out=xv[:, :, 0, 0:3], in_=c.xbcT[rows, t0 - 3:t0].rearrange("(k p) t -> p k t", p=128)),
                        reads=tkeys("xbcT", t0 - 3, t0), writes=[("c_xw", b)])
                av = acc[:, 0:T].rearrange("p (s t) -> p s t", t=L)
                for k in range(4):
                    cc = cg * 4 + k
                    S.op("dve", lambda e, xv=xv, av=av, k=k, cc=cc, L=L: e.tensor_scalar(
                        out=av, in0=xv[:, k, :, 0:L], scalar1=c.convw[:, cc, 0:1], scalar2=c.convb[:, cc:cc + 1],
                        op0=ALU.mult, op1=ALU.add), reads=[("c_xw", b), "lay_conv"], writes=["c_acc"])
                    for i in range(1, 4):
                        S.op("dve", lambda e, xv=xv, av=av, k=k, cc=cc, L=L, i=i: e.scalar_tensor_tensor(
                            out=av, in0=xv[:, k, :, i:i + L], scalar=c.convw[:, cc, i:i + 1], in1=av,
                            op0=ALU.mult, op1=ALU.add), reads=[("c_xw", b), "lay_conv", "c_acc"], writes=["c_acc"])
                    if cg < 4:
                        S.op("act", lambda e, k=k, T=T: e.activation(out=xc[:, k, 0:T], in_=acc[:, 0:T], func=AF.Silu),
                             reads=["c_acc"], writes=["c_xc"])
                    else:
                        S.op("act", lambda e, k=k, T=T: e.activation(out=xcb[:, k, 0:T], in_=acc[:, 0:T], func=AF.Silu),
                             reads=["c_acc"], writes=["c_xcb"])
                if cg >= 4:
                    dst = c.BT if cg == 4 else c.CT
                    nm = "BT" if cg == 4 else "CT"
                    S.dma("sp", lambda e, dst=dst, t0=t0, T=T: e.dma_start(
                        out=dst[:, t0:t0 + T].rearrange("(k p) t -> p k t", p=128), in_=xcb[:, :, 0:T]),
                        reads=["c_xcb"], writes=tkeys(nm, t0, t0 + T))
                if cg < 5:
                    for j in range((T + 127) // 128):
                        nt_ = min(128, T - j * 128)
                        a0 = t0 + j * 128
                        pb_ = it % 2
                        it += 1
                        if cg < 4:
                            for k in range(4):
                                S.op("pe", lambda e, k=k, j=j, nt_=nt_, pb_=pb_: e.transpose(
                                    ptf[pb_][0:nt_, k * 128:(k + 1) * 128], xc[:, k, j * 128:j * 128 + nt_], c.ident_f[:, :]),
                                    reads=["c_xc", "ident_f"], writes=[("c_pf", pb_)])
                            S.op("act", lambda e, nt_=nt_, pb_=pb_: e.copy(out=of[pb_][0:nt_, :], in_=ptf[pb_][0:nt_, :]),
                                 reads=[("c_pf", pb_)], writes=[("c_of", pb_)])
                            S.dma("sp", lambda e, cg=cg, a0=a0, nt_=nt_, pb_=pb_: e.dma_start(
                                out=c.xS[a0:a0 + nt_, cg * 512:(cg + 1) * 512], in_=of[pb_][0:nt_, :]),
                                reads=[("c_of", pb_)], writes=tkeys("xS", a0, a0 + nt_))
                        else:
                            for k in range(4):
                                S.op("pe", lambda e, k=k, j=j, nt_=nt_, pb_=pb_: e.transpose(
                                    ptb[pb_][0:nt_, k * 128:(k + 1) * 128], xcb[:, k, j * 128:j * 128 + nt_], c.ident_b[:, :]),
                                    reads=["c_xcb", "ident_b"], writes=[("c_pb", pb_)])
                            S.op("act", lambda e, nt_=nt_, pb_=pb_: e.copy(out=ob[pb_][0:nt_, :], in_=ptb[pb_][0:nt_, :]),
                                 reads=[("c_pb", pb_)], writes=[("c_ob", pb_)])
                            S.dma("sp", lambda e, a0=a0, nt_=nt_, pb_=pb_: e.dma_start(
                                out=c.Btm[a0:a0 + nt_, :], in_=ob[pb_][0:nt_, :]),
                                reads=[("c_ob", pb_)], writes=tkeys("Btm", a0, a0 + nt_))
        S.dma("sp", lambda e: e.dma_start(out=tail[:, :, 0, :], in_=c.xbcT[:, SEQ - 3:SEQ].rearrange("(k p) t -> p k t", p=128)),
              reads=tkeys("xbcT", SEQ - 3, SEQ), writes=["c_tail"])
        for i in range(3):
            S.dma("sp", lambda e, i=i: e.dma_start(out=c.oc_p[layer, i].rearrange("(k p) -> p k", p=128), in_=tail[:, :, 0, i],
                                                   allow_slow_non_contiguous=True), reads=["c_tail"], writes=[("oc_p", layer, i)])
        for s_ in range(c.NSEQ):
            a = SEQ + 8 * s_ + 5
            S.dma("sp", lambda e, s_=s_, a=a: e.dma_start(out=tail[:, :, s_, :], in_=c.xbcT[:, a:a + 3].rearrange("(k p) t -> p k t", p=128)),
                  reads=tkeys("xbcT", a, a + 3), writes=["c_tail"])
            for i in range(3):
                S.dma("sp", lambda e, s_=s_, i=i: e.dma_start(out=c.oc_s[layer, s_, i].rearrange("(k p) -> p k", p=128), in_=tail[:, :, s_, i],
                                                              allow_slow_non_contiguous=True), reads=["c_tail"], writes=[("oc_s", layer, s_, i)])


def phase_ssd(c, layer):
    nc, S = c.nc, c.S
    S.barrier()
    SEQ, NSEQ = c.SEQ, c.NSEQ
    with ExitStack() as st:
        x_tm = [sb(nc, st, f"s_x{i}", [128, 2048], F32) for i in range(2)]
        z_tm = [sb(nc, st, f"s_z{i}", [128, 2048], F32) for i in range(2)]
        B_tm = [sb(nc, st, f"s_B{i}", [128, 512], BF16) for i in range(2)]
        BTt = [sb(nc, st, f"s_BT{i}", [128, 4, 128], BF16) for i in range(2)]
        CTt = [sb(nc, st, f"s_CT{i}", [128, 4, 128], BF16) for i in range(2)]
        dtr = [sb(nc, st, f"s_dtr{i}", [128, 32], F32) for i in range(2)]
        xb = sb(nc, st, "s_xb", [128, 32], F32)
        ab = sb(nc, st, "s_ab", [128, 32], F32)
        dt = sb(nc, st, "s_dt", [128, 32], F32)
        la = sb(nc, st, "s_la", [128, 32], F32)
        acs = sb(nc, st, "s_acs", [128, 32], F32)
        acsT = sb(nc, st, "s_acsT", [128, 128], F32)
        Rm = [sb(nc, st, f"s_Rm{i}", [128, 8, 128], BF16) for i in range(3)]
        las = [sb(nc, st, f"s_las{i}", [128, 32], BF16) for i in range(3)]
        lres = sb(nc, st, "s_lres", [128, 32], F32)
        ones_b = sb(nc, st, "s_onesb", [128, 128], BF16)
        diff = sb(nc, st, "s_diff", [128, 8, 128], F32)
        dA = sb(nc, st, "s_dA", [128, 8, 128], F32)
        PAs = sb(nc, st, "s_PAs", [128, 8, 128], F32)
        CBs = sb(nc, st, "s_CBs", [128, 128], F32)
        CBL = sb(nc, st, "s_CBL", [128, 8, 128], BF16)
        Cdec = sb(nc, st, "s_Cdec", [128, 8, 128], BF16)
        xdt = sb(nc, st, "s_xdt", [128, 8, 64], BF16)
        xdec = sb(nc, st, "s_xdec", [128, 8, 64], BF16)
        dte = sb(nc, st, "s_dte", [128, 8], F32)
        t1 = sb(nc, st, "s_t1", [128, 512], F32)
        y1 = sb(nc, st, "s_y1", [128, 512], F32)
        sz = sb(nc, st, "s_sz", [128, 512], F32)
        jk = sb(nc, st, "s_jk", [128, 512], F32)
        ss = sb(nc, st, "s_ss", [128, 1], F32)
        y3 = sb(nc, st, "s_y3", [128, 512], BF16)
        yT = [sb(nc, st, f"s_yT{i}", [128, 4, 128], BF16) for i in range(2)]
        hT = sb(nc, st, "s_hT", [128, 32, 64], F32)
        hTb = sb(nc, st, "s_hTb", [128, 32, 64], BF16)
        h0 = sb(nc, st, "s_h0", [128, 32, 128], F32)
        utri = sb(nc, st, "s_utri", [128, 128], F32)
        negtri = sb(nc, st, "s_negtri", [128, 128], F32)
        p_acs = ps(nc, st, "s_pacs", [128, 4, 128], F32)
        PA0 = ps(nc, st, "s_PA0", [128, 4, 128], F32)
        CBT = ps(nc, st, "s_CBT", [128, 512], F32)
        py = ps(nc, st, "s_py", [128, 512], F32)
        pst = ps(nc, st, "s_pst", [128, 512], F32)
        pT = ps(nc, st, "s_pT", [128, 8, 128], BF16)
        PA1 = ps(nc, st, "s_PA1", [128, 4, 128], F32)
        PAh = [PA0, PA1]

        S.op("pool", lambda e: e.memset(utri[:, :], 1.0), writes=["s_utri"])
        S.op("pool", lambda e: e.affine_select(out=utri[:, :], in_=utri[:, :], pattern=[[1, 128]], base=0, channel_multiplier=-1,
                                               compare_op=ALU.is_ge, fill=0.0), reads=["s_utri"], writes=["s_utri"])
        S.op("pool", lambda e: e.memset(negtri[:, :], 0.0), writes=["s_negtri"])
        S.op("pool", lambda e: e.affine_select(out=negtri[:, :], in_=negtri[:, :], pattern=[[1, 128]], base=0, channel_multiplier=-1,
                                               compare_op=ALU.is_ge, fill=-30000.0), reads=["s_negtri"], writes=["s_negtri"])
        S.op("pool", lambda e: e.memset(ones_b[:, :], 1.0), writes=["s_onesb"])
        S.op("pool", lambda e: e.memset(h0[64:128, :, :], 0.0), writes=["s_h0"])

        def hkeys(name):
            return [(name, g) for g in range(4)]

        def load_state(sidx):
            import os
            if "LD_LIM" in os.environ:
                S.lim = int(os.environ["LD_LIM"])
                S.nrec = 0
            S.dma("sp", lambda e: e.dma_start(out=h0[0:64, :, :], in_=c.sssm[layer, sidx].rearrange("h p n -> p h n")),
                  writes=["s_h0"])
            for g in range(4):
                for j in range(8):
                    S.op("pe", lambda e, g=g, j=j: e.matmul(PAh[j // 4][:, j % 4, :], lhsT=h0[:, 8 * g + j, :], rhs=c.ident_f[:, :],
                                                            start=True, stop=True),
                         reads=["s_h0", "ident_f"], writes=["s_PA"])
                for hh in range(2):
                    S.op("act", lambda e, hh=hh: e.copy(out=PAs[:, 4 * hh:4 * hh + 4, :], in_=PAh[hh][:, :, :]),
                         reads=["s_PA"], writes=["s_PAs"])
                S.op("dve", lambda e, g=g: e.tensor_copy(out=hT[:, 8 * g:8 * g + 8, :], in_=PAs[:, :, 0:64]),
                     reads=["s_PAs"], writes=[("s_hT", g)])
                S.op("pool", lambda e, g=g: e.tensor_copy(out=hTb[:, 8 * g:8 * g + 8, :], in_=PAs[:, :, 0:64]),
                     reads=["s_PAs"], writes=[("s_hTb", g)])

        def store_state(dst):
            import os
            if "SS_LIM" in os.environ:
                S.lim = int(os.environ["SS_LIM"])
                S.nrec = 0
            for g in range(4):
                for j in range(8):
                    S.op("pe", lambda e, g=g, j=j: e.matmul(PAh[j // 4][0:64, j % 4, :], lhsT=hT[:, 8 * g + j, :], rhs=c.ident_f[:, :],
                                                            start=True, stop=True),
                         reads=[("s_hT", g), "ident_f"], writes=["s_PA"])
                for hh in range(2):
                    S.op("act", lambda e, g=g, hh=hh: e.copy(out=h0[0:64, 8 * g + 4 * hh:8 * g + 4 * hh + 4, :], in_=PAh[hh][0:64, :, :]),
                         reads=["s_PA"], writes=["s_h0"])
            S.dma("sp", lambda e: e.dma_start(out=dst.rearrange("h p n -> p h n"), in_=h0[0:64, :, :]),
                  reads=["s_h0"], writes=[("oh", id(dst))], is_output=True)

        ci = [0]

        def chunk(t0, L):
            b = ci[0] % 2
            ci[0] += 1
            S.dma("sp", lambda e: e.dma_start(out=x_tm[b][0:L, :], in_=c.xS[t0:t0 + L, :]), reads=tkeys("xS", t0, t0 + L), writes=[("s_x", b)])
            import os
            NOL = os.environ.get("SSD_NOLOAD", "")
            if "z" not in NOL:
                S.dma("sp", lambda e: e.dma_start(out=z_tm[b][0:L, :], in_=c.zS[t0:t0 + L, :]), reads=tkeys("zS", t0, t0 + L), writes=[("s_z", b)])
            if "B" not in NOL:
                S.dma("sp", lambda e: e.dma_start(out=B_tm[b][0:L, :], in_=c.Btm[t0:t0 + L, :]), reads=tkeys("Btm", t0, t0 + L), writes=[("s_B", b)])
            if "T" not in NOL:
                S.dma("sp", lambda e: e.dma_start(out=BTt[b][:, :, 0:L], in_=c.BT[:, t0:t0 + L].rearrange("(g n) t -> n g t", n=128)),
                      reads=tkeys("BT", t0, t0 + L), writes=[("s_BT", b)])
                S.dma("sp", lambda e: e.dma_start(out=CTt[b][:, :, 0:L], in_=c.CT[:, t0:t0 + L].rearrange("(g n) t -> n g t", n=128)),
                      reads=tkeys("CT", t0, t0 + L), writes=[("s_CT", b)])
            S.dma("sp", lambda e: e.dma_start(out=dtr[b][0:L, :], in_=c.dtS[t0:t0 + L, :]), reads=tkeys("dtS", t0, t0 + L), writes=[("s_dtr", b)])
            S.op("dve", lambda e: e.tensor_tensor(out=xb[0:L, :], in0=dtr[b][0:L, :], in1=c.dtb_rep[0:L, :], op=ALU.add),
                 reads=[("s_dtr", b), "lay_ssd"], writes=["s_xb"])
            S.op("dve", lambda e: e.scalar_tensor_tensor(out=ab[0:L, :], in0=xb[0:L, :], scalar=-1.0, in1=xb[0:L, :], op0=ALU.mult, op1=ALU.max),
                 reads=["s_xb"], writes=["s_ab"])
            S.op("act", lambda e: e.activation(out=ab[0:L, :], in_=ab[0:L, :], func=AF.Exp, scale=-1.0), reads=["s_ab"], writes=["s_ab"])
            S.op("act", lambda e: e.activation(out=ab[0:L, :], in_=ab[0:L, :], func=AF.Ln, bias=c.one_t[0:L, 0:1]),
                 reads=["s_ab", "consts"], writes=["s_ab"])
            S.op("dve", lambda e: e.scalar_tensor_tensor(out=dt[0:L, :], in0=xb[0:L, :], scalar=0.0, in1=ab[0:L, :], op0=ALU.max, op1=ALU.add),
                 reads=["s_xb", "s_ab"], writes=["s_dt"])
            S.op("dve", lambda e: e.tensor_tensor(out=la[0:L, :], in0=dt[0:L, :], in1=c.a_rep[0:L, :], op=ALU.mult),
                 reads=["s_dt", "lay_ssd"], writes=["s_la"])
            S.op("dve", lambda e: e.tensor_copy(out=las[0][0:L, :], in_=la[0:L, :]), reads=["s_la"], writes=["s_las"])
            S.op("dve", lambda e: e.tensor_tensor(out=lres[0:L, :], in0=la[0:L, :], in1=las[0][0:L, :], op=ALU.subtract),
                 reads=["s_la", "s_las"], writes=["s_lres"])
            S.op("dve", lambda e: e.tensor_copy(out=las[1][0:L, :], in_=lres[0:L, :]), reads=["s_lres"], writes=["s_las"])
            S.op("dve", lambda e: e.tensor_tensor(out=lres[0:L, :], in0=lres[0:L, :], in1=las[1][0:L, :], op=ALU.subtract),
                 reads=["s_lres", "s_las"], writes=["s_lres"])
            S.op("dve", lambda e: e.tensor_copy(out=las[2][0:L, :], in_=lres[0:L, :]), reads=["s_lres"], writes=["s_las"])
            S.op("pe", lambda e: e.matmul(p_acs[0:L, 0, 0:32], lhsT=utri[0:L, 0:L], rhs=la[0:L, :], start=True, stop=True),
                 reads=["s_utri", "s_la"], writes=["s_pacs"])
            S.op("act", lambda e: e.copy(out=acs[0:L, :], in_=p_acs[0:L, 0, 0:32]), reads=["s_pacs"], writes=["s_acs"])
            for g in range(4):
                hs = slice(8 * g, 8 * g + 8)
                cs_ = slice(512 * g, 512 * g + 512)
                for k3 in range(3):
                    S.op("dve", lambda e, hs=hs, k3=k3: e.tensor_tensor(out=Rm[k3][0:L, :, 0:L], in0=las[k3][0:L, hs].unsqueeze(2).broadcast_to([L, 8, L]),
                                                                      in1=utri[0:L, 0:L].unsqueeze(1).broadcast_to([L, 8, L]), op=ALU.mult),
                         reads=["s_las", "s_utri"], writes=[("s_Rm", k3)])
                for hh in range(2):
                    for k3 in range(3):
                        S.op("pe", lambda e, hh=hh, k3=k3: e.matmul(PAh[hh][:, :, 0:L], lhsT=ones_b[0:L, :], rhs=Rm[k3][0:L, 4 * hh:4 * hh + 4, 0:L],
                                                                    start=(k3 == 0), stop=(k3 == 2)),
                             reads=[("s_Rm", k3), "s_onesb"], writes=["s_PA"])
                for hh in range(2):
                    S.op("act", lambda e, hh=hh: e.activation(out=dA[:, 4 * hh:4 * hh + 4, 0:L], in_=PAh[hh][:, :, 0:L], func=AF.Exp),
                         reads=["s_PA"], writes=["s_dA"])
                    S.op("act", lambda e, hh=hh: e.copy(out=PAs[:, 4 * hh:4 * hh + 4, 0:L], in_=PAh[hh][:, :, 0:L]),
                         reads=["s_PA"], writes=["s_PAs"])
                    S.op("dve", lambda e, g=g, hh=hh: e.tensor_tensor(out=diff[0:L, 4 * hh:4 * hh + 4, 0:L], in0=PAs[0:L, 4 * hh:4 * hh + 4, 0:L],
                                                                    in1=acs[0:L, 8 * g + 4 * hh:8 * g + 4 * hh + 4].unsqueeze(2).broadcast_to([L, 4, L]),
                                                                    op=ALU.subtract),
                         reads=["s_PAs", "s_acs"], writes=["s_diff"])
                S.op("pool", lambda e: e.tensor_tensor(out=diff[0:L, :, 0:L], in0=diff[0:L, :, 0:L],
                                                       in1=negtri[0:L, 0:L].unsqueeze(1).broadcast_to([L, 8, L]), op=ALU.add),
                     reads=["s_diff", "s_negtri"], writes=["s_diff"])
                S.op("act", lambda e: e.activation(out=diff[0:L, :, 0:L], in_=diff[0:L, :, 0:L], func=AF.Exp), reads=["s_diff"], writes=["s_diff"])
                S.op("pe", lambda e, g=g: e.matmul(CBT[0:L, 0:L], lhsT=BTt[b][:, g, 0:L], rhs=CTt[b][:, g, 0:L], start=True, stop=True),
                     reads=[("s_BT", b), ("s_CT", b)], writes=["s_CBT"])
                S.op("act", lambda e: e.copy(out=CBs[0:L, 0:L], in_=CBT[0:L, 0:L]), reads=["s_CBT"], writes=["s_CBs"])
                S.op("dve", lambda e: e.tensor_tensor(out=CBL[0:L, :, 0:L], in0=diff[0:L, :, 0:L],
                                                      in1=CBs[0:L, 0:L].unsqueeze(1).broadcast_to([L, 8, L]), op=ALU.mult),
                     reads=["s_diff", "s_CBs"], writes=["s_CBL"])
                S.op("pool", lambda e, g=g: e.tensor_tensor(out=Cdec[:, :, 0:L], in0=dA[:, :, 0:L],
                                                            in1=CTt[b][:, g, 0:L].unsqueeze(1).broadcast_to([128, 8, L]), op=ALU.mult),
                     reads=["s_dA", ("s_CT", b)], writes=["s_Cdec"])
                S.op("dve", lambda e, hs=hs, cs_=cs_: e.tensor_tensor(out=xdt[0:L, :, :], in0=x_tm[b][0:L, cs_].rearrange("p (j d) -> p j d", d=64),
                                                                    in1=dt[0:L, hs].unsqueeze(2).broadcast_to([L, 8, 64]), op=ALU.mult),
                     reads=[("s_x", b), "s_dt"], writes=["s_xdt"])
                for j in range(8):
                    S.op("pe", lambda e, j=j: e.matmul(py[0:L, 64 * j:64 * j + 64], lhsT=CBL[0:L, j, 0:L], rhs=xdt[0:L, j, :], start=True, stop=False),
                         reads=["s_CBL", "s_xdt"], writes=["s_py"])
                    S.op("pe", lambda e, j=j, g=g: e.matmul(py[0:L, 64 * j:64 * j + 64], lhsT=Cdec[:, j, 0:L], rhs=hTb[:, 8 * g + j, :], start=False, stop=True),
                         reads=["s_Cdec", ("s_hTb", g)], writes=["s_py"])
                S.op("pool", lambda e, hs=hs, cs_=cs_: e.tensor_tensor(out=t1[0:L, :].rearrange("p (j d) -> p j d", d=64),
                                                                     in0=x_tm[b][0:L, cs_].rearrange("p (j d) -> p j d", d=64),
                                                                     in1=c.dsk_rep[0:L, hs].unsqueeze(2).broadcast_to([L, 8, 64]), op=ALU.mult),
                     reads=[("s_x", b), "lay_ssd"], writes=["s_t1"])
                S.op("dve", lambda e: e.tensor_tensor(out=y1[0:L, :], in0=t1[0:L, :], in1=py[0:L, :], op=ALU.add),
                     reads=["s_t1", "s_py"], writes=["s_y1"])
                S.op("act", lambda e, cs_=cs_: e.activation(out=sz[0:L, :], in_=z_tm[b][0:L, cs_], func=AF.Silu), reads=[("s_z", b)], writes=["s_sz"])
                S.op("dve", lambda e: e.tensor_tensor(out=y1[0:L, :], in0=y1[0:L, :], in1=sz[0:L, :], op=ALU.mult),
                     reads=["s_y1", "s_sz"], writes=["s_y1"])
                S.op("act", lambda e: e.activation(out=jk[0:L, :], in_=y1[0:L, :], func=AF.Square, accum_out=ss[0:L, 0:1]),
                     reads=["s_y1"], writes=["s_jk", "s_ss"])
                S.op("act", lambda e: e.activation(out=ss[0:L, :], in_=ss[0:L, :], func=AF.Sqrt, scale=1.0 / 512, bias=c.eps_t[0:L, 0:1]),
                     reads=["s_ss", "consts"], writes=["s_ss"])
                S.op("dve", lambda e: e.reciprocal(out=ss[0:L, :], in_=ss[0:L, :]), reads=["s_ss"], writes=["s_ss"])
                S.op("dve", lambda e, cs_=cs_: e.scalar_tensor_tensor(out=y3[0:L, :], in0=y1[0:L, :], scalar=ss[0:L, 0:1], in1=c.normg_rep[0:L, cs_],
                                                                    op0=ALU.mult, op1=ALU.mult),
                     reads=["s_y1", "s_ss", "lay_ssd"], writes=["s_y3"])
                yb = (ci[0] * 4 + g) % 2
                for k in range(4):
                    S.op("pe", lambda e, k=k: e.transpose(pT[:, k, 0:L], y3[0:L, 128 * k:128 * k + 128], c.ident_b[0:L, 0:L]),
                         reads=["s_y3", "ident_b"], writes=["s_pT"])
                S.op("act", lambda e, yb=yb: e.copy(out=yT[yb][:, :, 0:L], in_=pT[:, 0:4, 0:L]), reads=["s_pT"], writes=[("s_yT", yb)])
                S.dma("sp", lambda e, g=g, yb=yb: e.dma_start(out=c.ynT[512 * g:512 * g + 512, t0:t0 + L].rearrange("(k p) t -> p k t", p=128),
                                                             in_=yT[yb][:, :, 0:L]),
                      reads=[("s_yT", yb)], writes=tkeys("ynT", t0, t0 + L))
                for hh in range(2):
                    S.op("dve", lambda e, g=g, hh=hh: e.tensor_tensor(out=dte[0:L, 4 * hh:4 * hh + 4], in0=PAs[0:L, 4 * hh:4 * hh + 4, L - 1],
                                                                    in1=acs[0:L, 8 * g + 4 * hh:8 * g + 4 * hh + 4], op=ALU.subtract),
                         reads=["s_PAs", "s_acs"], writes=["s_dte"])
                S.op("act", lambda e: e.activation(out=dte[0:L, :], in_=dte[0:L, :], func=AF.Exp), reads=["s_dte"], writes=["s_dte"])
                S.op("dve", lambda e: e.tensor_tensor(out=xdec[0:L, :, :], in0=xdt[0:L, :, :],
                                                      in1=dte[0:L, :].unsqueeze(2).broadcast_to([L, 8, 64]), op=ALU.mult),
                     reads=["s_xdt", "s_dte"], writes=["s_xdec"])
                S.op("pe", lambda e, g=g: e.matmul(pst[:, :], lhsT=B_tm[b][0:L, 128 * g:128 * g + 128],
                                                   rhs=xdec[0:L, :, :].rearrange("p j d -> p (j d)"), start=True, stop=True),
                     reads=[("s_B", b), "s_xdec"], writes=["s_pst"])
                S.op("dve", lambda e, hs=hs: e.tensor_tensor(out=hT[:, hs, :], in0=hT[:, hs, :],
                                                           in1=dA[:, :, L - 1:L].broadcast_to([128, 8, 64]), op=ALU.mult),
                     reads=[("s_hT", g), "s_dA"], writes=[("s_hT", g)])
                S.op("dve", lambda e, hs=hs: e.tensor_tensor(out=hT[:, hs, :], in0=hT[:, hs, :],
                                                           in1=pst[:, :].rearrange("p (j d) -> p j d", d=64), op=ALU.add),
                     reads=[("s_hT", g), "s_pst"], writes=[("s_hT", g)])
                S.op("pool", lambda e, hs=hs: e.tensor_copy(out=hTb[:, hs, :], in_=hT[:, hs, :]),
                     reads=[("s_hT", g)], writes=[("s_hTb", g)])

        S.op("pool", lambda e: e.memset(hT[:, :, :], 0.0), writes=hkeys("s_hT"))
        S.op("pool", lambda e: e.memset(hTb[:, :, :], 0.0), writes=hkeys("s_hTb"))
        import os
        MODE = int(os.environ.get("SSD_MODE", "9"))
        if "SSD_LIM" in os.environ:
            S.lim = int(os.environ["SSD_LIM"])
            S.nrec = 0
        for t0 in range(0, SEQ, 128):
            if MODE >= 1:
                chunk(t0, 128)
        if MODE >= 2:
            store_state(c.oh_p[layer])
        for s_ in range(NSEQ):
            if MODE >= 3:
                load_state(s_)
            if MODE >= 4:
                chunk(SEQ + 8 * s_, 8)
            if MODE >= 5:
                store_state(c.oh_s[layer, s_])
        S.lim = None


def phase_dsa_prompt(c, layer):
    nc, S = c.nc, c.S
    S.barrier()
    SEQ = c.SEQ
    TOPK = min(256, SEQ // 4)
    NIT = 26
    NKT = SEQ // 128
    with ExitStack() as st:
        kiT = sb(nc, st, "d_kiT", [64, SEQ], BF16)
        kTa = sb(nc, st, "d_kT", [64, 4, SEQ], BF16)
        Va = sb(nc, st, "d_V", [128, NKT, 4, 65], BF16)
        sc = sb(nc, st, "d_sc", [128, SEQ], F32)
        mneg = sb(nc, st, "d_mneg", [128, SEQ], BF16)
        qi8 = sb(nc, st, "d_qi8", [64, 8, 128], BF16)
        wi = sb(nc, st, "d_wi", [128, 8], F32)
        rl = [sb(nc, st, f"d_rl{i}", [128, 512], F32) for i in range(2)]
        sm = sb(nc, st, "d_sm", [128, 8], F32)
        QT4 = [sb(nc, st, f"d_QT4{i}", [64, 4, 128], BF16) for i in range(2)]
        PT = [sb(nc, st, f"d_PT{i}", [128, 512], BF16) for i in range(2)]
        I4 = sb(nc, st, "d_I4", [128, 4, 128], BF16)
        rc = sb(nc, st, "d_rc", [128, 512], F32)
        bcs = sb(nc, st, "d_bcs", [64, 512], F32)
        OT = [sb(nc, st, f"d_OT{i}", [64, 512], BF16) for i in range(2)]
        p_i = [ps(nc, st, f"d_pi{i}", [128, 512], F32) for i in range(2)]
        p_s = [ps(nc, st, f"d_ps{i}", [128, 512], F32) for i in range(2)]
        p_o = [ps(nc, st, f"d_po{i}", [128, 512], F32) for i in range(2)]
        p_b = ps(nc, st, "d_pb", [128, 512], F32)

        S.dma("sp", lambda e: e.dma_start(out=kiT[:, :], in_=c.kiT[0, :, 0:SEQ]), reads=tkeys("kiT", 0, SEQ), writes=["d_kiT"])
        for kvh in range(4):
            S.dma("sp", lambda e, kvh=kvh: e.dma_start(out=kTa[:, kvh, :], in_=c.kT[kvh, :, 0:SEQ]), reads=tkeys("kT", 0, SEQ), writes=["d_kT"])
        S.op("pool", lambda e: e.memset(Va[:, :, :, 64:65], 1.0), writes=["d_V"])
        for kt in range(NKT):
            S.dma("sp", lambda e, kt=kt: e.dma_start(out=Va[:, kt, :, 0:64], in_=c.vB[kt * 128:(kt + 1) * 128, :].rearrange("p (h d) -> p h d", d=64)),
                  reads=tkeys("vB", kt * 128, kt * 128 + 128), writes=["d_V"])
        for g in range(4):
            S.op("dve", lambda e, g=g: e.tensor_copy(out=I4[:, g, :], in_=c.ident_b[:, :]), reads=["ident_b"], writes=["d_I4"])

        it = 0
        fillreg = {}
        for qt in range(NKT):
            t0 = qt * 128
            Sk = (qt + 1) * 128
            S.dma("sp", lambda e, t0=t0: e.dma_start(out=qi8[:, :, :], in_=c.qiT[:, :, t0:t0 + 128].rearrange("h d t -> d h t")),
                  reads=tkeys("qiT", t0, t0 + 128), writes=["d_qi8"])
            S.dma("sp", lambda e, t0=t0: e.dma_start(out=wi[:, :], in_=c.wiS[t0:t0 + 128, :]), reads=tkeys("wiS", t0, t0 + 128), writes=["d_wi"])
            for kb in range((Sk + 511) // 512):
                w = min(512, Sk - kb * 512)
                k0 = kb * 512
                for hh in range(8):
                    pb_ = it % 2
                    it += 1
                    S.op("pe", lambda e, hh=hh, k0=k0, w=w, pb_=pb_: e.matmul(p_i[pb_][:, 0:w], lhsT=qi8[:, hh, :], rhs=kiT[:, k0:k0 + w],
                                                                             start=True, stop=True),
                         reads=["d_qi8", "d_kiT"], writes=[("d_pi", pb_)])
                    S.op("act", lambda e, w=w, pb_=pb_: e.activation(out=rl[pb_][:, 0:w], in_=p_i[pb_][:, 0:w], func=AF.Relu),
                         reads=[("d_pi", pb_)], writes=[("d_rl", pb_)])
                    if hh == 0:
                        S.op("dve", lambda e, k0=k0, w=w, pb_=pb_: e.tensor_scalar(out=sc[:, k0:k0 + w], in0=rl[pb_][:, 0:w], scalar1=wi[:, 0:1],
                                                                                 scalar2=None, op0=ALU.mult),
                             reads=[("d_rl", pb_), "d_wi"], writes=["d_sc"])
                    else:
                        S.op("dve", lambda e, hh=hh, k0=k0, w=w, pb_=pb_: e.scalar_tensor_tensor(
                            out=sc[:, k0:k0 + w], in0=rl[pb_][:, 0:w], scalar=wi[:, hh:hh + 1], in1=sc[:, k0:k0 + w], op0=ALU.mult, op1=ALU.add),
                            reads=[("d_rl", pb_), "d_wi", "d_sc"], writes=["d_sc"])
            S.op("dve", lambda e, Sk=Sk: e.tensor_reduce(out=sm[:, 5:6], in_=sc[:, 0:Sk], axis=AX.X, op=ALU.max), reads=["d_sc"], writes=["d_sm"])
            S.op("dve", lambda e, Sk=Sk: e.tensor_reduce(out=sm[:, 0:1], in_=sc[:, 0:Sk], axis=AX.X, op=ALU.min), reads=["d_sc"], writes=["d_sm"])
            S.op("dve", lambda e: e.tensor_tensor(out=sm[:, 1:2], in0=sm[:, 5:6], in1=sm[:, 0:1], op=ALU.subtract), reads=["d_sm"], writes=["d_sm"])
            def _asel(e, Sk=Sk):
                if "r" not in fillreg:
                    fillreg["r"] = e.to_reg(-1e30)
                return e.affine_select(out=sc[:, Sk - 128:Sk], in_=sc[:, Sk - 128:Sk], pattern=[[-1, 128]], base=0,
                                       channel_multiplier=1, compare_op=ALU.is_ge, fill=fillreg["r"])
            S.op("pool", _asel,
                 reads=["d_sc", "d_sm"], writes=["d_sc"])
            for _ in range(NIT):
                S.op("dve", lambda e: e.tensor_scalar(out=sm[:, 1:2], in0=sm[:, 1:2], scalar1=0.5, scalar2=None, op0=ALU.mult),
                     reads=["d_sm"], writes=["d_sm"])
                S.op("dve", lambda e: e.tensor_tensor(out=sm[:, 2:3], in0=sm[:, 0:1], in1=sm[:, 1:2], op=ALU.add), reads=["d_sm"], writes=["d_sm"])
                S.op("dve", lambda e, Sk=Sk: e.tensor_scalar(out=mneg[:, 0:Sk], in0=sc[:, 0:Sk], scalar1=sm[:, 2:3], scalar2=None,
                                                           op0=ALU.is_ge, op1=ALU.add, accum_out=sm[:, 3:4]),
                     reads=["d_sc", "d_sm"], writes=["d_mneg", "d_sm"])
                S.op("dve", lambda e: e.tensor_scalar(out=sm[:, 4:5], in0=sm[:, 3:4], scalar1=float(TOPK), scalar2=None, op0=ALU.is_ge),
                     reads=["d_sm"], writes=["d_sm"])
                S.op("dve", lambda e: e.scalar_tensor_tensor(out=sm[:, 0:1], in0=sm[:, 4:5], scalar=sm[:, 1:2], in1=sm[:, 0:1],
                                                             op0=ALU.mult, op1=ALU.add), reads=["d_sm"], writes=["d_sm"])
            S.op("dve", lambda e: e.tensor_scalar(out=sm[:, 6:7], in0=sm[:, 0:1], scalar1=-1e29, scalar2=None, op0=ALU.max),
                 reads=["d_sm"], writes=["d_sm"])
            S.op("dve", lambda e, Sk=Sk: e.tensor_scalar(out=mneg[:, 0:Sk], in0=sc[:, 0:Sk], scalar1=sm[:, 6:7], scalar2=-30000.0,
                                                       op0=ALU.is_lt, op1=ALU.mult), reads=["d_sc", "d_sm"], writes=["d_mneg"])
            for kvh in range(4):
                qb = (qt * 4 + kvh) % 2
                S.dma("sp", lambda e, kvh=kvh, t0=t0, qb=qb: e.dma_start(out=QT4[qb][:, :, :],
                                                                        in_=c.qT[4 * kvh:4 * kvh + 4, :, t0:t0 + 128].rearrange("h d t -> d h t")),
                      reads=tkeys("qT", t0, t0 + 128), writes=[("d_QT4", qb)])
                for kt in range(qt + 1):
                    pb_ = it % 2
                    it += 1
                    S.op("pe", lambda e, kvh=kvh, kt=kt, qb=qb, pb_=pb_: e.matmul(p_s[pb_][:, :], lhsT=kTa[:, kvh, kt * 128:(kt + 1) * 128],
                                                                                 rhs=QT4[qb][:, :, :].rearrange("d h t -> d (h t)"),
                                                                                 start=True, stop=False),
                         reads=["d_kT", ("d_QT4", qb)], writes=[("d_ps", pb_)])
                    S.op("pe", lambda e, kt=kt, pb_=pb_: e.matmul(p_s[pb_][:, :], lhsT=mneg[:, kt * 128:(kt + 1) * 128],
                                                                 rhs=I4[:, :, :].rearrange("p h t -> p (h t)"), start=False, stop=True),
                         reads=["d_mneg", "d_I4"], writes=[("d_ps", pb_)])
                    S.op("act", lambda e, pb_=pb_: e.activation(out=PT[pb_][:, :], in_=p_s[pb_][:, :], func=AF.Exp),
                         reads=[("d_ps", pb_)], writes=[("d_PT", pb_)])
                    S.op("pe", lambda e, kvh=kvh, kt=kt, qb=qb, pb_=pb_, qt=qt: e.matmul(p_o[qb][0:65, :], lhsT=Va[:, kt, kvh, :], rhs=PT[pb_][:, :],
                                                                                        start=(kt == 0), stop=(kt == qt)),
                         reads=["d_V", ("d_PT", pb_)], writes=[("d_po", qb)])
                S.op("dve", lambda e, qb=qb: e.reciprocal(out=rc[64:65, :], in_=p_o[qb][64:65, :]), reads=[("d_po", qb)], writes=["d_rc"])
                S.op("pe", lambda e: e.matmul(p_b[0:64, :], lhsT=c.ones_f[64:65, 0:64], rhs=rc[64:65, :], start=True, stop=True),
                     reads=["d_rc", "ones_f"], writes=["d_pb"])
                S.op("act", lambda e: e.copy(out=bcs[:, :], in_=p_b[0:64, :]), reads=["d_pb"], writes=["d_bcs"])
                S.op("dve", lambda e, qb=qb: e.tensor_tensor(out=OT[qb][:, :], in0=p_o[qb][0:64, :], in1=bcs[:, :], op=ALU.mult),
                     reads=[("d_po", qb), "d_bcs"], writes=[("d_OT", qb)])
                S.dma("sp", lambda e, kvh=kvh, t0=t0, qb=qb: e.dma_start(
                    out=c.yaT[256 * kvh:256 * kvh + 256, t0:t0 + 128].rearrange("(g d) t -> d g t", d=64),
                    in_=OT[qb][:, :].rearrange("d (g t) -> d g t", t=128)),
                    reads=[("d_OT", qb)], writes=tkeys("yaT", t0, t0 + 128))


def phase_dsa_sample(c, layer):
    nc, S = c.nc, c.S
    S.barrier()
    SEQ, NSEQ, NPG = c.SEQ, c.NSEQ, c.NPAGES
    PAST = NPG * 128
    LK = PAST + 8
    TOPK = min(256, LK // 4)
    NIT = 26
    fillreg = {}
    with ExitStack() as st:
        pt_i = sb(nc, st, "e_pt", [128, 1], I32)
        idxq = [sb(nc, st, f"e_idxq{i}", [128, 1], I32) for i in range(4)]
        sc = sb(nc, st, "e_sc", [8, LK], F32)
        mneg = sb(nc, st, "e_mneg", [8, LK], BF16)
        qi8 = sb(nc, st, "e_qi8", [64, 8, 8], BF16)
        kin = sb(nc, st, "e_kin", [64, 8], BF16)
        wi = sb(nc, st, "e_wi", [8, 8], F32)
        rl = [sb(nc, st, f"e_rl{i}", [8, 512], F32) for i in range(2)]
        sm = sb(nc, st, "e_sm", [8, 8], F32)
        I4s = sb(nc, st, "e_I4s", [8, 4, 8], BF16)
        QT4 = sb(nc, st, "e_QT4", [64, 16, 8], BF16)
        kTn = sb(nc, st, "e_kTn", [64, 4, 8], BF16)
        Vn = sb(nc, st, "e_Vn", [8, 4, 65], BF16)
        kTj = [sb(nc, st, f"e_kTj{i}", [64, 4, 128], BF16) for i in range(2)]
        PT = [sb(nc, st, f"e_PT{i}", [128, 32], BF16) for i in range(2)]
        rc = sb(nc, st, "e_rc", [128, 32], F32)
        bcs = sb(nc, st, "e_bcs", [64, 32], F32)
        OT = sb(nc, st, "e_OT", [64, 4, 32], BF16)
        p_t = [ps(nc, st, f"e_pt{i}", [128, 512], F32) for i in range(2)]
        p_s = [ps(nc, st, f"e_ps{i}", [128, 512], F32) for i in range(2)]
        p_o = [ps(nc, st, f"e_po{i}", [128, 512], F32) for i in range(4)]
        for g in range(4):
            S.op("dve", lambda e, g=g: e.tensor_copy(out=I4s[:, g, :], in_=c.ident_b[0:8, 0:8]), reads=["ident_b"], writes=["e_I4s"])
        it = 0
        for si in range(NSEQ):
            t0 = SEQ + 8 * si
            S.dma("sp", lambda e, si=si: e.dma_start(out=pt_i[0:NPG, :], in_=c.ptab[si].rearrange("(p o) -> p o", o=1)), writes=["e_pt"])
            for q in range(4):
                S.op("dve", lambda e, q=q: e.tensor_scalar(out=idxq[q][0:NPG, :], in0=pt_i[0:NPG, :], scalar1=4.0, scalar2=float(q),
                                                         op0=ALU.mult, op1=ALU.add), reads=["e_pt"], writes=[("e_idxq", q)])
            S.dma("sp", lambda e, t0=t0: e.dma_start(out=qi8[:, :, :], in_=c.qiT[:, :, t0:t0 + 8].rearrange("h d t -> d h t")),
                  reads=tkeys("qiT", t0, t0 + 8), writes=["e_qi8"])
            S.dma("sp", lambda e, t0=t0: e.dma_start(out=kin[:, :], in_=c.kiT[0, :, t0:t0 + 8]), reads=tkeys("kiT", t0, t0 + 8), writes=["e_kin"])
            S.dma("sp", lambda e, t0=t0: e.dma_start(out=wi[:, :], in_=c.wiS[t0:t0 + 8, :]), reads=tkeys("wiS", t0, t0 + 8), writes=["e_wi"])
            S.dma("sp", lambda e, t0=t0: e.dma_start(out=QT4[:, :, :], in_=c.qT[:, :, t0:t0 + 8].rearrange("h d t -> d h t")),
                  reads=tkeys("qT", t0, t0 + 8), writes=["e_QT4"])
            S.dma("sp", lambda e, t0=t0: e.dma_start(out=kTn[:, :, :], in_=c.kT[:, :, t0:t0 + 8].rearrange("h d t -> d h t")),
                  reads=tkeys("kT", t0, t0 + 8), writes=["e_kTn"])
            S.op("pool", lambda e: e.memset(Vn[:, :, 64:65], 1.0), writes=["e_Vn"])
            S.dma("sp", lambda e, t0=t0: e.dma_start(out=Vn[:, :, 0:64], in_=c.vB[t0:t0 + 8, :].rearrange("p (h d) -> p h d", d=64)),
                  reads=tkeys("vB", t0, t0 + 8), writes=["e_Vn"])

            def accum(pb_, hh, cols):
                if hh == 0:
                    S.op("dve", lambda e: e.tensor_scalar(out=sc[:, cols], in0=rl[pb_][:, 0:cols.stop - cols.start], scalar1=wi[:, 0:1],
                                                          scalar2=None, op0=ALU.mult), reads=[("e_rl", pb_), "e_wi"], writes=["e_sc"])
                else:
                    S.op("dve", lambda e: e.scalar_tensor_tensor(out=sc[:, cols], in0=rl[pb_][:, 0:cols.stop - cols.start], scalar=wi[:, hh:hh + 1],
                                                                 in1=sc[:, cols], op0=ALU.mult, op1=ALU.add),
                         reads=[("e_rl", pb_), "e_wi", "e_sc"], writes=["e_sc"])

            with ExitStack() as st2:
                KI = sb(nc, st2, "e_KI", [128, 8192], F32)
                kiTs = sb(nc, st2, "e_kiTs", [64, 128, 128], BF16)
                S.dma("pool", lambda e: e.indirect_dma_start(out=KI[0:NPG, :], out_offset=None, in_=c.cache_i[layer],
                                                             in_offset=bass.IndirectOffsetOnAxis(ap=pt_i[0:NPG, 0:1], axis=0)),
                      reads=["e_pt"], writes=["e_KI"])
                for jb in range(32):
                    pb_ = it % 2
                    it += 1
                    for jj in range(4):
                        j = 4 * jb + jj
                        S.op("pe", lambda e, j=j, jj=jj, pb_=pb_: e.matmul(p_t[pb_][0:64, jj * 128:jj * 128 + NPG], lhsT=KI[0:NPG, j * 64:(j + 1) * 64],
                                                                          rhs=c.ident_f[0:NPG, 0:NPG], start=True, stop=True),
                             reads=["e_KI", "ident_f"], writes=[("e_pt", pb_)])
                    S.op("act", lambda e, jb=jb, pb_=pb_: e.copy(out=kiTs[:, 4 * jb:4 * jb + 4, 0:NPG],
                                                                in_=p_t[pb_][0:64, :].rearrange("p (a b) -> p a b", b=128)[:, :, 0:NPG]),
                         reads=[("e_pt", pb_)], writes=["e_kiTs"])
                for jb in range(32):
                    for hh in range(8):
                        pb_ = it % 2
                        it += 1
                        for jj in range(4):
                            S.op("pe", lambda e, hh=hh, jb=jb, jj=jj, pb_=pb_: e.matmul(p_s[pb_][0:8, jj * NPG:(jj + 1) * NPG], lhsT=qi8[:, hh, :],
                                                                                       rhs=kiTs[:, 4 * jb + jj, 0:NPG], start=True, stop=True),
                                 reads=["e_qi8", "e_kiTs"], writes=[("e_ps", pb_)])
                        S.op("act", lambda e, pb_=pb_: e.activation(out=rl[pb_][:, 0:4 * NPG], in_=p_s[pb_][0:8, 0:4 * NPG], func=AF.Relu),
                             reads=[("e_ps", pb_)], writes=[("e_rl", pb_)])
                        accum(pb_, hh, slice(4 * jb * NPG, 4 * (jb + 1) * NPG))
            for hh in range(8):
                pb_ = it % 2
                it += 1
                S.op("pe", lambda e, hh=hh, pb_=pb_: e.matmul(p_s[pb_][0:8, 0:8], lhsT=qi8[:, hh, :], rhs=kin[:, :], start=True, stop=True),
                     reads=["e_qi8", "e_kin"], writes=[("e_ps", pb_)])
                S.op("act", lambda e, pb_=pb_: e.activation(out=rl[pb_][:, 0:8], in_=p_s[pb_][0:8, 0:8], func=AF.Relu),
                     reads=[("e_ps", pb_)], writes=[("e_rl", pb_)])
                accum(pb_, hh, slice(PAST, PAST + 8))
            S.op("dve", lambda e: e.tensor_reduce(out=sm[:, 5:6], in_=sc[:, :], axis=AX.X, op=ALU.max), reads=["e_sc"], writes=["e_sm"])
            S.op("dve", lambda e: e.tensor_reduce(out=sm[:, 0:1], in_=sc[:, :], axis=AX.X, op=ALU.min), reads=["e_sc"], writes=["e_sm"])
            S.op("dve", lambda e: e.tensor_tensor(out=sm[:, 1:2], in0=sm[:, 5:6], in1=sm[:, 0:1], op=ALU.subtract), reads=["e_sm"], writes=["e_sm"])

            def _asel(e):
                if "r" not in fillreg:
                    fillreg["r"] = e.to_reg(-1e30)
                return e.affine_select(out=sc[:, PAST:PAST + 8], in_=sc[:, PAST:PAST + 8], pattern=[[-1, 8]], base=0,
                                       channel_multiplier=1, compare_op=ALU.is_ge, fill=fillreg["r"])
            S.op("pool", _asel, reads=["e_sc", "e_sm"], writes=["e_sc"])
            for _ in range(NIT):
                S.op("dve", lambda e: e.tensor_scalar(out=sm[:, 1:2], in0=sm[:, 1:2], scalar1=0.5, scalar2=None, op0=ALU.mult),
                     reads=["e_sm"], writes=["e_sm"])
                S.op("dve", lambda e: e.tensor_tensor(out=sm[:, 2:3], in0=sm[:, 0:1], in1=sm[:, 1:2], op=ALU.add), reads=["e_sm"], writes=["e_sm"])
                S.op("dve", lambda e: e.tensor_scalar(out=mneg[:, :], in0=sc[:, :], scalar1=sm[:, 2:3], scalar2=None,
                                                      op0=ALU.is_ge, op1=ALU.add, accum_out=sm[:, 3:4]),
                     reads=["e_sc", "e_sm"], writes=["e_mneg", "e_sm"])
                S.op("dve", lambda e: e.tensor_scalar(out=sm[:, 4:5], in0=sm[:, 3:4], scalar1=float(TOPK), scalar2=None, op0=ALU.is_ge),
                     reads=["e_sm"], writes=["e_sm"])
                S.op("dve", lambda e: e.scalar_tensor_tensor(out=sm[:, 0:1], in0=sm[:, 4:5], scalar=sm[:, 1:2], in1=sm[:, 0:1],
                                                             op0=ALU.mult, op1=ALU.add), reads=["e_sm"], writes=["e_sm"])
            S.op("dve", lambda e: e.tensor_scalar(out=sm[:, 6:7], in0=sm[:, 0:1], scalar1=-1e29, scalar2=None, op0=ALU.max),
                 reads=["e_sm"], writes=["e_sm"])
            S.op("dve", lambda e: e.tensor_scalar(out=mneg[:, :], in0=sc[:, :], scalar1=sm[:, 6:7], scalar2=-30000.0,
                                                  op0=ALU.is_lt, op1=ALU.mult), reads=["e_sc", "e_sm"], writes=["e_mneg"])
            with ExitStack() as st3:
                Kq = sb(nc, st3, "e_Kq", [128, 32, 256], F32)
                Vq = sb(nc, st3, "e_Vq", [128, 32, 256], F32)
                Vb = sb(nc, st3, "e_Vb", [128, 32, 4, 65], BF16)
                S.op("pool", lambda e: e.memset(Vb[:, :, :, 64:65], 1.0), writes=["e_Vb"])
                first = [True] * 4
                for q in range(4):
                    S.dma("pool", lambda e, q=q: e.indirect_dma_start(out=Kq[0:NPG, :, :].rearrange("p a b -> p (a b)"), out_offset=None,
                                                                     in_=c.cache_k[layer],
                                                                     in_offset=bass.IndirectOffsetOnAxis(ap=idxq[q][0:NPG, 0:1], axis=0)),
                          reads=[("e_idxq", q)], writes=["e_Kq"])
                    S.dma("pool", lambda e, q=q: e.indirect_dma_start(out=Vq[0:NPG, :, :].rearrange("p a b -> p (a b)"), out_offset=None,
                                                                     in_=c.cache_v[layer],
                                                                     in_offset=bass.IndirectOffsetOnAxis(ap=idxq[q][0:NPG, 0:1], axis=0)),
                          reads=[("e_idxq", q)], writes=["e_Vq"])
                    S.op("dve", lambda e: e.tensor_copy(out=Vb[0:NPG, :, :, 0:64], in_=Vq[0:NPG, :, :].rearrange("p a (h d) -> p a h d", d=64)),
                         reads=["e_Vq"], writes=["e_Vb"])
                    for kvh in range(4):
                        for jb in range(8):
                            tb = it % 2
                            it += 1
                            for jj in range(4):
                                jl = 4 * jb + jj
                                S.op("pe", lambda e, jl=jl, jj=jj, kvh=kvh, tb=tb: e.matmul(
                                    p_t[tb][0:64, jj * 128:jj * 128 + NPG], lhsT=Kq[0:NPG, jl, kvh * 64:(kvh + 1) * 64],
                                    rhs=c.ident_f[0:NPG, 0:NPG], start=True, stop=True),
                                    reads=["e_Kq", "ident_f"], writes=[("e_pt", tb)])
                            S.op("act", lambda e, tb=tb: e.copy(out=kTj[tb][:, :, 0:NPG],
                                                                in_=p_t[tb][0:64, :].rearrange("p (a b) -> p a b", b=128)[:, :, 0:NPG]),
                                 reads=[("e_pt", tb)], writes=[("e_kTj", tb)])
                            for jj in range(4):
                                jl = 4 * jb + jj
                                jg = 32 * q + jl
                                pb_ = it % 2
                                it += 1
                                S.op("pe", lambda e, jj=jj, kvh=kvh, tb=tb, pb_=pb_: e.matmul(
                                    p_s[pb_][0:NPG, 0:32], lhsT=kTj[tb][:, jj, 0:NPG], rhs=QT4[:, 4 * kvh:4 * kvh + 4, :].rearrange("d h t -> d (h t)"),
                                    start=True, stop=False), reads=[("e_kTj", tb), "e_QT4"], writes=[("e_ps", pb_)])
                                S.op("pe", lambda e, jg=jg, pb_=pb_: e.matmul(
                                    p_s[pb_][0:NPG, 0:32], lhsT=mneg[0:8, jg * NPG:(jg + 1) * NPG], rhs=I4s[:, :, :].rearrange("p h t -> p (h t)"),
                                    start=False, stop=True), reads=["e_mneg", "e_I4s"], writes=[("e_ps", pb_)])
                                S.op("act", lambda e, pb_=pb_: e.activation(out=PT[pb_][0:NPG, :], in_=p_s[pb_][0:NPG, 0:32], func=AF.Exp),
                                     reads=[("e_ps", pb_)], writes=[("e_PT", pb_)])
                                S.op("pe", lambda e, jl=jl, kvh=kvh, pb_=pb_, fst=first[kvh]: e.matmul(
                                    p_o[kvh][0:65, 0:32], lhsT=Vb[0:NPG, jl, kvh, :], rhs=PT[pb_][0:NPG, :], start=fst, stop=False),
                                    reads=["e_Vb", ("e_PT", pb_)], writes=[("e_po", kvh)])
                                first[kvh] = False
                for kvh in range(4):
                    pb_ = it % 2
                    it += 1
                    S.op("pe", lambda e, kvh=kvh, pb_=pb_: e.matmul(p_s[pb_][0:8, 0:32], lhsT=kTn[:, kvh, :],
                                                                   rhs=QT4[:, 4 * kvh:4 * kvh + 4, :].rearrange("d h t -> d (h t)"), start=True, stop=False),
                         reads=["e_kTn", "e_QT4"], writes=[("e_ps", pb_)])
                    S.op("pe", lambda e, pb_=pb_: e.matmul(p_s[pb_][0:8, 0:32], lhsT=mneg[0:8, PAST:PAST + 8],
                                                          rhs=I4s[:, :, :].rearrange("p h t -> p (h t)"), start=False, stop=True),
                         reads=["e_mneg", "e_I4s"], writes=[("e_ps", pb_)])
                    S.op("act", lambda e, pb_=pb_: e.activation(out=PT[pb_][0:8, :], in_=p_s[pb_][0:8, 0:32], func=AF.Exp),
                         reads=[("e_ps", pb_)], writes=[("e_PT", pb_)])
                    S.op("pe", lambda e, kvh=kvh, pb_=pb_: e.matmul(p_o[kvh][0:65, 0:32], lhsT=Vn[0:8, kvh, :], rhs=PT[pb_][0:8, :], start=False, stop=True),
                         reads=["e_Vn", ("e_PT", pb_)], writes=[("e_po", kvh)])
                    S.op("dve", lambda e, kvh=kvh: e.reciprocal(out=rc[64:65, :], in_=p_o[kvh][64:65, 0:32]), reads=[("e_po", kvh)], writes=["e_rc"])
                    tb = it % 2
                    it += 1
                    S.op("pe", lambda e, tb=tb: e.matmul(p_t[tb][0:64, 0:32], lhsT=c.ones_f[64:65, 0:64], rhs=rc[64:65, :], start=True, stop=True),
                         reads=["e_rc", "ones_f"], writes=[("e_pt", tb)])
                    S.op("act", lambda e, tb=tb: e.copy(out=bcs[:, :], in_=p_t[tb][0:64, 0:32]), reads=[("e_pt", tb)], writes=["e_bcs"])
                    S.op("dve", lambda e, kvh=kvh: e.tensor_tensor(out=OT[:, kvh, :], in0=p_o[kvh][0:64, 0:32], in1=bcs[:, :], op=ALU.mult),
                         reads=[("e_po", kvh), "e_bcs"], writes=["e_OT"])
                    S.dma("sp", lambda e, kvh=kvh, t0=t0: e.dma_start(
                        out=c.yaT[256 * kvh:256 * kvh + 256, t0:t0 + 8].rearrange("(g d) t -> d g t", d=64),
                        in_=OT[:, kvh, :].rearrange("d (g t) -> d g t", t=8)),
                        reads=["e_OT"], writes=tkeys("yaT", t0, t0 + 8))

def phase_merge(c, layer, tiles):
    nc, S = c.nc, c.S
    S.barrier()
    TS = c.TS
    wbs = c.w_bs[layer].rearrange("(ko ki) n -> ki ko n", ki=128)
    wba = c.w_ba[layer].rearrange("(ko ki) n -> ki ko n", ki=128)
    wo = c.w_o[layer].rearrange("(ko ki) n -> ki ko n", ki=128)
    with ExitStack() as st:
        yn = sb(nc, st, "m_yn", [128, 16, TS], BF16)
        ya = sb(nc, st, "m_ya", [128, 8, TS], BF16)
        mT = sb(nc, st, "m_mT", [128, 8, TS], BF16)
        wS = [sb(nc, st, f"m_wS{i}", [128, 24, 128], F32) for i in range(2)]
        wB = [sb(nc, st, f"m_wB{i}", [128, 24, 128], BF16) for i in range(2)]
        gsg = [sb(nc, st, f"m_g{i}", [128, 2, TS], F32) for i in range(2)]
        xs = [sb(nc, st, f"m_xs{i}", [128, TS], F32) for i in range(2)]
        m1 = [sb(nc, st, f"m_m1{i}", [128, 512], F32) for i in range(2)]
        m2 = [sb(nc, st, f"m_m2{i}", [128, 512], F32) for i in range(2)]
        xo = [sb(nc, st, f"m_xo{i}", [128, 512], F32) for i in range(2)]
        p1 = [ps(nc, st, f"m_p1{i}", [128, 512], F32) for i in range(2)]
        p2 = [ps(nc, st, f"m_p2{i}", [128, 512], F32) for i in range(2)]
        p3 = [ps(nc, st, f"m_p3{i}", [128, 512], F32) for i in range(2)]
        wi_ = 0
        it = 0
        for (t0, T) in tiles:
            nh = (T + 511) // 512
            S.dma("sp", lambda e, t0=t0, T=T: e.dma_start(out=yn[:, :, 0:T], in_=c.ynT[:, t0:t0 + T].rearrange("(k p) t -> p k t", p=128)),
                  reads=tkeys("ynT", t0, t0 + T), writes=["m_yn"])
            S.dma("sp", lambda e, t0=t0, T=T: e.dma_start(out=ya[:, :, 0:T], in_=c.yaT[:, t0:t0 + T].rearrange("(k p) t -> p k t", p=128)),
                  reads=tkeys("yaT", t0, t0 + T), writes=["m_ya"])
            for cc in range(8):
                b = wi_ % 2
                wi_ += 1
                cs_ = slice(cc * 128, cc * 128 + 128)
                S.dma("sp", lambda e, b=b, cs_=cs_: e.dma_start(out=wS[b][:, 0:16, :], in_=wbs[:, :, cs_]), writes=[("m_wS", b)])
                S.dma("sp", lambda e, b=b, cs_=cs_: e.dma_start(out=wS[b][:, 16:24, :], in_=wba[:, :, cs_]), writes=[("m_wS", b)])
                S.op("pool", lambda e, b=b: e.tensor_copy(out=wB[b][:, :, :], in_=wS[b][:, :, :]), reads=[("m_wS", b)], writes=[("m_wB", b)])
                S.dma("sp", lambda e, b=b, cs_=cs_, t0=t0, T=T: e.dma_start(out=gsg[b][:, 0, 0:T], in_=c.gsT[cs_, t0:t0 + T]),
                      reads=tkeys("gsT", t0, t0 + T), writes=[("m_g", b)])
                S.dma("sp", lambda e, b=b, cs_=cs_, t0=t0, T=T: e.dma_start(out=gsg[b][:, 1, 0:T], in_=c.gaT[cs_, t0:t0 + T]),
                      reads=tkeys("gaT", t0, t0 + T), writes=[("m_g", b)])
                for h in range(nh):
                    w = min(512, T - h * 512)
                    hs = slice(h * 512, h * 512 + w)
                    pb_ = it % 2
                    it += 1
                    for ko in range(16):
                        S.op("pe", lambda e, b=b, ko=ko, hs=hs, w=w, pb_=pb_: e.matmul(p1[pb_][:, 0:w], lhsT=wB[b][:, ko, :], rhs=yn[:, ko, hs],
                                                                                      start=(ko == 0), stop=(ko == 15)),
                             reads=[("m_wB", b), "m_yn"], writes=[("m_p1", pb_)])
                    for ko in range(8):
                        S.op("pe", lambda e, b=b, ko=ko, hs=hs, w=w, pb_=pb_: e.matmul(p2[pb_][:, 0:w], lhsT=wB[b][:, 16 + ko, :], rhs=ya[:, ko, hs],
                                                                                      start=(ko == 0), stop=(ko == 7)),
                             reads=[("m_wB", b), "m_ya"], writes=[("m_p2", pb_)])
                    S.op("dve", lambda e, b=b, hs=hs, w=w, pb_=pb_: e.tensor_tensor(out=m1[pb_][:, 0:w], in0=p1[pb_][:, 0:w], in1=gsg[b][:, 0, hs], op=ALU.mult),
                         reads=[("m_p1", pb_), ("m_g", b)], writes=[("m_m1", pb_)])
                    S.op("dve", lambda e, b=b, hs=hs, w=w, pb_=pb_: e.tensor_tensor(out=m2[pb_][:, 0:w], in0=p2[pb_][:, 0:w], in1=gsg[b][:, 1, hs], op=ALU.mult),
                         reads=[("m_p2", pb_), ("m_g", b)], writes=[("m_m2", pb_)])
                    S.op("pool", lambda e, cc=cc, hs=hs, w=w, pb_=pb_: e.tensor_tensor(out=mT[:, cc, hs], in0=m1[pb_][:, 0:w], in1=m2[pb_][:, 0:w], op=ALU.add),
                         reads=[("m_m1", pb_), ("m_m2", pb_)], writes=[("m_mT", cc)])
            for c2 in range(8):
                b = wi_ % 2
                wi_ += 1
                cs_ = slice(c2 * 128, c2 * 128 + 128)
                S.dma("sp", lambda e, b=b, cs_=cs_: e.dma_start(out=wS[b][:, 0:8, :], in_=wo[:, :, cs_]), writes=[("m_wS", b)])
                S.op("pool", lambda e, b=b: e.tensor_copy(out=wB[b][:, 0:8, :], in_=wS[b][:, 0:8, :]), reads=[("m_wS", b)], writes=[("m_wB", b)])
                S.dma("sp", lambda e, b=b, cs_=cs_, t0=t0, T=T: e.dma_start(out=xs[b][:, 0:T], in_=c.xT[cs_, t0:t0 + T]),
                      reads=tkeys("xT", t0, t0 + T), writes=[("m_xs", b)])
                for h in range(nh):
                    w = min(512, T - h * 512)
                    hs = slice(h * 512, h * 512 + w)
                    pb_ = it % 2
                    it += 1
                    for cc in range(8):
                        S.op("pe", lambda e, b=b, cc=cc, hs=hs, w=w, pb_=pb_: e.matmul(p3[pb_][:, 0:w], lhsT=wB[b][:, cc, :], rhs=mT[:, cc, hs],
                                                                                      start=(cc == 0), stop=(cc == 7)),
                             reads=[("m_wB", b), ("m_mT", cc)], writes=[("m_p3", pb_)])
                    S.op("dve", lambda e, b=b, hs=hs, w=w, pb_=pb_: e.tensor_tensor(out=xo[pb_][:, 0:w], in0=p3[pb_][:, 0:w], in1=xs[b][:, hs], op=ALU.add),
                         reads=[("m_p3", pb_), ("m_xs", b)], writes=[("m_xo", pb_)])
                    a0 = t0 + h * 512
                    S.dma("sp", lambda e, cs_=cs_, a0=a0, w=w, pb_=pb_: e.dma_start(out=c.xT[cs_, a0:a0 + w], in_=xo[pb_][:, 0:w]),
                          reads=[("m_xo", pb_)], writes=tkeys("xT", a0, a0 + w))

def phase_final(c, tiles_out):
    nc, S = c.nc, c.S
    S.barrier()
    gain = c.gains[:, 6 * 8:6 * 8 + 8]
    with ExitStack() as st:
        xs = sb(nc, st, "y_xs", [128, 8, 128], F32)
        hn = sb(nc, st, "y_hn", [128, 8, 128], F32)
        yo = [sb(nc, st, f"y_o{i}", [128, D], F32) for i in range(2)]
        pt = [ps(nc, st, f"y_p{i}", [128, D], F32) for i in range(2)]
        with ExitStack() as st2:
            sq = [sb(nc, st2, f"y_sq{i}", [128, 128], F32) for i in range(2)]
            rstd = sb(nc, st2, "y_rstd", [128, 128], F32)
            pss = ps(nc, st2, "y_pss", [128, 128], F32)
            for i, (t0, n, dst) in enumerate(tiles_out):
                b = i % 2
                S.dma("sp", lambda e, t0=t0, n=n: e.dma_start(out=xs[:, :, 0:n],
                                                             in_=c.xT[:, t0:t0 + n].rearrange("(dc p) t -> p dc t", p=128)),
                      reads=[("xT", t0 // 128)], writes=["y_xs"])
                for dc in range(8):
                    sb_ = dc % 2
                    S.op("act", lambda e, sb_=sb_, dc=dc, n=n: e.activation(out=sq[sb_][:, 0:n], in_=xs[:, dc, 0:n], func=AF.Square),
                         reads=["y_xs"], writes=[("y_sq", sb_)])
                    S.op("pe", lambda e, sb_=sb_, dc=dc, n=n: e.matmul(pss[:, 0:n], lhsT=c.ones_f[:, :], rhs=sq[sb_][:, 0:n],
                                                                       start=(dc == 0), stop=(dc == 7)),
                         reads=[("y_sq", sb_), "ones_f"], writes=["y_pss"])
                S.op("act", lambda e, n=n: e.activation(out=rstd[:, 0:n], in_=pss[:, 0:n], func=AF.Sqrt, scale=1.0 / D,
                                                        bias=c.eps_t[:, 0:1]),
                     reads=["y_pss", "consts"], writes=["y_rstd"])
                S.op("dve", lambda e, n=n: e.reciprocal(out=rstd[:, 0:n], in_=rstd[:, 0:n]), reads=["y_rstd"], writes=["y_rstd"])
                for dc in range(8):
                    S.op("dve", lambda e, dc=dc, n=n: e.scalar_tensor_tensor(out=hn[:, dc, 0:n], in0=xs[:, dc, 0:n],
                                                                           scalar=gain[:, dc:dc + 1], in1=rstd[:, 0:n],
                                                                           op0=ALU.mult, op1=ALU.mult),
                         reads=["y_xs", "y_rstd", "gains"], writes=["y_hn"])
                for dc in range(8):
                    S.op("pe", lambda e, b=b, dc=dc, n=n: e.transpose(pt[b][0:n, dc * 128:(dc + 1) * 128], hn[:, dc, 0:n],
                                                                     c.ident_f[:, :]),
                         reads=["y_hn", "ident_f"], writes=[("y_p", b)])
                S.op("act", lambda e, b=b, n=n: e.copy(out=yo[b][0:n, :], in_=pt[b][0:n, :]),
                     reads=[("y_p", b)], writes=[("y_o", b)])
                S.dma("sp", lambda e, b=b, n=n, dst=dst: e.dma_start(out=dst, in_=yo[b][0:n, :]),
                      reads=[("y_o", b)], writes=[("yout", i)], is_output=True)


def setup_consts(c, st):
    nc, S = c.nc, c.S
    c.ident_f = sb(nc, st, "ident_f", [128, 128], F32)
    c.ones_f = sb(nc, st, "ones_f", [128, 128], F32)
    c.eps_t = sb(nc, st, "eps_t", [128, 1], F32)
    c.one_t = sb(nc, st, "one_t", [128, 1], F32)
    c.gains = sb(nc, st, "gains_sb", [128, 7 * 8], F32)
    c.ident_b = sb(nc, st, "ident_b", [128, 128], BF16)
    c.convw = sb(nc, st, "convw", [128, 24, 4], F32)
    c.convb = sb(nc, st, "convb", [128, 24], F32)
    c.dtb_rep = sb(nc, st, "dtb_rep", [128, 32], F32)
    c.a_rep = sb(nc, st, "a_rep", [128, 32], F32)
    c.dsk_rep = sb(nc, st, "dsk_rep", [128, 32], F32)
    c.normg_rep = sb(nc, st, "normg_rep", [128, 2048], F32)
    S.op("pool", lambda e: e.memset(c.ones_f[:, :], 1.0), writes=["ones_f"])
    S.op("pool", lambda e: e.memset(c.eps_t[:, :], EPS), writes=["consts"])
    S.op("pool", lambda e: e.memset(c.one_t[:, :], 1.0), writes=["consts"])
    S.op("pool", lambda e: e.memset(c.ident_f[:, :], 1.0), writes=["ident_f"])
    S.op("pool", lambda e: e.affine_select(out=c.ident_f[:, :], in_=c.ident_f[:, :], pattern=[[-1, 128]], base=0,
                                           channel_multiplier=1, compare_op=ALU.is_equal, fill=0.0),
         reads=["ident_f"], writes=["ident_f"])
    S.op("dve", lambda e: e.tensor_copy(out=c.ident_b[:, :], in_=c.ident_f[:, :]), reads=["ident_f"], writes=["ident_b"])
    S.dma("sp", lambda e: e.dma_start(out=c.gains[:, :].rearrange("p (g dc) -> p g dc", dc=8),
                                      in_=c.gains_d.rearrange("g (dc p) -> p g dc", p=128),
                                      allow_slow_non_contiguous=True),
          writes=["gains"])


def build(cfg):
    nc = bass.Bass("TRN2", target_bir_lowering=False)
    c = Ctx()
    c.nc = nc
    SEQ, NSEQ, DEPTH = cfg["SEQ"], cfg["NSEQ"], cfg["DEPTH"]
    NS = NSEQ * 8
    NT = SEQ + NS
    c.SEQ, c.NSEQ, c.NS, c.NT, c.DEPTH = SEQ, NSEQ, NS, NT, DEPTH
    c.NPAGES, c.NPOOL = cfg["NPAGES"], cfg["NPOOL"]
    c.TS = cfg["TS"]
    upto = cfg.get("upto", 99)

    def din(name, shape, dt=F32):
        return nc.dram_tensor(name, list(shape), dt, kind="ExternalInput").ap()

    def dout(name, shape, dt=F32):
        return nc.dram_tensor(name, list(shape), dt, kind="ExternalOutput").ap()

    def dscr(name, shape, dt=F32):
        return nc.dram_tensor(name, list(shape), dt, kind="Internal").ap()

    c.xp = din("x_prompt", [SEQ, D])
    c.xsm = din("x_sample", [NS, D])
    c.gains_d = din("gains", [7, D])
    w1d = [din(f"ffn{i + 1}_w1", [DEPTH, D, 2 * DFF]) for i in range(2)]
    w2d = [din(f"ffn{i + 1}_w2", [DEPTH, DFF, D]) for i in range(2)]
    c.w1 = [[w1d[i][l] for l in range(DEPTH)] for i in range(2)]
    c.w2 = [[w2d[i][l] for l in range(DEPTH)] for i in range(2)]
    c.w_in = din("w_in", [DEPTH, D, DPROJ])
    c.conv_w_d = din("conv_w", [DEPTH, 4, 3072])
    c.conv_b_d = din("conv_b", [DEPTH, 3072])
    c.dt_bias_d = din("dt_bias", [DEPTH, 32])
    c.a_log_d = din("a_log", [DEPTH, 32])
    c.d_skip_d = din("d_skip", [DEPTH, 32])
    c.ssd_norm_d = din("ssd_norm", [DEPTH, 2048])
    c.w_bs = din("w_branch_ssd", [DEPTH, 2048, D])
    c.w_ba = din("w_branch_attn", [DEPTH, D, D])
    c.w_o = din("w_out", [DEPTH, D, D])
    c.sconv = din("state_conv", [DEPTH, NSEQ, 3, 3072])
    c.sssm = din("state_ssm", [DEPTH, NSEQ, 32, 64, 128])
    if cfg.get("sample_dsa", False):
        c.cache_k = [din(f"cache_k{l}", [c.NPOOL * 4, 8192]) for l in range(DEPTH)]
        c.cache_v = [din(f"cache_v{l}", [c.NPOOL * 4, 8192]) for l in range(DEPTH)]
        c.cache_i = [din(f"cache_idx_k{l}", [c.NPOOL, 128 * 64]) for l in range(DEPTH)]
        c.ptab = din("page_table", [NSEQ, c.NPAGES], I32)

    c.yp = dout("y_prompt", [SEQ, D])
    c.ys = dout("y_sample", [NS, D])
    c.ok_p = dout("ok_p", [DEPTH, SEQ, 256])
    c.ov_p = dout("ov_p", [DEPTH, SEQ, 256])
    c.oi_p = dout("oi_p", [DEPTH, SEQ, 64])
    c.oh_p = dout("oh_p", [DEPTH, 32, 64, 128])
    c.oc_p = dout("oc_p", [DEPTH, 3, 3072])
    c.ok_s = dout("ok_s", [DEPTH, NS, 256])
    c.ov_s = dout("ov_s", [DEPTH, NS, 256])
    c.oi_s = dout("oi_s", [DEPTH, NS, 64])
    c.oh_s = dout("oh_s", [DEPTH, NSEQ, 32, 64, 128])
    c.oc_s = dout("oc_s", [DEPTH, NSEQ, 3, 3072])

    c.xT = dscr("xT_scratch", [D, NT])
    c.zS = dscr("zS", [NT, 2048])
    c.xbcT = dscr("xbcT", [3072, NT])
    c.dtS = dscr("dtS", [NT, 32])
    c.qT = dscr("qT", [16, 64, NT], BF16)
    c.kT = dscr("kT", [4, 64, NT], BF16)
    c.qiT = dscr("qiT", [8, 64, NT], BF16)
    c.kiT = dscr("kiT", [1, 64, NT], BF16)
    c.vB = dscr("vB", [NT, 256], BF16)
    c.wiS = dscr("wiS", [NT, 8])
    c.gsT = dscr("gsT", [D, NT])
    c.gaT = dscr("gaT", [D, NT])
    c.xS = dscr("xS", [NT, 2048])
    c.Btm = dscr("Btm", [NT, 512], BF16)
    c.BT = dscr("BT", [512, NT], BF16)
    c.CT = dscr("CT", [512, NT], BF16)
    c.ynT = dscr("ynT", [2048, NT], BF16)
    c.yaT = dscr("yaT", [D, NT], BF16)

    tiles = [(t0, min(c.TS, SEQ - t0)) for t0 in range(0, SEQ, c.TS)] + [(SEQ, NS)]
    wins = [(t0, 1, min(c.TS, SEQ - t0)) for t0 in range(0, SEQ, c.TS)] + [(SEQ, NSEQ, 8)]
    with ExitStack() as st:
        c.S = Sched(nc, st)
        setup_consts(c, st)
        phase_to_fm(c, c.xp, 0, SEQ, 0)
        phase_to_fm(c, c.xsm, 0, NS, SEQ)
        for l in range(DEPTH):
            load_layer_params(c, l)
            if upto >= 1:
                phase_ffn(c, l, 0, tiles)
            if upto >= 2:
                phase_inproj(c, l, tiles)
                phase_conv(c, l, wins)
            if upto >= 3:
                phase_ssd(c, l)
            if upto >= 4:
                phase_dsa_prompt(c, l)
            if upto >= 5 and cfg.get("sample_dsa", False):
                phase_dsa_sample(c, l)
            if upto >= 5:
                phase_merge(c, l, tiles)
                phase_ffn(c, l, 1, tiles)
        outs = [(t0, min(128, SEQ - t0), c.yp[t0:t0 + min(128, SEQ - t0), :]) for t0 in range(0, SEQ, 128)]
        outs.append((SEQ, NS, c.ys[0:NS, :]))
        phase_final(c, outs)
        c.S.finish()
        c.S.emit()
    return nc


def make_in_maps(inputs, cfg):
    SEQ, NSEQ, DEPTH = cfg["SEQ"], cfg["NSEQ"], cfg["DEPTH"]
    NS = NSEQ * 8
    f = lambda k: np.ascontiguousarray(np.asarray(inputs[k]))
    xp, xs = f("x_prompt"), f("x_sample")
    gains = np.ascontiguousarray(np.concatenate(
        [np.stack([f("ffn1_norm")[l], f("mix_norm")[l], f("ffn2_norm")[l]]) for l in range(DEPTH)]
        + [np.zeros((3, D), np.float32)] * (2 - DEPTH) + [f("final_norm")[None]], 0))
    shared = {"gains": gains}
    for k in ("ffn1_w1", "ffn1_w2", "ffn2_w1", "ffn2_w2", "w_in", "conv_w", "conv_b", "dt_bias", "a_log", "d_skip", "ssd_norm",
              "w_branch_ssd", "w_branch_attn", "w_out"):
        shared[k] = f(k)
    if cfg.get("sample_dsa", False):
        ck, cv, ci = f("cache_k"), f("cache_v"), f("cache_idx_k")
        npool = ck.shape[1]
        for l in range(DEPTH):
            shared[f"cache_k{l}"] = np.ascontiguousarray(ck[l].reshape(npool * 4, 8192))
            shared[f"cache_v{l}"] = np.ascontiguousarray(cv[l].reshape(npool * 4, 8192))
            shared[f"cache_idx_k{l}"] = np.ascontiguousarray(ci[l].reshape(npool, 128 * 64))
    sconv, sssm, pt = f("state_conv"), f("state_ssm"), f("page_table")
    in_maps = []
    for cid in range(8):
        m = dict(shared)
        m["x_prompt"] = xp[cid % xp.shape[0]]
        sl = slice(cid * NSEQ, (cid + 1) * NSEQ)
        m["x_sample"] = np.ascontiguousarray(xs[sl].reshape(NS, D))
        m["state_conv"] = np.ascontiguousarray(sconv[:, sl])
        m["state_ssm"] = np.ascontiguousarray(sssm[:, sl])
        if cfg.get("sample_dsa", False):
            m["page_table"] = np.ascontiguousarray(pt[sl]).astype(np.int32)
        in_maps.append(m)
    return in_maps


def gather_outputs(r, cfg, nb):
    SEQ, NSEQ, DEPTH = cfg["SEQ"], cfg["NSEQ"], cfg["DEPTH"]
    cat_p = lambda k, sh: np.stack([r[b][k] for b in range(nb)], 1).reshape(sh).astype(np.float32)
    cat_s = lambda k, sh: np.concatenate([r[cid][k].reshape((DEPTH, NSEQ) + r[cid][k].shape[1:][1:] if False else r[cid][k].shape) for cid in range(8)], 1)
    y_prompt = np.stack([r[b]["y_prompt"] for b in range(nb)]).astype(np.float32)
    y_sample = np.concatenate([r[cid]["y_sample"].reshape(NSEQ, 8, D) for cid in range(8)], 0).astype(np.float32)
    kp = cat_p("ok_p", (DEPTH, nb, SEQ, 4, 64))
    vp = cat_p("ov_p", (DEPTH, nb, SEQ, 4, 64))
    ip = cat_p("oi_p", (DEPTH, nb, SEQ, 64))
    hp = cat_p("oh_p", (DEPTH, nb, 32, 64, 128))
    cp = cat_p("oc_p", (DEPTH, nb, 3, 3072))
    ks = np.concatenate([r[cid]["ok_s"].reshape(DEPTH, NSEQ, 8, 4, 64) for cid in range(8)], 1).astype(np.float32)
    vs = np.concatenate([r[cid]["ov_s"].reshape(DEPTH, NSEQ, 8, 4, 64) for cid in range(8)], 1).astype(np.float32)
    is_ = np.concatenate([r[cid]["oi_s"].reshape(DEPTH, NSEQ, 8, 64) for cid in range(8)], 1).astype(np.float32)
    hs = np.concatenate([r[cid]["oh_s"] for cid in range(8)], 1).astype(np.float32)
    cs = np.concatenate([r[cid]["oc_s"] for cid in range(8)], 1).astype(np.float32)
    return (y_prompt, y_sample, kp, vp, ip, hp, cp, ks, vs, is_, hs, cs)


def kernel(**inputs):
    cfg = dict(SEQ=8192, NSEQ=4, NPAGES=128, NPOOL=5120, TS=1024, DEPTH=2, upto=5, sample_dsa=True)
    nc = build(cfg)
    in_maps = make_in_maps(inputs, cfg)
    res = run_bass_kernel_spmd(nc, in_maps, core_ids=list(range(8)))
    return gather_outputs(res.results, cfg, 2)
```

```python
import numpy as np
from contextlib import ExitStack
import concourse.bass as bass
import concourse.mybir as mybir
from concourse.bass_utils import run_bass_kernel_spmd

F32 = mybir.dt.float32
BF16 = mybir.dt.bfloat16
I32 = mybir.dt.int32
AF = mybir.ActivationFunctionType
ALU = mybir.AluOpType
AX = mybir.AxisListType

NDMA_SEM = 8
D = 1024
DFF = 2816
EPS = 1e-6


class Sched:
    ENGS = ("pe", "act", "dve", "pool", "sp")

    def __init__(self, nc, stack):
        self.nc = nc
        self.ops = {e: [] for e in self.ENGS}
        self.cnt = {e: 0 for e in self.ENGS}
        self.esem = {e: stack.enter_context(nc.semaphore("es_" + e)) for e in self.ENGS if e != "sp"}
        self.dsem = {q: [stack.enter_context(nc.semaphore(f"ds_{q}{i}")) for i in range(NDMA_SEM)]
                     for q in ("sp", "act", "pool")}
        self.dcnt = {q: 0 for q in self.dsem}
        self.dval = {q: [0] * NDMA_SEM for q in self.dsem}
        self.lastw = {}
        self.readers = {}
        self.seen = {e: {} for e in self.ENGS}
        self.out_tokens = []

    def _sem(self, name):
        if name[0] == "E":
            return self.esem[name[1:]]
        q, i = name[1:].split(":")
        return self.dsem[q][int(i)]

    def _deps(self, eng, reads, writes):
        toks = set()
        for k in reads:
            t = self.lastw.get(k)
            if t is not None:
                toks.add(t)
        for k in writes:
            t = self.lastw.get(k)
            if t is not None:
                toks.add(t)
            for r in self.readers.get(k, ()):
                toks.add(r)
        best = {}
        for (s, v, pe) in toks:
            if pe == "pe" and eng == "pe":
                continue
            if best.get(s, 0) < v:
                best[s] = v
        waits = []
        for s, v in best.items():
            if self.seen[eng].get(s, 0) >= v:
                continue
            self.seen[eng][s] = v
            waits.append((s, v))
        return waits

    def _commit(self, tok, reads, writes):
        for k in reads:
            self.readers.setdefault(k, []).append(tok)
        for k in writes:
            self.lastw[k] = tok
            self.readers[k] = []

    lim = None
    nrec = 0

    def _skip(self):
        if self.lim is None:
            return False
        self.nrec += 1
        return self.nrec > self.lim

    def op(self, eng, fn, reads=(), writes=()):
        if self._skip():
            return None
        waits = self._deps(eng, reads, writes)
        self.cnt[eng] += 1
        tok = ("E" + eng, self.cnt[eng], eng)
        self.ops[eng].append((waits, fn, (tok[0], 1)))
        self._commit(tok, reads, writes)
        return tok

    DRAM_KEYS = frozenset(["xT", "xbcT", "gsT", "gaT", "qT", "kT", "qiT", "kiT", "zS", "dtS", "vB", "wiS", "xS", "Btm", "BT", "CT",
                           "ynT", "yaT", "ok", "ov", "oi", "oh", "oc_p", "oc_s", "yout"])

    def dma(self, q, fn, reads=(), writes=(), is_output=False):
        if self._skip():
            return None
        if q == "sp" and any(isinstance(k, tuple) and k[0] in self.DRAM_KEYS for k in writes):
            q = "act"
        waits = self._deps(q, reads, writes)
        i = self.dcnt[q] % NDMA_SEM
        self.dcnt[q] += 1
        sname = f"D{q}:{i}"
        prev = self.dval[q][i]
        if prev > 0 and self.seen[q].get(sname, 0) < prev:
            self.seen[q][sname] = prev
            waits.append((sname, prev))
        self.dval[q][i] = prev + 16
        tok = (sname, prev + 16, "dma")
        self.ops[q].append((waits, fn, (sname, 16)))
        self._commit(tok, reads, writes)
        if is_output:
            self.out_tokens.append(tok)
        return tok

    def barrier(self):
        allw = {}
        for e in self.ENGS:
            if e != "sp" and self.cnt[e] > 0:
                allw["E" + e] = self.cnt[e]
        for q in self.dsem:
            for i in range(NDMA_SEM):
                if self.dval[q][i] > 0:
                    allw[f"D{q}:{i}"] = self.dval[q][i]
        for e in self.ENGS:
            waits = []
            for s_, v in allw.items():
                if self.seen[e].get(s_, 0) < v:
                    self.seen[e][s_] = v
                    waits.append((s_, v))
            if waits:
                self.ops[e].append((waits, None, None))

    def finish(self):
        best = {}
        for q in self.dsem:
            for i in range(NDMA_SEM):
                if self.dval[q][i] > 0:
                    best[f"D{q}:{i}"] = self.dval[q][i]
        self.ops["sp"].append((list(best.items()), None, None))

    def emit(self):
        nc = self.nc
        with nc.Block() as block:
            def run(engname):
                def body(eng):
                    for waits, fn, inc in self.ops[engname]:
                        for (s, v) in waits:
                            eng.wait_ge(self._sem(s), v)
                        if fn is None:
                            continue
                        ins = fn(eng)
                        ins.then_inc(self._sem(inc[0]), inc[1])
                return body
            block.tensor(run("pe"))
            block.scalar(run("act"))
            block.vector(run("dve"))
            block.gpsimd(run("pool"))
            block.sync(run("sp"))


class Ctx:
    pass


_UID = [0]


def sb(nc, st, name, shape, dt):
    _UID[0] += 1
    return st.enter_context(nc.sbuf_tensor(f"{name}_{_UID[0]}", list(shape), dt))


def ps(nc, st, name, shape, dt=F32):
    _UID[0] += 1
    return st.enter_context(nc.psum_tensor(f"{name}_{_UID[0]}", list(shape), dt))


def phase_to_fm(c, src, t_src0, ntok, t_dst0):
    nc, S = c.nc, c.S
    S.barrier()
    with ExitStack() as st:
        xin = [sb(nc, st, f"p0x{i}", [128, D], F32) for i in range(2)]
        xo = [sb(nc, st, f"p0o{i}", [128, 8, 128], F32) for i in range(2)]
        pt = [ps(nc, st, f"p0p{i}", [128, 8, 128], F32) for i in range(2)]
        nt = (ntok + 127) // 128
        for i in range(nt):
            n = min(128, ntok - i * 128)
            b = i % 2
            r0 = t_src0 + i * 128
            S.dma("sp", lambda e, b=b, r0=r0, n=n: e.dma_start(out=xin[b][0:n, :], in_=src[r0:r0 + n, :]),
                  writes=[("p0x", b)])
            for dc in range(8):
                S.op("pe", lambda e, b=b, dc=dc, n=n: e.transpose(pt[b][:, dc, 0:n], xin[b][0:n, dc * 128:(dc + 1) * 128],
                                                                 c.ident_f[0:n, 0:n]),
                     reads=[("p0x", b), "ident_f"], writes=[("p0p", b)])
            S.op("act", lambda e, b=b, n=n: e.copy(out=xo[b][:, :, 0:n], in_=pt[b][:, :, 0:n]),
                 reads=[("p0p", b)], writes=[("p0o", b)])
            d0 = t_dst0 + i * 128
            S.dma("sp", lambda e, b=b, d0=d0, n=n: e.dma_start(
                out=c.xT[:, d0:d0 + n].rearrange("(dc p) t -> p dc t", p=128), in_=xo[b][:, :, 0:n]),
                reads=[("p0o", b)], writes=[("xT", d0 // 128)])


def rmsnorm_fm(c, st, xs, xs_key, T, gain_ap, hnT, hn_key, tag):
    nc, S = c.nc, c.S
    sq = [sb(nc, st, f"{tag}sq{i}", [128, T], F32) for i in range(2)]
    rstd = sb(nc, st, f"{tag}rstd", [128, T], F32)
    nh = (T + 511) // 512
    pss = ps(nc, st, f"{tag}pss", [128, nh, 512], F32)
    for dc in range(8):
        b = dc % 2
        S.op("act", lambda e, b=b, dc=dc: e.activation(out=sq[b][:, :], in_=xs[:, dc, :], func=AF.Square),
             reads=[xs_key], writes=[(tag + "sq", b)])
        for h in range(nh):
            w = min(512, T - h * 512)
            S.op("pe", lambda e, b=b, dc=dc, h=h, w=w: e.matmul(pss[:, h, 0:w], lhsT=c.ones_f[:, :], rhs=sq[b][:, h * 512:h * 512 + w],
                                                              start=(dc == 0), stop=(dc == 7)),
                 reads=[(tag + "sq", b), "ones_f"], writes=[(tag + "pss")])
    for h in range(nh):
        w = min(512, T - h * 512)
        S.op("act", lambda e, h=h, w=w: e.activation(out=rstd[:, h * 512:h * 512 + w], in_=pss[:, h, 0:w], func=AF.Sqrt,
                                                    scale=1.0 / D, bias=c.eps_t[:, 0:1]),
             reads=[(tag + "pss"), "consts"], writes=[(tag + "rstd")])
    S.op("dve", lambda e: e.reciprocal(out=rstd[:, :], in_=rstd[:, :]), reads=[(tag + "rstd")], writes=[(tag + "rstd")])
    for dc in range(8):
        S.op("dve", lambda e, dc=dc: e.scalar_tensor_tensor(out=hnT[:, dc, :], in0=xs[:, dc, :], scalar=gain_ap[:, dc:dc + 1],
                                                          in1=rstd[:, :], op0=ALU.mult, op1=ALU.mult),
             reads=[xs_key, (tag + "rstd"), "gains"], writes=[hn_key])


def phase_ffn(c, layer, which, tiles):
    nc, S = c.nc, c.S
    S.barrier()
    w1 = c.w1[which][layer]
    w2 = c.w2[which][layer]
    gain = c.gains[:, (layer * 3 + (0 if which == 0 else 2)) * 8:(layer * 3 + (0 if which == 0 else 2)) * 8 + 8]
    TS = c.TS
    w1v = w1.rearrange("(ko ki) n -> ki ko n", ki=128)
    w2v = w2.rearrange("(fo fi) n -> fi fo n", fi=128)
    NF = DFF // 128
    with ExitStack() as st:
        xs = sb(nc, st, "f_xs", [128, 8, TS], F32)
        hnT = sb(nc, st, "f_hnT", [128, 8, TS], BF16)
        actT = sb(nc, st, "f_actT", [128, NF, TS], BF16)
        w1s = [sb(nc, st, f"f_w1s{i}", [128, 2, 8, 128], F32) for i in range(2)]
        w1b = [sb(nc, st, f"f_w1b{i}", [128, 2, 8, 128], BF16) for i in range(2)]
        w2s = [sb(nc, st, f"f_w2s{i}", [128, NF, 128], F32) for i in range(2)]
        w2b = [sb(nc, st, f"f_w2b{i}", [128, NF, 128], BF16) for i in range(2)]
        sa = [sb(nc, st, f"f_sa{i}", [128, 512], F32) for i in range(2)]
        xo = [sb(nc, st, f"f_xo{i}", [128, 512], F32) for i in range(2)]
        pa = [ps(nc, st, f"f_pa{i}", [128, 512], F32) for i in range(2)]
        pb = [ps(nc, st, f"f_pb{i}", [128, 512], F32) for i in range(2)]
        po = [ps(nc, st, f"f_po{i}", [128, 512], F32) for i in range(2)]
        wi = 0
        w2i = 0
        it = 0
        for (t0, T) in tiles:
            S.dma("sp", lambda e, t0=t0, T=T: e.dma_start(out=xs[:, :, 0:T],
                                                         in_=c.xT[:, t0:t0 + T].rearrange("(dc p) t -> p dc t", p=128)),
                  reads=[("xT", k) for k in range(t0 // 128, (t0 + T + 127) // 128)], writes=["f_xs"])
            with ExitStack() as st2:
                rmsnorm_fm(c, st2, xs[:, :, 0:T], "f_xs", T, gain, hnT[:, :, 0:T], "f_hnT", "fn")
            nh = (T + 511) // 512
            for j in range(NF):
                b = wi % 2
                wi += 1
                S.dma("sp", lambda e, b=b, j=j: e.dma_start(out=w1s[b][:, 0, :, :], in_=w1v[:, :, j * 128:(j + 1) * 128]),
                      writes=[("f_w1s", b)])
                S.dma("sp", lambda e, b=b, j=j: e.dma_start(out=w1s[b][:, 1, :, :], in_=w1v[:, :, DFF + j * 128:DFF + (j + 1) * 128]),
                      writes=[("f_w1s", b)])
                S.op("pool", lambda e, b=b: e.tensor_copy(out=w1b[b][:, :, :, :], in_=w1s[b][:, :, :, :]),
                     reads=[("f_w1s", b)], writes=[("f_w1b", b)])
                for h in range(nh):
                    w = min(512, T - h * 512)
                    pbuf = it % 2
                    it += 1
                    for ko in range(8):
                        S.op("pe", lambda e, b=b, ko=ko, h=h, w=w, pbuf=pbuf: e.matmul(
                            pa[pbuf][:, 0:w], lhsT=w1b[b][:, 0, ko, :], rhs=hnT[:, ko, h * 512:h * 512 + w],
                            start=(ko == 0), stop=(ko == 7)),
                            reads=[("f_w1b", b), "f_hnT"], writes=[("f_pa", pbuf)])
                    for ko in range(8):
                        S.op("pe", lambda e, b=b, ko=ko, h=h, w=w, pbuf=pbuf: e.matmul(
                            pb[pbuf][:, 0:w], lhsT=w1b[b][:, 1, ko, :], rhs=hnT[:, ko, h * 512:h * 512 + w],
                            start=(ko == 0), stop=(ko == 7)),
                            reads=[("f_w1b", b), "f_hnT"], writes=[("f_pb", pbuf)])
                    S.op("act", lambda e, w=w, pbuf=pbuf: e.activation(out=sa[pbuf][:, 0:w], in_=pa[pbuf][:, 0:w], func=AF.Silu),
                         reads=[("f_pa", pbuf)], writes=[("f_sa", pbuf)])
                    S.op("dve", lambda e, j=j, h=h, w=w, pbuf=pbuf: e.tensor_tensor(
                        out=actT[:, j, h * 512:h * 512 + w], in0=sa[pbuf][:, 0:w], in1=pb[pbuf][:, 0:w], op=ALU.mult),
                        reads=[("f_sa", pbuf), ("f_pb", pbuf)], writes=[("f_actT", j)])
            for cc in range(8):
                b = w2i % 2
                w2i += 1
                S.dma("sp", lambda e, b=b, cc=cc: e.dma_start(out=w2s[b][:, :, :], in_=w2v[:, :, cc * 128:(cc + 1) * 128]),
                      writes=[("f_w2s", b)])
                S.op("pool", lambda e, b=b: e.tensor_copy(out=w2b[b][:, :, :], in_=w2s[b][:, :, :]),
                     reads=[("f_w2s", b)], writes=[("f_w2b", b)])
                for h in range(nh):
                    w = min(512, T - h * 512)
                    pbuf = it % 2
                    it += 1
                    for fo in range(NF):
                        S.op("pe", lambda e, b=b, fo=fo, h=h, w=w, pbuf=pbuf: e.matmul(
                            po[pbuf][:, 0:w], lhsT=w2b[b][:, fo, :], rhs=actT[:, fo, h * 512:h * 512 + w],
                            start=(fo == 0), stop=(fo == NF - 1)),
                            reads=[("f_w2b", b), ("f_actT", fo)], writes=[("f_po", pbuf)])
                    S.op("dve", lambda e, cc=cc, h=h, w=w, pbuf=pbuf: e.scalar_tensor_tensor(
                        out=xo[pbuf][:, 0:w], in0=po[pbuf][:, 0:w], scalar=0.5, in1=xs[:, cc, h * 512:h * 512 + w],
                        op0=ALU.mult, op1=ALU.add),
                        reads=[("f_po", pbuf), "f_xs"], writes=[("f_xo", pbuf)])
                    a0 = t0 + h * 512
                    S.dma("sp", lambda e, cc=cc, a0=a0, w=w, pbuf=pbuf: e.dma_start(
                        out=c.xT[cc * 128:(cc + 1) * 128, a0:a0 + w], in_=xo[pbuf][:, 0:w]),
                        reads=[("f_xo", pbuf)], writes=[("xT", k) for k in range(a0 // 128, (a0 + w + 127) // 128)])


C_Z, C_XBC, C_DT, C_Q, C_K, C_V, C_QI, C_KI, C_WI, C_GS, C_GA = 0, 2048, 5120, 5152, 6176, 6432, 6688, 7200, 7264, 7272, 8296
DPROJ = 9320
IDX_W_SCALE = (8 ** -0.5) * (64 ** -0.5)


def inproj_slabs():
    sl = []
    for i in range(4):
        sl.append((C_Z + 512 * i, 512, [("tm", 0, 512, "z", 512 * i)]))
    for i in range(6):
        sl.append((C_XBC + 512 * i, 512, [("fm", 128 * j, 128, "xbc", 512 * i + 128 * j) for j in range(4)]))
    sl.append((C_DT, 32, [("tm", 0, 32, "dt", 0)]))
    for i in range(2):
        sl.append((C_Q + 512 * i, 512, [("fm", 64 * j, 64, "q", 8 * i + j) for j in range(8)]))
    sl.append((C_K, 512, [("tm", 0, 512, "kv", 0)] + [("fm", 64 * j, 64, "k", j) for j in range(4)]))
    sl.append((C_QI, 512, [("fm", 64 * j, 64, "qi", j) for j in range(8)]))
    sl.append((C_KI, 72, [("tm", 0, 72, "kiw", 0), ("fm", 0, 64, "ki", 0)]))
    for i in range(2):
        sl.append((C_GS + 512 * i, 512, [("fm", 128 * j, 128, "gs", 512 * i + 128 * j) for j in range(4)]))
    for i in range(2):
        sl.append((C_GA + 512 * i, 512, [("fm", 128 * j, 128, "ga", 512 * i + 128 * j) for j in range(4)]))
    return sl


def tkeys(name, a, b):
    return [(name, k) for k in range(a // 128, (b + 127) // 128)]


def phase_inproj(c, layer, tiles):
    nc, S = c.nc, c.S
    S.barrier()
    SEQ = c.SEQ
    wv = c.w_in[layer].rearrange("(ko ki) n -> ki ko n", ki=128)
    gain = c.gains[:, (layer * 3 + 1) * 8:(layer * 3 + 1) * 8 + 8]
    TS = c.TS
    slabs = inproj_slabs()
    with ExitStack() as st:
        xs = sb(nc, st, "a_xs", [128, 8, TS], F32)
        hnT = sb(nc, st, "a_hnT", [128, 8, TS], BF16)
        wS = [sb(nc, st, f"a_wS{i}", [128, 8, 512], F32) for i in range(2)]
        wB = [sb(nc, st, f"a_wB{i}", [128, 8, 512], BF16) for i in range(2)]
        sf = [sb(nc, st, f"a_sf{i}", [128, 512], F32) for i in range(3)]
        sh = [sb(nc, st, f"a_sh{i}", [128, 512], BF16) for i in range(3)]
        pf = [ps(nc, st, f"a_pf{i}", [128, 512], F32) for i in range(3)]
        wi_ = 0
        it = 0
        for (t0, T) in tiles:
            is_s = t0 >= SEQ
            S.dma("sp", lambda e, t0=t0, T=T: e.dma_start(out=xs[:, :, 0:T],
                                                         in_=c.xT[:, t0:t0 + T].rearrange("(dc p) t -> p dc t", p=128)),
                  reads=tkeys("xT", t0, t0 + T), writes=["a_xs"])
            with ExitStack() as st2:
                rmsnorm_fm(c, st2, xs[:, :, 0:T], "a_xs", T, gain, hnT[:, :, 0:T], "a_hnT", "an")
            for (c0, ncol, jobs) in slabs:
                b = wi_ % 2
                wi_ += 1
                S.dma("sp", lambda e, b=b, c0=c0, ncol=ncol: e.dma_start(out=wS[b][:, :, 0:ncol], in_=wv[:, :, c0:c0 + ncol]),
                      writes=[("a_wS", b)])
                S.op("pool", lambda e, b=b, ncol=ncol: e.tensor_copy(out=wB[b][:, :, 0:ncol], in_=wS[b][:, :, 0:ncol]),
                     reads=[("a_wS", b)], writes=[("a_wB", b)])
                for (kind, off, n, name, idx) in jobs:
                    if kind == "fm":
                        for h in range((T + 511) // 512):
                            w = min(512, T - h * 512)
                            a0 = t0 + h * 512
                            pb_ = it % 3
                            it += 1
                            for ko in range(8):
                                S.op("pe", lambda e, b=b, ko=ko, off=off, n=n, h=h, w=w, pb_=pb_: e.matmul(
                                    pf[pb_][0:n, 0:w], lhsT=wB[b][:, ko, off:off + n], rhs=hnT[:, ko, h * 512:h * 512 + w],
                                    start=(ko == 0), stop=(ko == 7)),
                                    reads=[("a_wB", b), "a_hnT"], writes=[("a_pf", pb_)])
                            if name == "xbc":
                                S.op("act", lambda e, n=n, w=w, pb_=pb_: e.copy(out=sf[pb_][0:n, 0:w], in_=pf[pb_][0:n, 0:w]),
                                     reads=[("a_pf", pb_)], writes=[("a_sf", pb_)])
                                S.dma("sp", lambda e, idx=idx, a0=a0, w=w, pb_=pb_: e.dma_start(
                                    out=c.xbcT[idx:idx + 128, a0:a0 + w], in_=sf[pb_][:, 0:w]),
                                    reads=[("a_sf", pb_)], writes=tkeys("xbcT", a0, a0 + w))
                            elif name in ("gs", "ga"):
                                dst = c.gsT if name == "gs" else c.gaT
                                S.op("act", lambda e, n=n, w=w, pb_=pb_: e.activation(out=sf[pb_][0:n, 0:w], in_=pf[pb_][0:n, 0:w],
                                                                                    func=AF.Sigmoid),
                                     reads=[("a_pf", pb_)], writes=[("a_sf", pb_)])
                                S.dma("sp", lambda e, dst=dst, idx=idx, a0=a0, w=w, pb_=pb_: e.dma_start(
                                    out=dst[idx:idx + 128, a0:a0 + w], in_=sf[pb_][:, 0:w]),
                                    reads=[("a_sf", pb_)], writes=tkeys(name + "T", a0, a0 + w))
                            else:
                                dst = {"q": c.qT, "k": c.kT, "qi": c.qiT, "ki": c.kiT}[name]
                                scl = 0.125 if name == "q" else 1.0
                                S.op("act", lambda e, n=n, w=w, pb_=pb_, scl=scl: e.mul(out=sh[pb_][0:n, 0:w], in_=pf[pb_][0:n, 0:w],
                                                                                      mul=scl),
                                     reads=[("a_pf", pb_)], writes=[("a_sh", pb_)])
                                S.dma("sp", lambda e, dst=dst, idx=idx, a0=a0, w=w, pb_=pb_: e.dma_start(
                                    out=dst[idx, :, a0:a0 + w], in_=sh[pb_][0:64, 0:w]),
                                    reads=[("a_sh", pb_)], writes=tkeys(name + "T", a0, a0 + w))
                    else:
                        for i in range((T + 127) // 128):
                            nt_ = min(128, T - i * 128)
                            a0 = t0 + i * 128
                            pb_ = it % 3
                            it += 1
                            for ko in range(8):
                                S.op("pe", lambda e, b=b, ko=ko, off=off, n=n, i=i, nt_=nt_, pb_=pb_: e.matmul(
                                    pf[pb_][0:nt_, 0:n], lhsT=hnT[:, ko, i * 128:i * 128 + nt_], rhs=wB[b][:, ko, off:off + n],
                                    start=(ko == 0), stop=(ko == 7)),
                                    reads=[("a_wB", b), "a_hnT"], writes=[("a_pf", pb_)])
                            S.op("act", lambda e, n=n, nt_=nt_, pb_=pb_: e.copy(out=sf[pb_][0:nt_, 0:n], in_=pf[pb_][0:nt_, 0:n]),
                                 reads=[("a_pf", pb_)], writes=[("a_sf", pb_)])
                            if name == "z":
                                S.dma("sp", lambda e, idx=idx, a0=a0, nt_=nt_, pb_=pb_: e.dma_start(
                                    out=c.zS[a0:a0 + nt_, idx:idx + 512], in_=sf[pb_][0:nt_, 0:512]),
                                    reads=[("a_sf", pb_)], writes=tkeys("zS", a0, a0 + nt_))
                            elif name == "dt":
                                S.dma("sp", lambda e, a0=a0, nt_=nt_, pb_=pb_: e.dma_start(
                                    out=c.dtS[a0:a0 + nt_, :], in_=sf[pb_][0:nt_, 0:32]),
                                    reads=[("a_sf", pb_)], writes=tkeys("dtS", a0, a0 + nt_))
                            elif name == "kv":
                                ko_, vo_ = (c.ok_s, c.ov_s) if is_s else (c.ok_p, c.ov_p)
                                r0 = a0 - SEQ if is_s else a0
                                S.dma("sp", lambda e, ko_=ko_, r0=r0, nt_=nt_, pb_=pb_: e.dma_start(
                                    out=ko_[layer, r0:r0 + nt_, :], in_=sf[pb_][0:nt_, 0:256]),
                                    reads=[("a_sf", pb_)], writes=[("ok", layer, a0)])
                                S.dma("sp", lambda e, vo_=vo_, r0=r0, nt_=nt_, pb_=pb_: e.dma_start(
                                    out=vo_[layer, r0:r0 + nt_, :], in_=sf[pb_][0:nt_, 256:512]),
                                    reads=[("a_sf", pb_)], writes=[("ov", layer, a0)])
                                S.op("dve", lambda e, nt_=nt_, pb_=pb_: e.tensor_copy(out=sh[pb_][0:nt_, 0:256], in_=sf[pb_][0:nt_, 256:512]),
                                     reads=[("a_sf", pb_)], writes=[("a_sh", pb_)])
                                S.dma("sp", lambda e, a0=a0, nt_=nt_, pb_=pb_: e.dma_start(
                                    out=c.vB[a0:a0 + nt_, :], in_=sh[pb_][0:nt_, 0:256]),
                                    reads=[("a_sh", pb_)], writes=tkeys("vB", a0, a0 + nt_))
                            elif name == "kiw":
                                io_ = c.oi_s if is_s else c.oi_p
                                r0 = a0 - SEQ if is_s else a0
                                S.dma("sp", lambda e, io_=io_, r0=r0, nt_=nt_, pb_=pb_: e.dma_start(
                                    out=io_[layer, r0:r0 + nt_, :], in_=sf[pb_][0:nt_, 0:64]),
                                    reads=[("a_sf", pb_)], writes=[("oi", layer, a0)])
                                S.op("dve", lambda e, nt_=nt_, pb_=pb_: e.tensor_scalar(
                                    out=sf[pb_][0:nt_, 64:72], in0=sf[pb_][0:nt_, 64:72], scalar1=IDX_W_SCALE, scalar2=None, op0=ALU.mult),
                                    reads=[("a_sf", pb_)], writes=[("a_sf", pb_)])
                                S.dma("sp", lambda e, a0=a0, nt_=nt_, pb_=pb_: e.dma_start(
                                    out=c.wiS[a0:a0 + nt_, :], in_=sf[pb_][0:nt_, 64:72]),
                                    reads=[("a_sf", pb_)], writes=tkeys("wiS", a0, a0 + nt_))


def load_layer_params(c, layer):
    S = c.S
    for i in range(4):
        S.dma("sp", lambda e, i=i: e.dma_start(out=c.convw[:, :, i], in_=c.conv_w_d[layer][i].rearrange("(k p) -> p k", p=128),
                                               allow_slow_non_contiguous=True), writes=["lay_conv"])
    S.dma("sp", lambda e: e.dma_start(out=c.convb[:, :], in_=c.conv_b_d[layer].rearrange("(k p) -> p k", p=128),
                                      allow_slow_non_contiguous=True), writes=["lay_conv"])
    S.dma("sp", lambda e: e.dma_start(out=c.dtb_rep[:, :], in_=c.dt_bias_d[layer].partition_broadcast(128)), writes=["lay_ssd"])
    S.dma("sp", lambda e: e.dma_start(out=c.a_rep[:, :], in_=c.a_log_d[layer].partition_broadcast(128)), writes=["lay_ssd"])
    S.dma("sp", lambda e: e.dma_start(out=c.dsk_rep[:, :], in_=c.d_skip_d[layer].partition_broadcast(128)), writes=["lay_ssd"])
    S.dma("sp", lambda e: e.dma_start(out=c.normg_rep[:, :], in_=c.ssd_norm_d[layer].partition_broadcast(128)), writes=["lay_ssd"])
    S.op("act", lambda e: e.activation(out=c.a_rep[:, :], in_=c.a_rep[:, :], func=AF.Exp), reads=["lay_ssd"], writes=["lay_ssd"])
    S.op("dve", lambda e: e.tensor_scalar(out=c.a_rep[:, :], in0=c.a_rep[:, :], scalar1=-1.0, scalar2=None, op0=ALU.mult),
         reads=["lay_ssd"], writes=["lay_ssd"])


def phase_conv(c, layer, wins):
    nc, S = c.nc, c.S
    S.barrier()
    SEQ = c.SEQ
    WMAX = max(3 + c.TS, c.NSEQ * 11)
    with ExitStack() as st:
        xw = [sb(nc, st, f"c_xw{i}", [128, 4, WMAX], F32) for i in range(2)]
        acc = sb(nc, st, "c_acc", [128, WMAX], F32)
        xc = sb(nc, st, "c_xc", [128, 4, c.TS], F32)
        xcb = sb(nc, st, "c_xcb", [128, 4, c.TS], BF16)
        of = [sb(nc, st, f"c_of{i}", [128, 512], F32) for i in range(2)]
        ob = [sb(nc, st, f"c_ob{i}", [128, 512], BF16) for i in range(2)]
        tail = sb(nc, st, "c_tail", [128, 24, c.NSEQ, 3], F32)
        ptf = [ps(nc, st, f"c_pf{i}", [128, 512], F32) for i in range(2)]
        ptb = [ps(nc, st, f"c_pb{i}", [128, 512], BF16) for i in range(2)]
        gi = 0
        it = 0
        for (t0, nseg, L) in wins:
            is_s = t0 >= SEQ
            T = nseg * L
            W = 3 + L
            for cg in range(6):
                b = gi % 2
                gi += 1
                rows = slice(cg * 512, (cg + 1) * 512)
                xv = xw[b][:, :, 0:nseg * W].rearrange("p k (s w) -> p k s w", w=W)
                for k in range(4):
                    r0 = cg * 512 + k * 128
                    for s_ in range(nseg):
                        S.dma("sp", lambda e, xv=xv, r0=r0, k=k, t0=t0, L=L, s_=s_: e.dma_start(
                            out=xv[:, k, s_, 3:3 + L], in_=c.xbcT[r0:r0 + 128, t0 + s_ * L:t0 + (s_ + 1) * L]),
                            reads=tkeys("xbcT", t0, t0 + T), writes=[("c_xw", b)])
                        if is_s:
                            S.dma("sp", lambda e, xv=xv, r0=r0, k=k, s_=s_: e.dma_start(
                                out=xv[:, k, s_, 0:3], in_=c.sconv[layer, s_, :, r0:r0 + 128].rearrange("i p -> p i"),
                                allow_slow_non_contiguous=True), writes=[("c_xw", b)])
                if is_s:
                    pass
                elif t0 == 0:
                    S.op("pool", lambda e, xv=xv: e.memset(xv[:, :, :, 0:3], 0.0), writes=[("c_xw", b)])
                else:
                    S.dma("sp", lambda e, xv=xv, rows=rows, t0=t0: e.dma_start(
                        out=xv[:, :, 0, 0:3], in_=c.xbcT[rows, t0 - 3:t0].rearrange("(k p) t -> p k t", p=128)),
                        reads=tkeys("xbcT", t0 - 3, t0), writes=[("c_xw", b)])
                av = acc[:, 0:T].rearrange("p (s t) -> p s t", t=L)
                for k in range(4):
                    cc = cg * 4 + k
                    S.op("dve", lambda e, xv=xv, av=av, k=k, cc=cc, L=L: e.tensor_scalar(
                        out=av, in0=xv[:, k, :, 0:L], scalar1=c.convw[:, cc, 0:1], scalar2=c.convb[:, cc:cc + 1],
                        op0=ALU.mult, op1=ALU.add), reads=[("c_xw", b), "lay_conv"], writes=["c_acc"])
                    for i in range(1, 4):
                        S.op("dve", lambda e, xv=xv, av=av, k=k, cc=cc, L=L, i=i: e.scalar_tensor_tensor(
                            out=av, in0=xv[:, k, :, i:i + L], scalar=c.convw[:, cc, i:i + 1], in1=av,
                            op0=ALU.mult, op1=ALU.add), reads=[("c_xw", b), "lay_conv", "c_acc"], writes=["c_acc"])
                    if cg < 4:
                        S.op("act", lambda e, k=k, T=T: e.activation(out=xc[:, k, 0:T], in_=acc[:, 0:T], func=AF.Silu),
                             reads=["c_acc"], writes=["c_xc"])
                    else:
                        S.op("act", lambda e, k=k, T=T: e.activation(out=xcb[:, k, 0:T], in_=acc[:, 0:T], func=AF.Silu),
                             reads=["c_acc"], writes=["c_xcb"])
                if cg >= 4:
                    dst = c.BT if cg == 4 else c.CT
                    nm = "BT" if cg == 4 else "CT"
                    S.dma("sp", lambda e, dst=dst, t0=t0, T=T: e.dma_start(
                        out=dst[:, t0:t0 + T].rearrange("(k p) t -> p k t", p=128), in_=xcb[:, :, 0:T]),
                        reads=["c_xcb"], writes=tkeys(nm, t0, t0 + T))
                if cg < 5:
                    for j in range((T + 127) // 128):
                        nt_ = min(128, T - j * 128)
                        a0 = t0 + j * 128
                        pb_ = it % 2
                        it += 1
                        if cg < 4:
                            for k in range(4):
                                S.op("pe", lambda e, k=k, j=j, nt_=nt_, pb_=pb_: e.transpose(
                                    ptf[pb_][0:nt_, k * 128:(k + 1) * 128], xc[:, k, j * 128:j * 128 + nt_], c.ident_f[:, :]),
                                    reads=["c_xc", "ident_f"], writes=[("c_pf", pb_)])
                            S.op("act", lambda e, nt_=nt_, pb_=pb_: e.copy(out=of[pb_][0:nt_, :], in_=ptf[pb_][0:nt_, :]),
                                 reads=[("c_pf", pb_)], writes=[("c_of", pb_)])
                            S.dma("sp", lambda e, cg=cg, a0=a0, nt_=nt_, pb_=pb_: e.dma_start(
                                out=c.xS[a0:a0 + nt_, cg * 512:(cg + 1) * 512], in_=of[pb_][0:nt_, :]),
                                reads=[("c_of", pb_)], writes=tkeys("xS", a0, a0 + nt_))
                        else:
                            for k in range(4):
                                S.op("pe", lambda e, k=k, j=j, nt_=nt_, pb_=pb_: e.transpose(
                                    ptb[pb_][0:nt_, k * 128:(k + 1) * 128], xcb[:, k, j * 128:j * 128 + nt_], c.ident_b[:, :]),
                                    reads=["c_xcb", "ident_b"], writes=[("c_pb", pb_)])
                            S.op("act", lambda e, nt_=nt_, pb_=pb_: e.copy(out=ob[pb_][0:nt_, :], in_=ptb[pb_][0:nt_, :]),
                                 reads=[("c_pb", pb_)], writes=[("c_ob", pb_)])
                            S.dma("sp", lambda e, a0=a0, nt_=nt_, pb_=pb_: e.dma_start(
                                out=c.Btm[a0:a0 + nt_, :], in_=ob[pb_][0:nt_, :]),
                                reads=[("c_ob", pb_)], writes=tkeys("Btm", a0, a0 + nt_))
        S.dma("sp", lambda e: e.dma_start(out=tail[:, :, 0, :], in_=c.xbcT[:, SEQ - 3:SEQ].rearrange("(k p) t -> p k t", p=128)),
              reads=tkeys("xbcT", SEQ - 3, SEQ), writes=["c_tail"])
        for i in range(3):
            S.dma("sp", lambda e, i=i: e.dma_start(out=c.oc_p[layer, i].rearrange("(k p) -> p k", p=128), in_=tail[:, :, 0, i],
                                                   allow_slow_non_contiguous=True), reads=["c_tail"], writes=[("oc_p", layer, i)])
        for s_ in range(c.NSEQ):
            a = SEQ + 8 * s_ + 5
            S.dma("sp", lambda e, s_=s_, a=a: e.dma_start(out=tail[:, :, s_, :], in_=c.xbcT[:, a:a + 3].rearrange("(k p) t -> p k t", p=128)),
                  reads=tkeys("xbcT", a, a + 3), writes=["c_tail"])
            for i in range(3):
                S.dma("sp", lambda e, s_=s_, i=i: e.dma_start(out=c.oc_s[layer, s_, i].rearrange("(k p) -> p k", p=128), in_=tail[:, :, s_, i],
                                                              allow_slow_non_contiguous=True), reads=["c_tail"], writes=[("oc_s", layer, s_, i)])


def phase_ssd(c, layer):
    nc, S = c.nc, c.S
    S.barrier()
    SEQ, NSEQ = c.SEQ, c.NSEQ
    with ExitStack() as st:
        x_tm = [sb(nc, st, f"s_x{i}", [128, 2048], F32) for i in range(2)]
        z_tm = [sb(nc, st, f"s_z{i}", [128, 2048], F32) for i in range(2)]
        B_tm = [sb(nc, st, f"s_B{i}", [128, 512], BF16) for i in range(2)]
        BTt = [sb(nc, st, f"s_BT{i}", [128, 4, 128], BF16) for i in range(2)]
        CTt = [sb(nc, st, f"s_CT{i}", [128, 4, 128], BF16) for i in range(2)]
        dtr = [sb(nc, st, f"s_dtr{i}", [128, 32], F32) for i in range(2)]
        xb = sb(nc, st, "s_xb", [128, 32], F32)
        ab = sb(nc, st, "s_ab", [128, 32], F32)
        dt = sb(nc, st, "s_dt", [128, 32], F32)
        la = sb(nc, st, "s_la", [128, 32], F32)
        acs = sb(nc, st, "s_acs", [128, 32], F32)
        acsT = sb(nc, st, "s_acsT", [128, 128], F32)
        Rm = [sb(nc, st, f"s_Rm{i}", [128, 8, 128], BF16) for i in range(3)]
        las = [sb(nc, st, f"s_las{i}", [128, 32], BF16) for i in range(3)]
        lres = sb(nc, st, "s_lres", [128, 32], F32)
        ones_b = sb(nc, st, "s_onesb", [128, 128], BF16)
        diff = sb(nc, st, "s_diff", [128, 8, 128], F32)
        dA = sb(nc, st, "s_dA", [128, 8, 128], F32)
        PAs = sb(nc, st, "s_PAs", [128, 8, 128], F32)
        CBs = sb(nc, st, "s_CBs", [128, 128], F32)
        CBL = sb(nc, st, "s_CBL", [128, 8, 128], BF16)
        Cdec = sb(nc, st, "s_Cdec", [128, 8, 128], BF16)
        xdt = sb(nc, st, "s_xdt", [128, 8, 64], BF16)
        xdec = sb(nc, st, "s_xdec", [128, 8, 64], BF16)
        dte = sb(nc, st, "s_dte", [128, 8], F32)
        t1 = sb(nc, st, "s_t1", [128, 512], F32)
        y1 = sb(nc, st, "s_y1", [128, 512], F32)
        sz = sb(nc, st, "s_sz", [128, 512], F32)
        jk = sb(nc, st, "s_jk", [128, 512], F32)
        ss = sb(nc, st, "s_ss", [128, 1], F32)
        y3 = sb(nc, st, "s_y3", [128, 512], BF16)
        yT = [sb(nc, st, f"s_yT{i}", [128, 4, 128], BF16) for i in range(2)]
        hT = sb(nc, st, "s_hT", [128, 32, 64], F32)
        hTb = sb(nc, st, "s_hTb", [128, 32, 64], BF16)
        h0 = sb(nc, st, "s_h0", [128, 32, 128], F32)
        utri = sb(nc, st, "s_utri", [128, 128], F32)
        negtri = sb(nc, st, "s_negtri", [128, 128], F32)
        p_acs = ps(nc, st, "s_pacs", [128, 4, 128], F32)
        PA0 = ps(nc, st, "s_PA0", [128, 4, 128], F32)
        CBT = ps(nc, st, "s_CBT", [128, 512], F32)
        py = ps(nc, st, "s_py", [128, 512], F32)
        pst = ps(nc, st, "s_pst", [128, 512], F32)
        pT = ps(nc, st, "s_pT", [128, 8, 128], BF16)
        PA1 = ps(nc, st, "s_PA1", [128, 4, 128], F32)
        PAh = [PA0, PA1]

        S.op("pool", lambda e: e.memset(utri[:, :], 1.0), writes=["s_utri"])
        S.op("pool", lambda e: e.affine_select(out=utri[:, :], in_=utri[:, :], pattern=[[1, 128]], base=0, channel_multiplier=-1,
                                               compare_op=ALU.is_ge, fill=0.0), reads=["s_utri"], writes=["s_utri"])
        S.op("pool", lambda e: e.memset(negtri[:, :], 0.0), writes=["s_negtri"])
        S.op("pool", lambda e: e.affine_select(out=negtri[:, :], in_=negtri[:, :], pattern=[[1, 128]], base=0, channel_multiplier=-1,
                                               compare_op=ALU.is_ge, fill=-30000.0), reads=["s_negtri"], writes=["s_negtri"])
        S.op("pool", lambda e: e.memset(ones_b[:, :], 1.0), writes=["s_onesb"])
        S.op("pool", lambda e: e.memset(h0[64:128, :, :], 0.0), writes=["s_h0"])

        def hkeys(name):
            return [(name, g) for g in range(4)]

        def load_state(sidx):
            import os
            if "LD_LIM" in os.environ:
                S.lim = int(os.environ["LD_LIM"])
                S.nrec = 0
            S.dma("sp", lambda e: e.dma_start(out=h0[0:64, :, :], in_=c.sssm[layer, sidx].rearrange("h p n -> p h n")),
                  writes=["s_h0"])
            for g in range(4):
                for j in range(8):
                    S.op("pe", lambda e, g=g, j=j: e.matmul(PAh[j // 4][:, j % 4, :], lhsT=h0[:, 8 * g + j, :], rhs=c.ident_f[:, :],
                                                            start=True, stop=True),
                         reads=["s_h0", "ident_f"], writes=["s_PA"])
                for hh in range(2):
                    S.op("act", lambda e, hh=hh: e.copy(out=PAs[:, 4 * hh:4 * hh + 4, :], in_=PAh[hh][:, :, :]),
                         reads=["s_PA"], writes=["s_PAs"])
                S.op("dve", lambda e, g=g: e.tensor_copy(out=hT[:, 8 * g:8 * g + 8, :], in_=PAs[:, :, 0:64]),
                     reads=["s_PAs"], writes=[("s_hT", g)])
                S.op("pool", lambda e, g=g: e.tensor_copy(out=hTb[:, 8 * g:8 * g + 8, :], in_=PAs[:, :, 0:64]),
                     reads=["s_PAs"], writes=[("s_hTb", g)])

        def store_state(dst):
            import os
            if "SS_LIM" in os.environ:
                S.lim = int(os.environ["SS_LIM"])
                S.nrec = 0
            for g in range(4):
                for j in range(8):
                    S.op("pe", lambda e, g=g, j=j: e.matmul(PAh[j // 4][0:64, j % 4, :], lhsT=hT[:, 8 * g + j, :], rhs=c.ident_f[:, :],
                                                            start=True, stop=True),
                         reads=[("s_hT", g), "ident_f"], writes=["s_PA"])
                for hh in range(2):
                    S.op("act", lambda e, g=g, hh=hh: e.copy(out=h0[0:64, 8 * g + 4 * hh:8 * g + 4 * hh + 4, :], in_=PAh[hh][0:64, :, :]),
                         reads=["s_PA"], writes=["s_h0"])
            S.dma("sp", lambda e: e.dma_start(out=dst.rearrange("h p n -> p h n"), in_=h0[0:64, :, :]),
                  reads=["s_h0"], writes=[("oh", id(dst))], is_output=True)

        ci = [0]

        def chunk(t0, L):
            b = ci[0] % 2
            ci[0] += 1
            S.dma("sp", lambda e: e.dma_start(out=x_tm[b][0:L, :], in_=c.xS[t0:t0 + L, :]), reads=tkeys("xS", t0, t0 + L), writes=[("s_x", b)])
            import os
            NOL = os.environ.get("SSD_NOLOAD", "")
            if "z" not in NOL:
                S.dma("sp", lambda e: e.dma_start(out=z_tm[b][0:L, :], in_=c.zS[t0:t0 + L, :]), reads=tkeys("zS", t0, t0 + L), writes=[("s_z", b)])
            if "B" not in NOL:
                S.dma("sp", lambda e: e.dma_start(out=B_tm[b][0:L, :], in_=c.Btm[t0:t0 + L, :]), reads=tkeys("Btm", t0, t0 + L), writes=[("s_B", b)])
            if "T" not in NOL:
                S.dma("sp", lambda e: e.dma_start(out=BTt[b][:, :, 0:L], in_=c.BT[:, t0:t0 + L].rearrange("(g n) t -> n g t", n=128)),
                      reads=tkeys("BT", t0, t0 + L), writes=[("s_BT", b)])
                S.dma("sp", lambda e: e.dma_start(out=CTt[b][:, :, 0:L], in_=c.CT[:, t0:t0 + L].rearrange("(g n) t -> n g t", n=128)),
                      reads=tkeys("CT", t0, t0 + L), writes=[("s_CT", b)])
            S.dma("sp", lambda e: e.dma_start(out=dtr[b][0:L, :], in_=c.dtS[t0:t0 + L, :]), reads=tkeys("dtS", t0, t0 + L), writes=[("s_dtr", b)])
            S.op("dve", lambda e: e.tensor_tensor(out=xb[0:L, :], in0=dtr[b][0:L, :], in1=c.dtb_rep[0:L, :], op=ALU.add),
                 reads=[("s_dtr", b), "lay_ssd"], writes=["s_xb"])
            S.op("dve", lambda e: e.scalar_tensor_tensor(out=ab[0:L, :], in0=xb[0:L, :], scalar=-1.0, in1=xb[0:L, :], op0=ALU.mult, op1=ALU.max),
                 reads=["s_xb"], writes=["s_ab"])
            S.op("act", lambda e: e.activation(out=ab[0:L, :], in_=ab[0:L, :], func=AF.Exp, scale=-1.0), reads=["s_ab"], writes=["s_ab"])
            S.op("act", lambda e: e.activation(out=ab[0:L, :], in_=ab[0:L, :], func=AF.Ln, bias=c.one_t[0:L, 0:1]),
                 reads=["s_ab", "consts"], writes=["s_ab"])
            S.op("dve", lambda e: e.scalar_tensor_tensor(out=dt[0:L, :], in0=xb[0:L, :], scalar=0.0, in1=ab[0:L, :], op0=ALU.max, op1=ALU.add),
                 reads=["s_xb", "s_ab"], writes=["s_dt"])
            S.op("dve", lambda e: e.tensor_tensor(out=la[0:L, :], in0=dt[0:L, :], in1=c.a_rep[0:L, :], op=ALU.mult),
                 reads=["s_dt", "lay_ssd"], writes=["s_la"])
            S.op("dve", lambda e: e.tensor_copy(out=las[0][0:L, :], in_=la[0:L, :]), reads=["s_la"], writes=["s_las"])
            S.op("dve", lambda e: e.tensor_tensor(out=lres[0:L, :], in0=la[0:L, :], in1=las[0][0:L, :], op=ALU.subtract),
                 reads=["s_la", "s_las"], writes=["s_lres"])
            S.op("dve", lambda e: e.tensor_copy(out=las[1][0:L, :], in_=lres[0:L, :]), reads=["s_lres"], writes=["s_las"])
            S.op("dve", lambda e: e.tensor_tensor(out=lres[0:L, :], in0=lres[0:L, :], in1=las[1][0:L, :], op=ALU.subtract),
                 reads=["s_lres", "s_las"], writes=["s_lres"])
            S.op("dve", lambda e: e.tensor_copy(out=las[2][0:L, :], in_=lres[0:L, :]), reads=["s_lres"], writes=["s_las"])
            S.op("pe", lambda e: e.matmul(p_acs[0:L, 0, 0:32], lhsT=utri[0:L, 0:L], rhs=la[0:L, :], start=True, stop=True),
                 reads=["s_utri", "s_la"], writes=["s_pacs"])
            S.op("act", lambda e: e.copy(out=acs[0:L, :], in_=p_acs[0:L, 0, 0:32]), reads=["s_pacs"], writes=["s_acs"])
            for g in range(4):
                hs = slice(8 * g, 8 * g + 8)
                cs_ = slice(512 * g, 512 * g + 512)
                for k3 in range(3):
                    S.op("dve", lambda e, hs=hs, k3=k3: e.tensor_tensor(out=Rm[k3][0:L, :, 0:L], in0=las[k3][0:L, hs].unsqueeze(2).broadcast_to([L, 8, L]),
                                                                      in1=utri[0:L, 0:L].unsqueeze(1).broadcast_to([L, 8, L]), op=ALU.mult),
                         reads=["s_las", "s_utri"], writes=[("s_Rm", k3)])
                for hh in range(2):
                    for k3 in range(3):
                        S.op("pe", lambda e, hh=hh, k3=k3: e.matmul(PAh[hh][:, :, 0:L], lhsT=ones_b[0:L, :], rhs=Rm[k3][0:L, 4 * hh:4 * hh + 4, 0:L],
                                                                    start=(k3 == 0), stop=(k3 == 2)),
                             reads=[("s_Rm", k3), "s_onesb"], writes=["s_PA"])
                for hh in range(2):
                    S.op("act", lambda e, hh=hh: e.activation(out=dA[:, 4 * hh:4 * hh + 4, 0:L], in_=PAh[hh][:, :, 0:L], func=AF.Exp),
                         reads=["s_PA"], writes=["s_dA"])
                    S.op("act", lambda e, hh=hh: e.copy(out=PAs[:, 4 * hh:4 * hh + 4, 0:L], in_=PAh[hh][:, :, 0:L]),
                         reads=["s_PA"], writes=["s_PAs"])
                    S.op("dve", lambda e, g=g, hh=hh: e.tensor_tensor(out=diff[0:L, 4 * hh:4 * hh + 4, 0:L], in0=PAs[0:L, 4 * hh:4 * hh + 4, 0:L],
                                                                    in1=acs[0:L, 8 * g + 4 * hh:8 * g + 4 * hh + 4].unsqueeze(2).broadcast_to([L, 4, L]),
                                                                    op=ALU.subtract),
                         reads=["s_PAs", "s_acs"], writes=["s_diff"])
                S.op("pool", lambda e: e.tensor_tensor(out=diff[0:L, :, 0:L], in0=diff[0:L, :, 0:L],
                                                       in1=negtri[0:L, 0:L].unsqueeze(1).broadcast_to([L, 8, L]), op=ALU.add),
                     reads=["s_diff", "s_negtri"], writes=["s_diff"])
                S.op("act", lambda e: e.activation(out=diff[0:L, :, 0:L], in_=diff[0:L, :, 0:L], func=AF.Exp), reads=["s_diff"], writes=["s_diff"])
                S.op("pe", lambda e, g=g: e.matmul(CBT[0:L, 0:L], lhsT=BTt[b][:, g, 0:L], rhs=CTt[b][:, g, 0:L], start=True, stop=True),
                     reads=[("s_BT", b), ("s_CT", b)], writes=["s_CBT"])
                S.op("act", lambda e: e.copy(out=CBs[0:L, 0:L], in_=CBT[0:L, 0:L]), reads=["s_CBT"], writes=["s_CBs"])
                S.op("dve", lambda e: e.tensor_tensor(out=CBL[0:L, :, 0:L], in0=diff[0:L, :, 0:L],
                                                      in1=CBs[0:L, 0:L].unsqueeze(1).broadcast_to([L, 8, L]), op=ALU.mult),
                     reads=["s_diff", "s_CBs"], writes=["s_CBL"])
                S.op("pool", lambda e, g=g: e.tensor_tensor(out=Cdec[:, :, 0:L], in0=dA[:, :, 0:L],
                                                            in1=CTt[b][:, g, 0:L].unsqueeze(1).broadcast_to([128, 8, L]), op=ALU.mult),
                     reads=["s_dA", ("s_CT", b)], writes=["s_Cdec"])
                S.op("dve", lambda e, hs=hs, cs_=cs_: e.tensor_tensor(out=xdt[0:L, :, :], in0=x_tm[b][0:L, cs_].rearrange("p (j d) -> p j d", d=64),
                                                                    in1=dt[0:L, hs].unsqueeze(2).broadcast_to([L, 8, 64]), op=ALU.mult),
                     reads=[("s_x", b), "s_dt"], writes=["s_xdt"])
                for j in range(8):
                    S.op("pe", lambda e, j=j: e.matmul(py[0:L, 64 * j:64 * j + 64], lhsT=CBL[0:L, j, 0:L], rhs=xdt[0:L, j, :], start=True, stop=False),
                         reads=["s_CBL", "s_xdt"], writes=["s_py"])
                    S.op("pe", lambda e, j=j, g=g: e.matmul(py[0:L, 64 * j:64 * j + 64], lhsT=Cdec[:, j, 0:L], rhs=hTb[:, 8 * g + j, :], start=False, stop=True),
                         reads=["s_Cdec", ("s_hTb", g)], writes=["s_py"])
                S.op("pool", lambda e, hs=hs, cs_=cs_: e.tensor_tensor(out=t1[0:L, :].rearrange("p (j d) -> p j d", d=64),
                                                                     in0=x_tm[b][0:L, cs_].rearrange("p (j d) -> p j d", d=64),
                                                                     in1=c.dsk_rep[0:L, hs].unsqueeze(2).broadcast_to([L, 8, 64]), op=ALU.mult),
                     reads=[("s_x", b), "lay_ssd"], writes=["s_t1"])
                S.op("dve", lambda e: e.tensor_tensor(out=y1[0:L, :], in0=t1[0:L, :], in1=py[0:L, :], op=ALU.add),
                     reads=["s_t1", "s_py"], writes=["s_y1"])
                S.op("act", lambda e, cs_=cs_: e.activation(out=sz[0:L, :], in_=z_tm[b][0:L, cs_], func=AF.Silu), reads=[("s_z", b)], writes=["s_sz"])
                S.op("dve", lambda e: e.tensor_tensor(out=y1[0:L, :], in0=y1[0:L, :], in1=sz[0:L, :], op=ALU.mult),
                     reads=["s_y1", "s_sz"], writes=["s_y1"])
                S.op("act", lambda e: e.activation(out=jk[0:L, :], in_=y1[0:L, :], func=AF.Square, accum_out=ss[0:L, 0:1]),
                     reads=["s_y1"], writes=["s_jk", "s_ss"])
                S.op("act", lambda e: e.activation(out=ss[0:L, :], in_=ss[0:L, :], func=AF.Sqrt, scale=1.0 / 512, bias=c.eps_t[0:L, 0:1]),
                     reads=["s_ss", "consts"], writes=["s_ss"])
                S.op("dve", lambda e: e.reciprocal(out=ss[0:L, :], in_=ss[0:L, :]), reads=["s_ss"], writes=["s_ss"])
                S.op("dve", lambda e, cs_=cs_: e.scalar_tensor_tensor(out=y3[0:L, :], in0=y1[0:L, :], scalar=ss[0:L, 0:1], in1=c.normg_rep[0:L, cs_],
                                                                    op0=ALU.mult, op1=ALU.mult),
                     reads=["s_y1", "s_ss", "lay_ssd"], writes=["s_y3"])
                yb = (ci[0] * 4 + g) % 2
                for k in range(4):
                    S.op("pe", lambda e, k=k: e.transpose(pT[:, k, 0:L], y3[0:L, 128 * k:128 * k + 128], c.ident_b[0:L, 0:L]),
                         reads=["s_y3", "ident_b"], writes=["s_pT"])
                S.op("act", lambda e, yb=yb: e.copy(out=yT[yb][:, :, 0:L], in_=pT[:, 0:4, 0:L]), reads=["s_pT"], writes=[("s_yT", yb)])
                S.dma("sp", lambda e, g=g, yb=yb: e.dma_start(out=c.ynT[512 * g:512 * g + 512, t0:t0 + L].rearrange("(k p) t -> p k t", p=128),
                                                             in_=yT[yb][:, :, 0:L]),
                      reads=[("s_yT", yb)], writes=tkeys("ynT", t0, t0 + L))
                for hh in range(2):
                    S.op("dve", lambda e, g=g, hh=hh: e.tensor_tensor(out=dte[0:L, 4 * hh:4 * hh + 4], in0=PAs[0:L, 4 * hh:4 * hh + 4, L - 1],
                                                                    in1=acs[0:L, 8 * g + 4 * hh:8 * g + 4 * hh + 4], op=ALU.subtract),
                         reads=["s_PAs", "s_acs"], writes=["s_dte"])
                S.op("act", lambda e: e.activation(out=dte[0:L, :], in_=dte[0:L, :], func=AF.Exp), reads=["s_dte"], writes=["s_dte"])
                S.op("dve", lambda e: e.tensor_tensor(out=xdec[0:L, :, :], in0=xdt[0:L, :, :],
                                                      in1=dte[0:L, :].unsqueeze(2).broadcast_to([L, 8, 64]), op=ALU.mult),
                     reads=["s_xdt", "s_dte"], writes=["s_xdec"])
                S.op("pe", lambda e, g=g: e.matmul(pst[:, :], lhsT=B_tm[b][0:L, 128 * g:128 * g + 128],
                                                   rhs=xdec[0:L, :, :].rearrange("p j d -> p (j d)"), start=True, stop=True),
                     reads=[("s_B", b), "s_xdec"], writes=["s_pst"])
                S.op("dve", lambda e, hs=hs: e.tensor_tensor(out=hT[:, hs, :], in0=hT[:, hs, :],
                                                           in1=dA[:, :, L - 1:L].broadcast_to([128, 8, 64]), op=ALU.mult),
                     reads=[("s_hT", g), "s_dA"], writes=[("s_hT", g)])
                S.op("dve", lambda e, hs=hs: e.tensor_tensor(out=hT[:, hs, :], in0=hT[:, hs, :],
                                                           in1=pst[:, :].rearrange("p (j d) -> p j d", d=64), op=ALU.add),
                     reads=[("s_hT", g), "s_pst"], writes=[("s_hT", g)])
                S.op("pool", lambda e, hs=hs: e.tensor_copy(out=hTb[:, hs, :], in_=hT[:, hs, :]),
                     reads=[("s_hT", g)], writes=[("s_hTb", g)])

        S.op("pool", lambda e: e.memset(hT[:, :, :], 0.0), writes=hkeys("s_hT"))
        S.op("pool", lambda e: e.memset(hTb[:, :, :], 0.0), writes=hkeys("s_hTb"))
        import os
        MODE = int(os.environ.get("SSD_MODE", "9"))
        if "SSD_LIM" in os.environ:
            S.lim = int(os.environ["SSD_LIM"])
            S.nrec = 0
        for t0 in range(0, SEQ, 128):
            if MODE >= 1:
                chunk(t0, 128)
        if MODE >= 2:
            store_state(c.oh_p[layer])
        for s_ in range(NSEQ):
            if MODE >= 3:
                load_state(s_)
            if MODE >= 4:
                chunk(SEQ + 8 * s_, 8)
            if MODE >= 5:
                store_state(c.oh_s[layer, s_])
        S.lim = None


def phase_dsa_prompt(c, layer):
    nc, S = c.nc, c.S
    S.barrier()
    SEQ = c.SEQ
    TOPK = min(256, SEQ // 4)
    NIT = 26
    NKT = SEQ // 128
    with ExitStack() as st:
        kiT = sb(nc, st, "d_kiT", [64, SEQ], BF16)
        kTa = sb(nc, st, "d_kT", [64, 4, SEQ], BF16)
        Va = sb(nc, st, "d_V", [128, NKT, 4, 65], BF16)
        sc = sb(nc, st, "d_sc", [128, SEQ], F32)
        mneg = sb(nc, st, "d_mneg", [128, SEQ], BF16)
        qi8 = sb(nc, st, "d_qi8", [64, 8, 128], BF16)
        wi = sb(nc, st, "d_wi", [128, 8], F32)
        rl = [sb(nc, st, f"d_rl{i}", [128, 512], F32) for i in range(2)]
        sm = sb(nc, st, "d_sm", [128, 8], F32)
        QT4 = [sb(nc, st, f"d_QT4{i}", [64, 4, 128], BF16) for i in range(2)]
        PT = [sb(nc, st, f"d_PT{i}", [128, 512], BF16) for i in range(2)]
        I4 = sb(nc, st, "d_I4", [128, 4, 128], BF16)
        rc = sb(nc, st, "d_rc", [128, 512], F32)
        bcs = sb(nc, st, "d_bcs", [64, 512], F32)
        OT = [sb(nc, st, f"d_OT{i}", [64, 512], BF16) for i in range(2)]
        p_i = [ps(nc, st, f"d_pi{i}", [128, 512], F32) for i in range(2)]
        p_s = [ps(nc, st, f"d_ps{i}", [128, 512], F32) for i in range(2)]
        p_o = [ps(nc, st, f"d_po{i}", [128, 512], F32) for i in range(2)]
        p_b = ps(nc, st, "d_pb", [128, 512], F32)

        S.dma("sp", lambda e: e.dma_start(out=kiT[:, :], in_=c.kiT[0, :, 0:SEQ]), reads=tkeys("kiT", 0, SEQ), writes=["d_kiT"])
        for kvh in range(4):
            S.dma("sp", lambda e, kvh=kvh: e.dma_start(out=kTa[:, kvh, :], in_=c.kT[kvh, :, 0:SEQ]), reads=tkeys("kT", 0, SEQ), writes=["d_kT"])
        S.op("pool", lambda e: e.memset(Va[:, :, :, 64:65], 1.0), writes=["d_V"])
        for kt in range(NKT):
            S.dma("sp", lambda e, kt=kt: e.dma_start(out=Va[:, kt, :, 0:64], in_=c.vB[kt * 128:(kt + 1) * 128, :].rearrange("p (h d) -> p h d", d=64)),
                  reads=tkeys("vB", kt * 128, kt * 128 + 128), writes=["d_V"])
        for g in range(4):
            S.op("dve", lambda e, g=g: e.tensor_copy(out=I4[:, g, :], in_=c.ident_b[:, :]), reads=["ident_b"], writes=["d_I4"])

        it = 0
        fillreg = {}
        for qt in range(NKT):
            t0 = qt * 128
            Sk = (qt + 1) * 128
            S.dma("sp", lambda e, t0=t0: e.dma_start(out=qi8[:, :, :], in_=c.qiT[:, :, t0:t0 + 128].rearrange("h d t -> d h t")),
                  reads=tkeys("qiT", t0, t0 + 128), writes=["d_qi8"])
            S.dma("sp", lambda e, t0=t0: e.dma_start(out=wi[:, :], in_=c.wiS[t0:t0 + 128, :]), reads=tkeys("wiS", t0, t0 + 128), writes=["d_wi"])
            for kb in range((Sk + 511) // 512):
                w = min(512, Sk - kb * 512)
                k0 = kb * 512
                for hh in range(8):
                    pb_ = it % 2
                    it += 1
                    S.op("pe", lambda e, hh=hh, k0=k0, w=w, pb_=pb_: e.matmul(p_i[pb_][:, 0:w], lhsT=qi8[:, hh, :], rhs=kiT[:, k0:k0 + w],
                                                                             start=True, stop=True),
                         reads=["d_qi8", "d_kiT"], writes=[("d_pi", pb_)])
                    S.op("act", lambda e, w=w, pb_=pb_: e.activation(out=rl[pb_][:, 0:w], in_=p_i[pb_][:, 0:w], func=AF.Relu),
                         reads=[("d_pi", pb_)], writes=[("d_rl", pb_)])
                    if hh == 0:
                        S.op("dve", lambda e, k0=k0, w=w, pb_=pb_: e.tensor_scalar(out=sc[:, k0:k0 + w], in0=rl[pb_][:, 0:w], scalar1=wi[:, 0:1],
                                                                                 scalar2=None, op0=ALU.mult),
                             reads=[("d_rl", pb_), "d_wi"], writes=["d_sc"])
                    else:
                        S.op("dve", lambda e, hh=hh, k0=k0, w=w, pb_=pb_: e.scalar_tensor_tensor(
                            out=sc[:, k0:k0 + w], in0=rl[pb_][:, 0:w], scalar=wi[:, hh:hh + 1], in1=sc[:, k0:k0 + w], op0=ALU.mult, op1=ALU.add),
                            reads=[("d_rl", pb_), "d_wi", "d_sc"], writes=["d_sc"])
            S.op("dve", lambda e, Sk=Sk: e.tensor_reduce(out=sm[:, 5:6], in_=sc[:, 0:Sk], axis=AX.X, op=ALU.max), reads=["d_sc"], writes=["d_sm"])
            S.op("dve", lambda e, Sk=Sk: e.tensor_reduce(out=sm[:, 0:1], in_=sc[:, 0:Sk], axis=AX.X, op=ALU.min), reads=["d_sc"], writes=["d_sm"])
            S.op("dve", lambda e: e.tensor_tensor(out=sm[:, 1:2], in0=sm[:, 5:6], in1=sm[:, 0:1], op=ALU.subtract), reads=["d_sm"], writes=["d_sm"])
            def _asel(e, Sk=Sk):
                if "r" not in fillreg:
                    fillreg["r"] = e.to_reg(-1e30)
                return e.affine_select(out=sc[:, Sk - 128:Sk], in_=sc[:, Sk - 128:Sk], pattern=[[-1, 128]], base=0,
                                       channel_multiplier=1, compare_op=ALU.is_ge, fill=fillreg["r"])
            S.op("pool", _asel,
                 reads=["d_sc", "d_sm"], writes=["d_sc"])
            for _ in range(NIT):
                S.op("dve", lambda e: e.tensor_scalar(out=sm[:, 1:2], in0=sm[:, 1:2], scalar1=0.5, scalar2=None, op0=ALU.mult),
                     reads=["d_sm"], writes=["d_sm"])
                S.op("dve", lambda e: e.tensor_tensor(out=sm[:, 2:3], in0=sm[:, 0:1], in1=sm[:, 1:2], op=ALU.add), reads=["d_sm"], writes=["d_sm"])
                S.op("dve", lambda e, Sk=Sk: e.tensor_scalar(out=mneg[:, 0:Sk], in0=sc[:, 0:Sk], scalar1=sm[:, 2:3], scalar2=None,
                                                           op0=ALU.is_ge, op1=ALU.add, accum_out=sm[:, 3:4]),
                     reads=["d_sc", "d_sm"], writes=["d_mneg", "d_sm"])
                S.op("dve", lambda e: e.tensor_scalar(out=sm[:, 4:5], in0=sm[:, 3:4], scalar1=float(TOPK), scalar2=None, op0=ALU.is_ge),
                     reads=["d_sm"], writes=["d_sm"])
                S.op("dve", lambda e: e.scalar_tensor_tensor(out=sm[:, 0:1], in0=sm[:, 4:5], scalar=sm[:, 1:2], in1=sm[:, 0:1],
                                                             op0=ALU.mult, op1=ALU.add), reads=["d_sm"], writes=["d_sm"])
            S.op("dve", lambda e: e.tensor_scalar(out=sm[:, 6:7], in0=sm[:, 0:1], scalar1=-1e29, scalar2=None, op0=ALU.max),
                 reads=["d_sm"], writes=["d_sm"])
            S.op("dve", lambda e, Sk=Sk: e.tensor_scalar(out=mneg[:, 0:Sk], in0=sc[:, 0:Sk], scalar1=sm[:, 6:7], scalar2=-30000.0,
                                                       op0=ALU.is_lt, op1=ALU.mult), reads=["d_sc", "d_sm"], writes=["d_mneg"])
            for kvh in range(4):
                qb = (qt * 4 + kvh) % 2
                S.dma("sp", lambda e, kvh=kvh, t0=t0, qb=qb: e.dma_start(out=QT4[qb][:, :, :],
                                                                        in_=c.qT[4 * kvh:4 * kvh + 4, :, t0:t0 + 128].rearrange("h d t -> d h t")),
                      reads=tkeys("qT", t0, t0 + 128), writes=[("d_QT4", qb)])
                for kt in range(qt + 1):
                    pb_ = it % 2
                    it += 1
                    S.op("pe", lambda e, kvh=kvh, kt=kt, qb=qb, pb_=pb_: e.matmul(p_s[pb_][:, :], lhsT=kTa[:, kvh, kt * 128:(kt + 1) * 128],
                                                                                 rhs=QT4[qb][:, :, :].rearrange("d h t -> d (h t)"),
                                                                                 start=True, stop=False),
                         reads=["d_kT", ("d_QT4", qb)], writes=[("d_ps", pb_)])
                    S.op("pe", lambda e, kt=kt, pb_=pb_: e.matmul(p_s[pb_][:, :], lhsT=mneg[:, kt * 128:(kt + 1) * 128],
                                                                 rhs=I4[:, :, :].rearrange("p h t -> p (h t)"), start=False, stop=True),
                         reads=["d_mneg", "d_I4"], writes=[("d_ps", pb_)])
                    S.op("act", lambda e, pb_=pb_: e.activation(out=PT[pb_][:, :], in_=p_s[pb_][:, :], func=AF.Exp),
                         reads=[("d_ps", pb_)], writes=[("d_PT", pb_)])
                    S.op("pe", lambda e, kvh=kvh, kt=kt, qb=qb, pb_=pb_, qt=qt: e.matmul(p_o[qb][0:65, :], lhsT=Va[:, kt, kvh, :], rhs=PT[pb_][:, :],
                                                                                        start=(kt == 0), stop=(kt == qt)),
                         reads=["d_V", ("d_PT", pb_)], writes=[("d_po", qb)])
                S.op("dve", lambda e, qb=qb: e.reciprocal(out=rc[64:65, :], in_=p_o[qb][64:65, :]), reads=[("d_po", qb)], writes=["d_rc"])
                S.op("pe", lambda e: e.matmul(p_b[0:64, :], lhsT=c.ones_f[64:65, 0:64], rhs=rc[64:65, :], start=True, stop=True),
                     reads=["d_rc", "ones_f"], writes=["d_pb"])
                S.op("act", lambda e: e.copy(out=bcs[:, :], in_=p_b[0:64, :]), reads=["d_pb"], writes=["d_bcs"])
                S.op("dve", lambda e, qb=qb: e.tensor_tensor(out=OT[qb][:, :], in0=p_o[qb][0:64, :], in1=bcs[:, :], op=ALU.mult),
                     reads=[("d_po", qb), "d_bcs"], writes=[("d_OT", qb)])
                S.dma("sp", lambda e, kvh=kvh, t0=t0, qb=qb: e.dma_start(
                    out=c.yaT[256 * kvh:256 * kvh + 256, t0:t0 + 128].rearrange("(g d) t -> d g t", d=64),
                    in_=OT[qb][:, :].rearrange("d (g t) -> d g t", t=128)),
                    reads=[("d_OT", qb)], writes=tkeys("yaT", t0, t0 + 128))


def phase_dsa_sample(c, layer):
    nc, S = c.nc, c.S
    S.barrier()
    SEQ, NSEQ, NPG = c.SEQ, c.NSEQ, c.NPAGES
    PAST = NPG * 128
    LK = PAST + 8
    TOPK = min(256, LK // 4)
    NIT = 26
    fillreg = {}
    with ExitStack() as st:
        pt_i = sb(nc, st, "e_pt", [128, 1], I32)
        idxq = [sb(nc, st, f"e_idxq{i}", [128, 1], I32) for i in range(4)]
        sc = sb(nc, st, "e_sc", [8, LK], F32)
        mneg = sb(nc, st, "e_mneg", [8, LK], BF16)
        qi8 = sb(nc, st, "e_qi8", [64, 8, 8], BF16)
        kin = sb(nc, st, "e_kin", [64, 8], BF16)
        wi = sb(nc, st, "e_wi", [8, 8], F32)
        rl = [sb(nc, st, f"e_rl{i}", [8, 512], F32) for i in range(2)]
        sm = sb(nc, st, "e_sm", [8, 8], F32)
        I4s = sb(nc, st, "e_I4s", [8, 4, 8], BF16)
        QT4 = sb(nc, st, "e_QT4", [64, 16, 8], BF16)
        kTn = sb(nc, st, "e_kTn", [64, 4, 8], BF16)
        Vn = sb(nc, st, "e_Vn", [8, 4, 65], BF16)
        kTj = [sb(nc, st, f"e_kTj{i}", [64, 4, 128], BF16) for i in range(2)]
        PT = [sb(nc, st, f"e_PT{i}", [128, 32], BF16) for i in range(2)]
        rc = sb(nc, st, "e_rc", [128, 32], F32)
        bcs = sb(nc, st, "e_bcs", [64, 32], F32)
        OT = sb(nc, st, "e_OT", [64, 4, 32], BF16)
        p_t = [ps(nc, st, f"e_pt{i}", [128, 512], F32) for i in range(2)]
        p_s = [ps(nc, st, f"e_ps{i}", [128, 512], F32) for i in range(2)]
        p_o = [ps(nc, st, f"e_po{i}", [128, 512], F32) for i in range(4)]
        for g in range(4):
            S.op("dve", lambda e, g=g: e.tensor_copy(out=I4s[:, g, :], in_=c.ident_b[0:8, 0:8]), reads=["ident_b"], writes=["e_I4s"])
        it = 0
        for si in range(NSEQ):
            t0 = SEQ + 8 * si
            S.dma("sp", lambda e, si=si: e.dma_start(out=pt_i[0:NPG, :], in_=c.ptab[si].rearrange("(p o) -> p o", o=1)), writes=["e_pt"])
            for q in range(4):
                S.op("dve", lambda e, q=q: e.tensor_scalar(out=idxq[q][0:NPG, :], in0=pt_i[0:NPG, :], scalar1=4.0, scalar2=float(q),
                                                         op0=ALU.mult, op1=ALU.add), reads=["e_pt"], writes=[("e_idxq", q)])
            S.dma("sp", lambda e, t0=t0: e.dma_start(out=qi8[:, :, :], in_=c.qiT[:, :, t0:t0 + 8].rearrange("h d t -> d h t")),
                  reads=tkeys("qiT", t0, t0 + 8), writes=["e_qi8"])
            S.dma("sp", lambda e, t0=t0: e.dma_start(out=kin[:, :], in_=c.kiT[0, :, t0:t0 + 8]), reads=tkeys("kiT", t0, t0 + 8), writes=["e_kin"])
            S.dma("sp", lambda e, t0=t0: e.dma_start(out=wi[:, :], in_=c.wiS[t0:t0 + 8, :]), reads=tkeys("wiS", t0, t0 + 8), writes=["e_wi"])
            S.dma("sp", lambda e, t0=t0: e.dma_start(out=QT4[:, :, :], in_=c.qT[:, :, t0:t0 + 8].rearrange("h d t -> d h t")),
                  reads=tkeys("qT", t0, t0 + 8), writes=["e_QT4"])
            S.dma("sp", lambda e, t0=t0: e.dma_start(out=kTn[:, :, :], in_=c.kT[:, :, t0:t0 + 8].rearrange("h d t -> d h t")),
                  reads=tkeys("kT", t0, t0 + 8), writes=["e_kTn"])
            S.op("pool", lambda e: e.memset(Vn[:, :, 64:65], 1.0), writes=["e_Vn"])
            S.dma("sp", lambda e, t0=t0: e.dma_start(out=Vn[:, :, 0:64], in_=c.vB[t0:t0 + 8, :].rearrange("p (h d) -> p h d", d=64)),
                  reads=tkeys("vB", t0, t0 + 8), writes=["e_Vn"])

            def accum(pb_, hh, cols):
                if hh == 0:
                    S.op("dve", lambda e: e.tensor_scalar(out=sc[:, cols], in0=rl[pb_][:, 0:cols.stop - cols.start], scalar1=wi[:, 0:1],
                                                          scalar2=None, op0=ALU.mult), reads=[("e_rl", pb_), "e_wi"], writes=["e_sc"])
                else:
                    S.op("dve", lambda e: e.scalar_tensor_tensor(out=sc[:, cols], in0=rl[pb_][:, 0:cols.stop - cols.start], scalar=wi[:, hh:hh + 1],
                                                                 in1=sc[:, cols], op0=ALU.mult, op1=ALU.add),
                         reads=[("e_rl", pb_), "e_wi", "e_sc"], writes=["e_sc"])

            with ExitStack() as st2:
                KI = sb(nc, st2, "e_KI", [128, 8192], F32)
                kiTs = sb(nc, st2, "e_kiTs", [64, 128, 128], BF16)
                S.dma("pool", lambda e: e.indirect_dma_start(out=KI[0:NPG, :], out_offset=None, in_=c.cache_i[layer],
                                                             in_offset=bass.IndirectOffsetOnAxis(ap=pt_i[0:NPG, 0:1], axis=0)),
                      reads=["e_pt"], writes=["e_KI"])
                for jb in range(32):
                    pb_ = it % 2
                    it += 1
                    for jj in range(4):
                        j = 4 * jb + jj
                        S.op("pe", lambda e, j=j, jj=jj, pb_=pb_: e.matmul(p_t[pb_][0:64, jj * 128:jj * 128 + NPG], lhsT=KI[0:NPG, j * 64:(j + 1) * 64],
                                                                          rhs=c.ident_f[0:NPG, 0:NPG], start=True, stop=True),
                             reads=["e_KI", "ident_f"], writes=[("e_pt", pb_)])
                    S.op("act", lambda e, jb=jb, pb_=pb_: e.copy(out=kiTs[:, 4 * jb:4 * jb + 4, 0:NPG],
                                                                in_=p_t[pb_][0:64, :].rearrange("p (a b) -> p a b", b=128)[:, :, 0:NPG]),
                         reads=[("e_pt", pb_)], writes=["e_kiTs"])
                for jb in range(32):
                    for hh in range(8):
                        pb_ = it % 2
                        it += 1
                        for jj in range(4):
                            S.op("pe", lambda e, hh=hh, jb=jb, jj=jj, pb_=pb_: e.matmul(p_s[pb_][0:8, jj * NPG:(jj + 1) * NPG], lhsT=qi8[:, hh, :],
                                                                                       rhs=kiTs[:, 4 * jb + jj, 0:NPG], start=True, stop=True),
                                 reads=["e_qi8", "e_kiTs"], writes=[("e_ps", pb_)])
                        S.op("act", lambda e, pb_=pb_: e.activation(out=rl[pb_][:, 0:4 * NPG], in_=p_s[pb_][0:8, 0:4 * NPG], func=AF.Relu),
                             reads=[("e_ps", pb_)], writes=[("e_rl", pb_)])
                        accum(pb_, hh, slice(4 * jb * NPG, 4 * (jb + 1) * NPG))
            for hh in range(8):
                pb_ = it % 2
                it += 1
                S.op("pe", lambda e, hh=hh, pb_=pb_: e.matmul(p_s[pb_][0:8, 0:8], lhsT=qi8[:, hh, :], rhs=kin[:, :], start=True, stop=True),
                     reads=["e_qi8", "e_kin"], writes=[("e_ps", pb_)])
                S.op("act", lambda e, pb_=pb_: e.activation(out=rl[pb_][:, 0:8], in_=p_s[pb_][0:8, 0:8], func=AF.Relu),
                     reads=[("e_ps", pb_)], writes=[("e_rl", pb_)])
                accum(pb_, hh, slice(PAST, PAST + 8))
            S.op("dve", lambda e: e.tensor_reduce(out=sm[:, 5:6], in_=sc[:, :], axis=AX.X, op=ALU.max), reads=["e_sc"], writes=["e_sm"])
            S.op("dve", lambda e: e.tensor_reduce(out=sm[:, 0:1], in_=sc[:, :], axis=AX.X, op=ALU.min), reads=["e_sc"], writes=["e_sm"])
            S.op("dve", lambda e: e.tensor_tensor(out=sm[:, 1:2], in0=sm[:, 5:6], in1=sm[:, 0:1], op=ALU.subtract), reads=["e_sm"], writes=["e_sm"])

            def _asel(e):
                if "r" not in fillreg:
                    fillreg["r"] = e.to_reg(-1e30)
                return e.affine_select(out=sc[:, PAST:PAST + 8], in_=sc[:, PAST:PAST + 8], pattern=[[-1, 8]], base=0,
                                       channel_multiplier=1, compare_op=ALU.is_ge, fill=fillreg["r"])
            S.op("pool", _asel, reads=["e_sc", "e_sm"], writes=["e_sc"])
            for _ in range(NIT):
                S.op("dve", lambda e: e.tensor_scalar(out=sm[:, 1:2], in0=sm[:, 1:2], scalar1=0.5, scalar2=None, op0=ALU.mult),
                     reads=["e_sm"], writes=["e_sm"])
                S.op("dve", lambda e: e.tensor_tensor(out=sm[:, 2:3], in0=sm[:, 0:1], in1=sm[:, 1:2], op=ALU.add), reads=["e_sm"], writes=["e_sm"])
                S.op("dve", lambda e: e.tensor_scalar(out=mneg[:, :], in0=sc[:, :], scalar1=sm[:, 2:3], scalar2=None,
                                                      op0=ALU.is_ge, op1=ALU.add, accum_out=sm[:, 3:4]),
                     reads=["e_sc", "e_sm"], writes=["e_mneg", "e_sm"])
                S.op("dve", lambda e: e.tensor_scalar(out=sm[:, 4:5], in0=sm[:, 3:4], scalar1=float(TOPK), scalar2=None, op0=ALU.is_ge),
                     reads=["e_sm"], writes=["e_sm"])
                S.op("dve", lambda e: e.scalar_tensor_tensor(out=sm[:, 0:1], in0=sm[:, 4:5], scalar=sm[:, 1:2], in1=sm[:, 0:1],
                                                             op0=ALU.mult, op1=ALU.add), reads=["e_sm"], writes=["e_sm"])
            S.op("dve", lambda e: e.tensor_scalar(out=sm[:, 6:7], in0=sm[:, 0:1], scalar1=-1e29, scalar2=None, op0=ALU.max),
                 reads=["e_sm"], writes=["e_sm"])
            S.op("dve", lambda e: e.tensor_scalar(out=mneg[:, :], in0=sc[:, :], scalar1=sm[:, 6:7], scalar2=-30000.0,
                                                  op0=ALU.is_lt, op1=ALU.mult), reads=["e_sc", "e_sm"], writes=["e_mneg"])
            with ExitStack() as st3:
                Kq = sb(nc, st3, "e_Kq", [128, 32, 256], F32)
                Vq = sb(nc, st3, "e_Vq", [128, 32, 256], F32)
                Vb = sb(nc, st3, "e_Vb", [128, 32, 4, 65], BF16)
                S.op("pool", lambda e: e.memset(Vb[:, :, :, 64:65], 1.0), writes=["e_Vb"])
                first = [True] * 4
                for q in range(4):
                    S.dma("pool", lambda e, q=q: e.indirect_dma_start(out=Kq[0:NPG, :, :].rearrange("p a b -> p (a b)"), out_offset=None,
                                                                     in_=c.cache_k[layer],
                                                                     in_offset=bass.IndirectOffsetOnAxis(ap=idxq[q][0:NPG, 0:1], axis=0)),
                          reads=[("e_idxq", q)], writes=["e_Kq"])
                    S.dma("pool", lambda e, q=q: e.indirect_dma_start(out=Vq[0:NPG, :, :].rearrange("p a b -> p (a b)"), out_offset=None,
                                                                     in_=c.cache_v[layer],
                                                                     in_offset=bass.IndirectOffsetOnAxis(ap=idxq[q][0:NPG, 0:1], axis=0)),
                          reads=[("e_idxq", q)], writes=["e_Vq"])
                    S.op("dve", lambda e: e.tensor_copy(out=Vb[0:NPG, :, :, 0:64], in_=Vq[0:NPG, :, :].rearrange("p a (h d) -> p a h d", d=64)),
                         reads=["e_Vq"], writes=["e_Vb"])
                    for kvh in range(4):
                        for jb in range(8):
                            tb = it % 2
                            it += 1
                            for jj in range(4):
                                jl = 4 * jb + jj
                                S.op("pe", lambda e, jl=jl, jj=jj, kvh=kvh, tb=tb: e.matmul(
                                    p_t[tb][0:64, jj * 128:jj * 128 + NPG], lhsT=Kq[0:NPG, jl, kvh * 64:(kvh + 1) * 64],
                                    rhs=c.ident_f[0:NPG, 0:NPG], start=True, stop=True),
                                    reads=["e_Kq", "ident_f"], writes=[("e_pt", tb)])
                            S.op("act", lambda e, tb=tb: e.copy(out=kTj[tb][:, :, 0:NPG],
                                                                in_=p_t[tb][0:64, :].rearrange("p (a b) -> p a b", b=128)[:, :, 0:NPG]),
                                 reads=[("e_pt", tb)], writes=[("e_kTj", tb)])
                            for jj in range(4):
                                jl = 4 * jb + jj
                                jg = 32 * q + jl
                                pb_ = it % 2
                                it += 1
                                S.op("pe", lambda e, jj=jj, kvh=kvh, tb=tb, pb_=pb_: e.matmul(
                                    p_s[pb_][0:NPG, 0:32], lhsT=kTj[tb][:, jj, 0:NPG], rhs=QT4[:, 4 * kvh:4 * kvh + 4, :].rearrange("d h t -> d (h t)"),
                                    start=True, stop=False), reads=[("e_kTj", tb), "e_QT4"], writes=[("e_ps", pb_)])
                                S.op("pe", lambda e, jg=jg, pb_=pb_: e.matmul(
                                    p_s[pb_][0:NPG, 0:32], lhsT=mneg[0:8, jg * NPG:(jg + 1) * NPG], rhs=I4s[:, :, :].rearrange("p h t -> p (h t)"),
                                    start=False, stop=True), reads=["e_mneg", "e_I4s"], writes=[("e_ps", pb_)])
                                S.op("act", lambda e, pb_=pb_: e.activation(out=PT[pb_][0:NPG, :], in_=p_s[pb_][0:NPG, 0:32], func=AF.Exp),
                                     reads=[("e_ps", pb_)], writes=[("e_PT", pb_)])
                                S.op("pe", lambda e, jl=jl, kvh=kvh, pb_=pb_, fst=first[kvh]: e.matmul(
                                    p_o[kvh][0:65, 0:32], lhsT=Vb[0:NPG, jl, kvh, :], rhs=PT[pb_][0:NPG, :], start=fst, stop=False),
                                    reads=["e_Vb", ("e_PT", pb_)], writes=[("e_po", kvh)])
                                first[kvh] = False
                for kvh in range(4):
                    pb_ = it % 2
                    it += 1
                    S.op("pe", lambda e, kvh=kvh, pb_=pb_: e.matmul(p_s[pb_][0:8, 0:32], lhsT=kTn[:, kvh, :],
                                                                   rhs=QT4[:, 4 * kvh:4 * kvh + 4, :].rearrange("d h t -> d (h t)"), start=True, stop=False),
                         reads=["e_kTn", "e_QT4"], writes=[("e_ps", pb_)])
                    S.op("pe", lambda e, pb_=pb_: e.matmul(p_s[pb_][0:8, 0:32], lhsT=mneg[0:8, PAST:PAST + 8],
                                                          rhs=I4s[:, :, :].rearrange("p h t -> p (h t)"), start=False, stop=True),
                         reads=["e_mneg", "e_I4s"], writes=[("e_ps", pb_)])
                    S.op("act", lambda e, pb_=pb_: e.activation(out=PT[pb_][0:8, :], in_=p_s[pb_][0:8, 0:32], func=AF.Exp),
                         reads=[("e_ps", pb_)], writes=[("e_PT", pb_)])
                    S.op("pe", lambda e, kvh=kvh, pb_=pb_: e.matmul(p_o[kvh][0:65, 0:32], lhsT=Vn[0:8, kvh, :], rhs=PT[pb_][0:8, :], start=False, stop=True),
                         reads=["e_Vn", ("e_PT", pb_)], writes=[("e_po", kvh)])
                    S.op("dve", lambda e, kvh=kvh: e.reciprocal(out=rc[64:65, :], in_=p_o[kvh][64:65, 0:32]), reads=[("e_po", kvh)], writes=["e_rc"])
                    tb = it % 2
                    it += 1
                    S.op("pe", lambda e, tb=tb: e.matmul(p_t[tb][0:64, 0:32], lhsT=c.ones_f[64:65, 0:64], rhs=rc[64:65, :], start=True, stop=True),
                         reads=["e_rc", "ones_f"], writes=[("e_pt", tb)])
                    S.op("act", lambda e, tb=tb: e.copy(out=bcs[:, :], in_=p_t[tb][0:64, 0:32]), reads=[("e_pt", tb)], writes=["e_bcs"])
                    S.op("dve", lambda e, kvh=kvh: e.tensor_tensor(out=OT[:, kvh, :], in0=p_o[kvh][0:64, 0:32], in1=bcs[:, :], op=ALU.mult),
                         reads=[("e_po", kvh), "e_bcs"], writes=["e_OT"])
                    S.dma("sp", lambda e, kvh=kvh, t0=t0: e.dma_start(
                        out=c.yaT[256 * kvh:256 * kvh + 256, t0:t0 + 8].rearrange("(g d) t -> d g t", d=64),
                        in_=OT[:, kvh, :].rearrange("d (g t) -> d g t", t=8)),
                        reads=["e_OT"], writes=tkeys("yaT", t0, t0 + 8))

def phase_merge(c, layer, tiles):
    nc, S = c.nc, c.S
    S.barrier()
    TS = c.TS
    wbs = c.w_bs[layer].rearrange("(ko ki) n -> ki ko n", ki=128)
    wba = c.w_ba[layer].rearrange("(ko ki) n -> ki ko n", ki=128)
    wo = c.w_o[layer].rearrange("(ko ki) n -> ki ko n", ki=128)
    with ExitStack() as st:
        yn = sb(nc, st, "m_yn", [128, 16, TS], BF16)
        ya = sb(nc, st, "m_ya", [128, 8, TS], BF16)
        mT = sb(nc, st, "m_mT", [128, 8, TS], BF16)
        wS = [sb(nc, st, f"m_wS{i}", [128, 24, 128], F32) for i in range(2)]
        wB = [sb(nc, st, f"m_wB{i}", [128, 24, 128], BF16) for i in range(2)]
        gsg = [sb(nc, st, f"m_g{i}", [128, 2, TS], F32) for i in range(2)]
        xs = [sb(nc, st, f"m_xs{i}", [128, TS], F32) for i in range(2)]
        m1 = [sb(nc, st, f"m_m1{i}", [128, 512], F32) for i in range(2)]
        m2 = [sb(nc, st, f"m_m2{i}", [128, 512], F32) for i in range(2)]
        xo = [sb(nc, st, f"m_xo{i}", [128, 512], F32) for i in range(2)]
        p1 = [ps(nc, st, f"m_p1{i}", [128, 512], F32) for i in range(2)]
        p2 = [ps(nc, st, f"m_p2{i}", [128, 512], F32) for i in range(2)]
        p3 = [ps(nc, st, f"m_p3{i}", [128, 512], F32) for i in range(2)]
        wi_ = 0
        it = 0
        for (t0, T) in tiles:
            nh = (T + 511) // 512
            S.dma("sp", lambda e, t0=t0, T=T: e.dma_start(out=yn[:, :, 0:T], in_=c.ynT[:, t0:t0 + T].rearrange("(k p) t -> p k t", p=128)),
                  reads=tkeys("ynT", t0, t0 + T), writes=["m_yn"])
            S.dma("sp", lambda e, t0=t0, T=T: e.dma_start(out=ya[:, :, 0:T], in_=c.yaT[:, t0:t0 + T].rearrange("(k p) t -> p k t", p=128)),
                  reads=tkeys("yaT", t0, t0 + T), writes=["m_ya"])
            for cc in range(8):
                b = wi_ % 2
                wi_ += 1
                cs_ = slice(cc * 128, cc * 128 + 128)
                S.dma("sp", lambda e, b=b, cs_=cs_: e.dma_start(out=wS[b][:, 0:16, :], in_=wbs[:, :, cs_]), writes=[("m_wS", b)])
                S.dma("sp", lambda e, b=b, cs_=cs_: e.dma_start(out=wS[b][:, 16:24, :], in_=wba[:, :, cs_]), writes=[("m_wS", b)])
                S.op("pool", lambda e, b=b: e.tensor_copy(out=wB[b][:, :, :], in_=wS[b][:, :, :]), reads=[("m_wS", b)], writes=[("m_wB", b)])
                S.dma("sp", lambda e, b=b, cs_=cs_, t0=t0, T=T: e.dma_start(out=gsg[b][:, 0, 0:T], in_=c.gsT[cs_, t0:t0 + T]),
                      reads=tkeys("gsT", t0, t0 + T), writes=[("m_g", b)])
                S.dma("sp", lambda e, b=b, cs_=cs_, t0=t0, T=T: e.dma_start(out=gsg[b][:, 1, 0:T], in_=c.gaT[cs_, t0:t0 + T]),
                      reads=tkeys("gaT", t0, t0 + T), writes=[("m_g", b)])
                for h in range(nh):
                    w = min(512, T - h * 512)
                    hs = slice(h * 512, h * 512 + w)
                    pb_ = it % 2
                    it += 1
                    for ko in range(16):
                        S.op("pe", lambda e, b=b, ko=ko, hs=hs, w=w, pb_=pb_: e.matmul(p1[pb_][:, 0:w], lhsT=wB[b][:, ko, :], rhs=yn[:, ko, hs],
                                                                                      start=(ko == 0), stop=(ko == 15)),
                             reads=[("m_wB", b), "m_yn"], writes=[("m_p1", pb_)])
                    for ko in range(8):
                        S.op("pe", lambda e, b=b, ko=ko, hs=hs, w=w, pb_=pb_: e.matmul(p2[pb_][:, 0:w], lhsT=wB[b][:, 16 + ko, :], rhs=ya[:, ko, hs],
                                                                                      start=(ko == 0), stop=(ko == 7)),
                             reads=[("m_wB", b), "m_ya"], writes=[("m_p2", pb_)])
                    S.op("dve", lambda e, b=b, hs=hs, w=w, pb_=pb_: e.tensor_tensor(out=m1[pb_][:, 0:w], in0=p1[pb_][:, 0:w], in1=gsg[b][:, 0, hs], op=ALU.mult),
                         reads=[("m_p1", pb_), ("m_g", b)], writes=[("m_m1", pb_)])
                    S.op("dve", lambda e, b=b, hs=hs, w=w, pb_=pb_: e.tensor_tensor(out=m2[pb_][:, 0:w], in0=p2[pb_][:, 0:w], in1=gsg[b][:, 1, hs], op=ALU.mult),
                         reads=[("m_p2", pb_), ("m_g", b)], writes=[("m_m2", pb_)])
                    S.op("pool", lambda e, cc=cc, hs=hs, w=w, pb_=pb_: e.tensor_tensor(out=mT[:, cc, hs], in0=m1[pb_][:, 0:w], in1=m2[pb_][:, 0:w], op=ALU.add),
                         reads=[("m_m1", pb_), ("m_m2", pb_)], writes=[("m_mT", cc)])
            for c2 in range(8):
                b = wi_ % 2
                wi_ += 1
                cs_ = slice(c2 * 128, c2 * 128 + 128)
                S.dma("sp", lambda e, b=b, cs_=cs_: e.dma_start(out=wS[b][:, 0:8, :], in_=wo[:, :, cs_]), writes=[("m_wS", b)])
                S.op("pool", lambda e, b=b: e.tensor_copy(out=wB[b][:, 0:8, :], in_=wS[b][:, 0:8, :]), reads=[("m_wS", b)], writes=[("m_wB", b)])
                S.dma("sp", lambda e, b=b, cs_=cs_, t0=t0, T=T: e.dma_start(out=xs[b][:, 0:T], in_=c.xT[cs_, t0:t0 + T]),
                      reads=tkeys("xT", t0, t0 + T), writes=[("m_xs", b)])
                for h in range(nh):
                    w = min(512, T - h * 512)
                    hs = slice(h * 512, h * 512 + w)
                    pb_ = it % 2
                    it += 1
                    for cc in range(8):
                        S.op("pe", lambda e, b=b, cc=cc, hs=hs, w=w, pb_=pb_: e.matmul(p3[pb_][:, 0:w], lhsT=wB[b][:, cc, :], rhs=mT[:, cc, hs],
                                                                                      start=(cc == 0), stop=(cc == 7)),
                             reads=[("m_wB", b), ("m_mT", cc)], writes=[("m_p3", pb_)])
                    S.op("dve", lambda e, b=b, hs=hs, w=w, pb_=pb_: e.tensor_tensor(out=xo[pb_][:, 0:w], in0=p3[pb_][:, 0:w], in1=xs[b][:, hs], op=ALU.add),
                         reads=[("m_p3", pb_), ("m_xs", b)], writes=[("m_xo", pb_)])
                    a0 = t0 + h * 512
                    S.dma("sp", lambda e, cs_=cs_, a0=a0, w=w, pb_=pb_: e.dma_start(out=c.xT[cs_, a0:a0 + w], in_=xo[pb_][:, 0:w]),
                          reads=[("m_xo", pb_)], writes=tkeys("xT", a0, a0 + w))

def phase_final(c, tiles_out):
    nc, S = c.nc, c.S
    S.barrier()
    gain = c.gains[:, 6 * 8:6 * 8 + 8]
    with ExitStack() as st:
        xs = sb(nc, st, "y_xs", [128, 8, 128], F32)
        hn = sb(nc, st, "y_hn", [128, 8, 128], F32)
        yo = [sb(nc, st, f"y_o{i}", [128, D], F32) for i in range(2)]
        pt = [ps(nc, st, f"y_p{i}", [128, D], F32) for i in range(2)]
        with ExitStack() as st2:
            sq = [sb(nc, st2, f"y_sq{i}", [128, 128], F32) for i in range(2)]
            rstd = sb(nc, st2, "y_rstd", [128, 128], F32)
            pss = ps(nc, st2, "y_pss", [128, 128], F32)
            for i, (t0, n, dst) in enumerate(tiles_out):
                b = i % 2
                S.dma("sp", lambda e, t0=t0, n=n: e.dma_start(out=xs[:, :, 0:n],
                                                             in_=c.xT[:, t0:t0 + n].rearrange("(dc p) t -> p dc t", p=128)),
                      reads=[("xT", t0 // 128)], writes=["y_xs"])
                for dc in range(8):
                    sb_ = dc % 2
                    S.op("act", lambda e, sb_=sb_, dc=dc, n=n: e.activation(out=sq[sb_][:, 0:n], in_=xs[:, dc, 0:n], func=AF.Square),
                         reads=["y_xs"], writes=[("y_sq", sb_)])
                    S.op("pe", lambda e, sb_=sb_, dc=dc, n=n: e.matmul(pss[:, 0:n], lhsT=c.ones_f[:, :], rhs=sq[sb_][:, 0:n],
                                                                       start=(dc == 0), stop=(dc == 7)),
                         reads=[("y_sq", sb_), "ones_f"], writes=["y_pss"])
                S.op("act", lambda e, n=n: e.activation(out=rstd[:, 0:n], in_=pss[:, 0:n], func=AF.Sqrt, scale=1.0 / D,
                                                        bias=c.eps_t[:, 0:1]),
                     reads=["y_pss", "consts"], writes=["y_rstd"])
                S.op("dve", lambda e, n=n: e.reciprocal(out=rstd[:, 0:n], in_=rstd[:, 0:n]), reads=["y_rstd"], writes=["y_rstd"])
                for dc in range(8):
                    S.op("dve", lambda e, dc=dc, n=n: e.scalar_tensor_tensor(out=hn[:, dc, 0:n], in0=xs[:, dc, 0:n],
                                                                           scalar=gain[:, dc:dc + 1], in1=rstd[:, 0:n],
                                                                           op0=ALU.mult, op1=ALU.mult),
                         reads=["y_xs", "y_rstd", "gains"], writes=["y_hn"])
                for dc in range(8):
                    S.op("pe", lambda e, b=b, dc=dc, n=n: e.transpose(pt[b][0:n, dc * 128:(dc + 1) * 128], hn[:, dc, 0:n],
                                                                     c.ident_f[:, :]),
                         reads=["y_hn", "ident_f"], writes=[("y_p", b)])
                S.op("act", lambda e, b=b, n=n: e.copy(out=yo[b][0:n, :], in_=pt[b][0:n, :]),
                     reads=[("y_p", b)], writes=[("y_o", b)])
                S.dma("sp", lambda e, b=b, n=n, dst=dst: e.dma_start(out=dst, in_=yo[b][0:n, :]),
                      reads=[("y_o", b)], writes=[("yout", i)], is_output=True)


def setup_consts(c, st):
    nc, S = c.nc, c.S
    c.ident_f = sb(nc, st, "ident_f", [128, 128], F32)
    c.ones_f = sb(nc, st, "ones_f", [128, 128], F32)
    c.eps_t = sb(nc, st, "eps_t", [128, 1], F32)
    c.one_t = sb(nc, st, "one_t", [128, 1], F32)
    c.gains = sb(nc, st, "gains_sb", [128, 7 * 8], F32)
    c.ident_b = sb(nc, st, "ident_b", [128, 128], BF16)
    c.convw = sb(nc, st, "convw", [128, 24, 4], F32)
    c.convb = sb(nc, st, "convb", [128, 24], F32)
    c.dtb_rep = sb(nc, st, "dtb_rep", [128, 32], F32)
    c.a_rep = sb(nc, st, "a_rep", [128, 32], F32)
    c.dsk_rep = sb(nc, st, "dsk_rep", [128, 32], F32)
    c.normg_rep = sb(nc, st, "normg_rep", [128, 2048], F32)
    S.op("pool", lambda e: e.memset(c.ones_f[:, :], 1.0), writes=["ones_f"])
    S.op("pool", lambda e: e.memset(c.eps_t[:, :], EPS), writes=["consts"])
    S.op("pool", lambda e: e.memset(c.one_t[:, :], 1.0), writes=["consts"])
    S.op("pool", lambda e: e.memset(c.ident_f[:, :], 1.0), writes=["ident_f"])
    S.op("pool", lambda e: e.affine_select(out=c.ident_f[:, :], in_=c.ident_f[:, :], pattern=[[-1, 128]], base=0,
                                           channel_multiplier=1, compare_op=ALU.is_equal, fill=0.0),
         reads=["ident_f"], writes=["ident_f"])
    S.op("dve", lambda e: e.tensor_copy(out=c.ident_b[:, :], in_=c.ident_f[:, :]), reads=["ident_f"], writes=["ident_b"])
    S.dma("sp", lambda e: e.dma_start(out=c.gains[:, :].rearrange("p (g dc) -> p g dc", dc=8),
                                      in_=c.gains_d.rearrange("g (dc p) -> p g dc", p=128),
                                      allow_slow_non_contiguous=True),
          writes=["gains"])


def build(cfg):
    nc = bass.Bass("TRN2", target_bir_lowering=False)
    c = Ctx()
    c.nc = nc
    SEQ, NSEQ, DEPTH = cfg["SEQ"], cfg["NSEQ"], cfg["DEPTH"]
    NS = NSEQ * 8
    NT = SEQ + NS
    c.SEQ, c.NSEQ, c.NS, c.NT, c.DEPTH = SEQ, NSEQ, NS, NT, DEPTH
    c.NPAGES, c.NPOOL = cfg["NPAGES"], cfg["NPOOL"]
    c.TS = cfg["TS"]
    upto = cfg.get("upto", 99)

    def din(name, shape, dt=F32):
        return nc.dram_tensor(name, list(shape), dt, kind="ExternalInput").ap()

    def dout(name, shape, dt=F32):
        return nc.dram_tensor(name, list(shape), dt, kind="ExternalOutput").ap()

    def dscr(name, shape, dt=F32):
        return nc.dram_tensor(name, list(shape), dt, kind="Internal").ap()

    c.xp = din("x_prompt", [SEQ, D])
    c.xsm = din("x_sample", [NS, D])
    c.gains_d = din("gains", [7, D])
    w1d = [din(f"ffn{i + 1}_w1", [DEPTH, D, 2 * DFF]) for i in range(2)]
    w2d = [din(f"ffn{i + 1}_w2", [DEPTH, DFF, D]) for i in range(2)]
    c.w1 = [[w1d[i][l] for l in range(DEPTH)] for i in range(2)]
    c.w2 = [[w2d[i][l] for l in range(DEPTH)] for i in range(2)]
    c.w_in = din("w_in", [DEPTH, D, DPROJ])
    c.conv_w_d = din("conv_w", [DEPTH, 4, 3072])
    c.conv_b_d = din("conv_b", [DEPTH, 3072])
    c.dt_bias_d = din("dt_bias", [DEPTH, 32])
    c.a_log_d = din("a_log", [DEPTH, 32])
    c.d_skip_d = din("d_skip", [DEPTH, 32])
    c.ssd_norm_d = din("ssd_norm", [DEPTH, 2048])
    c.w_bs = din("w_branch_ssd", [DEPTH, 2048, D])
    c.w_ba = din("w_branch_attn", [DEPTH, D, D])
    c.w_o = din("w_out", [DEPTH, D, D])
    c.sconv = din("state_conv", [DEPTH, NSEQ, 3, 3072])
    c.sssm = din("state_ssm", [DEPTH, NSEQ, 32, 64, 128])
    if cfg.get("sample_dsa", False):
        c.cache_k = [din(f"cache_k{l}", [c.NPOOL * 4, 8192]) for l in range(DEPTH)]
        c.cache_v = [din(f"cache_v{l}", [c.NPOOL * 4, 8192]) for l in range(DEPTH)]
        c.cache_i = [din(f"cache_idx_k{l}", [c.NPOOL, 128 * 64]) for l in range(DEPTH)]
        c.ptab = din("page_table", [NSEQ, c.NPAGES], I32)

    c.yp = dout("y_prompt", [SEQ, D])
    c.ys = dout("y_sample", [NS, D])
    c.ok_p = dout("ok_p", [DEPTH, SEQ, 256])
    c.ov_p = dout("ov_p", [DEPTH, SEQ, 256])
    c.oi_p = dout("oi_p", [DEPTH, SEQ, 64])
    c.oh_p = dout("oh_p", [DEPTH, 32, 64, 128])
    c.oc_p = dout("oc_p", [DEPTH, 3, 3072])
    c.ok_s = dout("ok_s", [DEPTH, NS, 256])
    c.ov_s = dout("ov_s", [DEPTH, NS, 256])
    c.oi_s = dout("oi_s", [DEPTH, NS, 64])
    c.oh_s = dout("oh_s", [DEPTH, NSEQ, 32, 64, 128])
    c.oc_s = dout("oc_s", [DEPTH, NSEQ, 3, 3072])

    c.xT = dscr("xT_scratch", [D, NT])
    c.zS = dscr("zS", [NT, 2048])
    c.xbcT = dscr("xbcT", [3072, NT])
    c.dtS = dscr("dtS", [NT, 32])
    c.qT = dscr("qT", [16, 64, NT], BF16)
    c.kT = dscr("kT", [4, 64, NT], BF16)
    c.qiT = dscr("qiT", [8, 64, NT], BF16)
    c.kiT = dscr("kiT", [1, 64, NT], BF16)
    c.vB = dscr("vB", [NT, 256], BF16)
    c.wiS = dscr("wiS", [NT, 8])
    c.gsT = dscr("gsT", [D, NT])
    c.gaT = dscr("gaT", [D, NT])
    c.xS = dscr("xS", [NT, 2048])
    c.Btm = dscr("Btm", [NT, 512], BF16)
    c.BT = dscr("BT", [512, NT], BF16)
    c.CT = dscr("CT", [512, NT], BF16)
    c.ynT = dscr("ynT", [2048, NT], BF16)
    c.yaT = dscr("yaT", [D, NT], BF16)

    tiles = [(t0, min(c.TS, SEQ - t0)) for t0 in range(0, SEQ, c.TS)] + [(SEQ, NS)]
    wins = [(t0, 1, min(c.TS, SEQ - t0)) for t0 in range(0, SEQ, c.TS)] + [(SEQ, NSEQ, 8)]
    with ExitStack() as st:
        c.S = Sched(nc, st)
        setup_consts(c, st)
        phase_to_fm(c, c.xp, 0, SEQ, 0)
        phase_to_fm(c, c.xsm, 0, NS, SEQ)
        for l in range(DEPTH):
            load_layer_params(c, l)
            if upto >= 1:
                phase_ffn(c, l, 0, tiles)
            if upto >= 2:
                phase_inproj(c, l, tiles)
                phase_conv(c, l, wins)
            if upto >= 3:
                phase_ssd(c, l)
            if upto >= 4:
                phase_dsa_prompt(c, l)
            if upto >= 5 and cfg.get("sample_dsa", False):
                phase_dsa_sample(c, l)
            if upto >= 5:
                phase_merge(c, l, tiles)
                phase_ffn(c, l, 1, tiles)
        outs = [(t0, min(128, SEQ - t0), c.yp[t0:t0 + min(128, SEQ - t0), :]) for t0 in range(0, SEQ, 128)]
        outs.append((SEQ, NS, c.ys[0:NS, :]))
        phase_final(c, outs)
        c.S.finish()
        c.S.emit()
    return nc


def make_in_maps(inputs, cfg):
    SEQ, NSEQ, DEPTH = cfg["SEQ"], cfg["NSEQ"], cfg["DEPTH"]
    NS = NSEQ * 8
    f = lambda k: np.ascontiguousarray(np.asarray(inputs[k]))
    xp, xs = f("x_prompt"), f("x_sample")
    gains = np.ascontiguousarray(np.concatenate(
        [np.stack([f("ffn1_norm")[l], f("mix_norm")[l], f("ffn2_norm")[l]]) for l in range(DEPTH)]
        + [np.zeros((3, D), np.float32)] * (2 - DEPTH) + [f("final_norm")[None]], 0))
    shared = {"gains": gains}
    for k in ("ffn1_w1", "ffn1_w2", "ffn2_w1", "ffn2_w2", "w_in", "conv_w", "conv_b", "dt_bias", "a_log", "d_skip", "ssd_norm",
              "w_branch_ssd", "w_branch_attn", "w_out"):
        shared[k] = f(k)
    if cfg.get("sample_dsa", False):
        ck, cv, ci = f("cache_k"), f("cache_v"), f("cache_idx_k")
        npool = ck.shape[1]
        for l in range(DEPTH):
            shared[f"cache_k{l}"] = np.ascontiguousarray(ck[l].reshape(npool * 4, 8192))
            shared[f"cache_v{l}"] = np.ascontiguousarray(cv[l].reshape(npool * 4, 8192))
            shared[f"cache_idx_k{l}"] = np.ascontiguousarray(ci[l].reshape(npool, 128 * 64))
    sconv, sssm, pt = f("state_conv"), f("state_ssm"), f("page_table")
    in_maps = []
    for cid in range(8):
        m = dict(shared)
        m["x_prompt"] = xp[cid % xp.shape[0]]
        sl = slice(cid * NSEQ, (cid + 1) * NSEQ)
        m["x_sample"] = np.ascontiguousarray(xs[sl].reshape(NS, D))
        m["state_conv"] = np.ascontiguousarray(sconv[:, sl])
        m["state_ssm"] = np.ascontiguousarray(sssm[:, sl])
        if cfg.get("sample_dsa", False):
            m["page_table"] = np.ascontiguousarray(pt[sl]).astype(np.int32)
        in_maps.append(m)
    return in_maps


def gather_outputs(r, cfg, nb):
    SEQ, NSEQ, DEPTH = cfg["SEQ"], cfg["NSEQ"], cfg["DEPTH"]
    cat_p = lambda k, sh: np.stack([r[b][k] for b in range(nb)], 1).reshape(sh).astype(np.float32)
    cat_s = lambda k, sh: np.concatenate([r[cid][k].reshape((DEPTH, NSEQ) + r[cid][k].shape[1:][1:] if False else r[cid][k].shape) for cid in range(8)], 1)
    y_prompt = np.stack([r[b]["y_prompt"] for b in range(nb)]).astype(np.float32)
    y_sample = np.concatenate([r[cid]["y_sample"].reshape(NSEQ, 8, D) for cid in range(8)], 0).astype(np.float32)
    kp = cat_p("ok_p", (DEPTH, nb, SEQ, 4, 64))
    vp = cat_p("ov_p", (DEPTH, nb, SEQ, 4, 64))
    ip = cat_p("oi_p", (DEPTH, nb, SEQ, 64))
    hp = cat_p("oh_p", (DEPTH, nb, 32, 64, 128))
    cp = cat_p("oc_p", (DEPTH, nb, 3, 3072))
    ks = np.concatenate([r[cid]["ok_s"].reshape(DEPTH, NSEQ, 8, 4, 64) for cid in range(8)], 1).astype(np.float32)
    vs = np.concatenate([r[cid]["ov_s"].reshape(DEPTH, NSEQ, 8, 4, 64) for cid in range(8)], 1).astype(np.float32)
    is_ = np.concatenate([r[cid]["oi_s"].reshape(DEPTH, NSEQ, 8, 64) for cid in range(8)], 1).astype(np.float32)
    hs = np.concatenate([r[cid]["oh_s"] for cid in range(8)], 1).astype(np.float32)
    cs = np.concatenate([r[cid]["oc_s"] for cid in range(8)], 1).astype(np.float32)
    return (y_prompt, y_sample, kp, vp, ip, hp, cp, ks, vs, is_, hs, cs)


def kernel(**inputs):
    cfg = dict(SEQ=8192, NSEQ=4, NPAGES=128, NPOOL=5120, TS=1024, DEPTH=2, upto=5, sample_dsa=True)
    nc = build(cfg)
    in_maps = make_in_maps(inputs, cfg)
    res = run_bass_kernel_spmd(nc, in_maps, core_ids=list(range(8)))
    return gather_outputs(res.results, cfg, 2)
```

```python
import numpy as np
from contextlib import ExitStack
import concourse.bass as bass
import concourse.mybir as mybir
from concourse.bass_utils import run_bass_kernel_spmd

F32 = mybir.dt.float32
BF16 = mybir.dt.bfloat16
I32 = mybir.dt.int32
AF = mybir.ActivationFunctionType
ALU = mybir.AluOpType
AX = mybir.AxisListType

NDMA_SEM = 8
D = 1024
DFF = 2816
EPS = 1e-6


class Sched:
    ENGS = ("pe", "act", "dve", "pool", "sp")

    def __init__(self, nc, stack):
        self.nc = nc
        self.ops = {e: [] for e in self.ENGS}
        self.cnt = {e: 0 for e in self.ENGS}
        self.esem = {e: stack.enter_context(nc.semaphore("es_" + e)) for e in self.ENGS if e != "sp"}
        self.dsem = {q: [stack.enter_context(nc.semaphore(f"ds_{q}{i}")) for i in range(NDMA_SEM)]
                     for q in ("sp", "act", "pool")}
        self.dcnt = {q: 0 for q in self.dsem}
        self.dval = {q: [0] * NDMA_SEM for q in self.dsem}
        self.lastw = {}
        self.readers = {}
        self.seen = {e: {} for e in self.ENGS}
        self.out_tokens = []

    def _sem(self, name):
        if name[0] == "E":
            return self.esem[name[1:]]
        q, i = name[1:].split(":")
        return self.dsem[q][int(i)]

    def _deps(self, eng, reads, writes):
        toks = set()
        for k in reads:
            t = self.lastw.get(k)
            if t is not None:
                toks.add(t)
        for k in writes:
            t = self.lastw.get(k)
            if t is not None:
                toks.add(t)
            for r in self.readers.get(k, ()):
                toks.add(r)
        best = {}
        for (s, v, pe) in toks:
            if pe == "pe" and eng == "pe":
                continue
            if best.get(s, 0) < v:
                best[s] = v
        waits = []
        for s, v in best.items():
            if self.seen[eng].get(s, 0) >= v:
                continue
            self.seen[eng][s] = v
            waits.append((s, v))
        return waits

    def _commit(self, tok, reads, writes):
        for k in reads:
            self.readers.setdefault(k, []).append(tok)
        for k in writes:
            self.lastw[k] = tok
            self.readers[k] = []

    lim = None
    nrec = 0

    def _skip(self):
        if self.lim is None:
            return False
        self.nrec += 1
        return self.nrec > self.lim

    def op(self, eng, fn, reads=(), writes=()):
        if self._skip():
            return None
        waits = self._deps(eng, reads, writes)
        self.cnt[eng] += 1
        tok = ("E" + eng, self.cnt[eng], eng)
        self.ops[eng].append((waits, fn, (tok[0], 1)))
        self._commit(tok, reads, writes)
        return tok

    DRAM_KEYS = frozenset(["xT", "xbcT", "gsT", "gaT", "qT", "kT", "qiT", "kiT", "zS", "dtS", "vB", "wiS", "xS", "Btm", "BT", "CT",
                           "ynT", "yaT", "ok", "ov", "oi", "oh", "oc_p", "oc_s", "yout"])

    def dma(self, q, fn, reads=(), writes=(), is_output=False):
        if self._skip():
            return None
        if q == "sp" and any(isinstance(k, tuple) and k[0] in self.DRAM_KEYS for k in writes):
            q = "act"
        waits = self._deps(q, reads, writes)
        i = self.dcnt[q] % NDMA_SEM
        self.dcnt[q] += 1
        sname = f"D{q}:{i}"
        prev = self.dval[q][i]
        if prev > 0 and self.seen[q].get(sname, 0) < prev:
            self.seen[q][sname] = prev
            waits.append((sname, prev))
        self.dval[q][i] = prev + 16
        tok = (sname, prev + 16, "dma")
        self.ops[q].append((waits, fn, (sname, 16)))
        self._commit(tok, reads, writes)
        if is_output:
            self.out_tokens.append(tok)
        return tok

    def barrier(self):
        allw = {}
        for e in self.ENGS:
            if e != "sp" and self.cnt[e] > 0:
                allw["E" + e] = self.cnt[e]
        for q in self.dsem:
            for i in range(NDMA_SEM):
                if self.dval[q][i] > 0:
                    allw[f"D{q}:{i}"] = self.dval[q][i]
        for e in self.ENGS:
            waits = []
            for s_, v in allw.items():
                if self.seen[e].get(s_, 0) < v:
                    self.seen[e][s_] = v
                    waits.append((s_, v))
            if waits:
                self.ops[e].append((waits, None, None))

    def finish(self):
        best = {}
        for q in self.dsem:
            for i in range(NDMA_SEM):
                if self.dval[q][i] > 0:
                    best[f"D{q}:{i}"] = self.dval[q][i]
        self.ops["sp"].append((list(best.items()), None, None))

    def emit(self):
        nc = self.nc
        with nc.Block() as block:
            def run(engname):
                def body(eng):
                    for waits, fn, inc in self.ops[engname]:
                        for (s, v) in waits:
                            eng.wait_ge(self._sem(s), v)
                        if fn is None:
                            continue
                        ins = fn(eng)
                        ins.then_inc(self._sem(inc[0]), inc[1])
                return body
            block.tensor(run("pe"))
            block.scalar(run("act"))
            block.vector(run("dve"))
            block.gpsimd(run("pool"))
            block.sync(run("sp"))


class Ctx:
    pass


_UID = [0]


def sb(nc, st, name, shape, dt):
    _UID[0] += 1
    return st.enter_context(nc.sbuf_tensor(f"{name}_{_UID[0]}", list(shape), dt))


def ps(nc, st, name, shape, dt=F32):
    _UID[0] += 1
    return st.enter_context(nc.psum_tensor(f"{name}_{_UID[0]}", list(shape), dt))


def phase_to_fm(c, src, t_src0, ntok, t_dst0):
    nc, S = c.nc, c.S
    S.barrier()
    with ExitStack() as st:
        xin = [sb(nc, st, f"p0x{i}", [128, D], F32) for i in range(2)]
        xo = [sb(nc, st, f"p0o{i}", [128, 8, 128], F32) for i in range(2)]
        pt = [ps(nc, st, f"p0p{i}", [128, 8, 128], F32) for i in range(2)]
        nt = (ntok + 127) // 128
        for i in range(nt):
            n = min(128, ntok - i * 128)
            b = i % 2
            r0 = t_src0 + i * 128
            S.dma("sp", lambda e, b=b, r0=r0, n=n: e.dma_start(out=xin[b][0:n, :], in_=src[r0:r0 + n, :]),
                  writes=[("p0x", b)])
            for dc in range(8):
                S.op("pe", lambda e, b=b, dc=dc, n=n: e.transpose(pt[b][:, dc, 0:n], xin[b][0:n, dc * 128:(dc + 1) * 128],
                                                                 c.ident_f[0:n, 0:n]),
                     reads=[("p0x", b), "ident_f"], writes=[("p0p", b)])
            S.op("act", lambda e, b=b, n=n: e.copy(out=xo[b][:, :, 0:n], in_=pt[b][:, :, 0:n]),
                 reads=[("p0p", b)], writes=[("p0o", b)])
            d0 = t_dst0 + i * 128
            S.dma("sp", lambda e, b=b, d0=d0, n=n: e.dma_start(
                out=c.xT[:, d0:d0 + n].rearrange("(dc p) t -> p dc t", p=128), in_=xo[b][:, :, 0:n]),
                reads=[("p0o", b)], writes=[("xT", d0 // 128)])


def rmsnorm_fm(c, st, xs, xs_key, T, gain_ap, hnT, hn_key, tag):
    nc, S = c.nc, c.S
    sq = [sb(nc, st, f"{tag}sq{i}", [128, T], F32) for i in range(2)]
    rstd = sb(nc, st, f"{tag}rstd", [128, T], F32)
    nh = (T + 511) // 512
    pss = ps(nc, st, f"{tag}pss", [128, nh, 512], F32)
    for dc in range(8):
        b = dc % 2
        S.op("act", lambda e, b=b, dc=dc: e.activation(out=sq[b][:, :], in_=xs[:, dc, :], func=AF.Square),
             reads=[xs_key], writes=[(tag + "sq", b)])
        for h in range(nh):
            w = min(512, T - h * 512)
            S.op("pe", lambda e, b=b, dc=dc, h=h, w=w: e.matmul(pss[:, h, 0:w], lhsT=c.ones_f[:, :], rhs=sq[b][:, h * 512:h * 512 + w],
                                                              start=(dc == 0), stop=(dc == 7)),
                 reads=[(tag + "sq", b), "ones_f"], writes=[(tag + "pss")])
    for h in range(nh):
        w = min(512, T - h * 512)
        S.op("act", lambda e, h=h, w=w: e.activation(out=rstd[:, h * 512:h * 512 + w], in_=pss[:, h, 0:w], func=AF.Sqrt,
                                                    scale=1.0 / D, bias=c.eps_t[:, 0:1]),
             reads=[(tag + "pss"), "consts"], writes=[(tag + "rstd")])
    S.op("dve", lambda e: e.reciprocal(out=rstd[:, :], in_=rstd[:, :]), reads=[(tag + "rstd")], writes=[(tag + "rstd")])
    for dc in range(8):
        S.op("dve", lambda e, dc=dc: e.scalar_tensor_tensor(out=hnT[:, dc, :], in0=xs[:, dc, :], scalar=gain_ap[:, dc:dc + 1],
                                                          in1=rstd[:, :], op0=ALU.mult, op1=ALU.mult),
             reads=[xs_key, (tag + "rstd"), "gains"], writes=[hn_key])


def phase_ffn(c, layer, which, tiles):
    nc, S = c.nc, c.S
    S.barrier()
    w1 = c.w1[which][layer]
    w2 = c.w2[which][layer]
    gain = c.gains[:, (layer * 3 + (0 if which == 0 else 2)) * 8:(layer * 3 + (0 if which == 0 else 2)) * 8 + 8]
    TS = c.TS
    w1v = w1.rearrange("(ko ki) n -> ki ko n", ki=128)
    w2v = w2.rearrange("(fo fi) n -> fi fo n", fi=128)
    NF = DFF // 128
    with ExitStack() as st:
        xs = sb(nc, st, "f_xs", [128, 8, TS], F32)
        hnT = sb(nc, st, "f_hnT", [128, 8, TS], BF16)
        actT = sb(nc, st, "f_actT", [128, NF, TS], BF16)
        w1s = [sb(nc, st, f"f_w1s{i}", [128, 2, 8, 128], F32) for i in range(2)]
        w1b = [sb(nc, st, f"f_w1b{i}", [128, 2, 8, 128], BF16) for i in range(2)]
        w2s = [sb(nc, st, f"f_w2s{i}", [128, NF, 128], F32) for i in range(2)]
        w2b = [sb(nc, st, f"f_w2b{i}", [128, NF, 128], BF16) for i in range(2)]
        sa = [sb(nc, st, f"f_sa{i}", [128, 512], F32) for i in range(2)]
        xo = [sb(nc, st, f"f_xo{i}", [128, 512], F32) for i in range(2)]
        pa = [ps(nc, st, f"f_pa{i}", [128, 512], F32) for i in range(2)]
        pb = [ps(nc, st, f"f_pb{i}", [128, 512], F32) for i in range(2)]
        po = [ps(nc, st, f"f_po{i}", [128, 512], F32) for i in range(2)]
        wi = 0
        w2i = 0
        it = 0
        for (t0, T) in tiles:
            S.dma("sp", lambda e, t0=t0, T=T: e.dma_start(out=xs[:, :, 0:T],
                                                         in_=c.xT[:, t0:t0 + T].rearrange("(dc p) t -> p dc t", p=128)),
                  reads=[("xT", k) for k in range(t0 // 128, (t0 + T + 127) // 128)], writes=["f_xs"])
            with ExitStack() as st2:
                rmsnorm_fm(c, st2, xs[:, :, 0:T], "f_xs", T, gain, hnT[:, :, 0:T], "f_hnT", "fn")
            nh = (T + 511) // 512
            for j in range(NF):
                b = wi % 2
                wi += 1
                S.dma("sp", lambda e, b=b, j=j: e.dma_start(out=w1s[b][:, 0, :, :], in_=w1v[:, :, j * 128:(j + 1) * 128]),
                      writes=[("f_w1s", b)])
                S.dma("sp", lambda e, b=b, j=j: e.dma_start(out=w1s[b][:, 1, :, :], in_=w1v[:, :, DFF + j * 128:DFF + (j + 1) * 128]),
                      writes=[("f_w1s", b)])
                S.op("pool", lambda e, b=b: e.tensor_copy(out=w1b[b][:, :, :, :], in_=w1s[b][:, :, :, :]),
                     reads=[("f_w1s", b)], writes=[("f_w1b", b)])
                for h in range(nh):
                    w = min(512, T - h * 512)
                    pbuf = it % 2
                    it += 1
                    for ko in range(8):
                        S.op("pe", lambda e, b=b, ko=ko, h=h, w=w, pbuf=pbuf: e.matmul(
                            pa[pbuf][:, 0:w], lhsT=w1b[b][:, 0, ko, :], rhs=hnT[:, ko, h * 512:h * 512 + w],
                            start=(ko == 0), stop=(ko == 7)),
                            reads=[("f_w1b", b), "f_hnT"], writes=[("f_pa", pbuf)])
                    for ko in range(8):
                        S.op("pe", lambda e, b=b, ko=ko, h=h, w=w, pbuf=pbuf: e.matmul(
                            pb[pbuf][:, 0:w], lhsT=w1b[b][:, 1, ko, :], rhs=hnT[:, ko, h * 512:h * 512 + w],
                            start=(ko == 0), stop=(ko == 7)),
                            reads=[("f_w1b", b), "f_hnT"], writes=[("f_pb", pbuf)])
                    S.op("act", lambda e, w=w, pbuf=pbuf: e.activation(out=sa[pbuf][:, 0:w], in_=pa[pbuf][:, 0:w], func=AF.Silu),
                         reads=[("f_pa", pbuf)], writes=[("f_sa", pbuf)])
                    S.op("dve", lambda e, j=j, h=h, w=w, pbuf=pbuf: e.tensor_tensor(
                        out=actT[:, j, h * 512:h * 512 + w], in0=sa[pbuf][:, 0:w], in1=pb[pbuf][:, 0:w], op=ALU.mult),
                        reads=[("f_sa", pbuf), ("f_pb", pbuf)], writes=[("f_actT", j)])
            for cc in range(8):
                b = w2i % 2
                w2i += 1
                S.dma("sp", lambda e, b=b, cc=cc: e.dma_start(out=w2s[b][:, :, :], in_=w2v[:, :, cc * 128:(cc + 1) * 128]),
                      writes=[("f_w2s", b)])
                S.op("pool", lambda e, b=b: e.tensor_copy(out=w2b[b][:, :, :], in_=w2s[b][:, :, :]),
                     reads=[("f_w2s", b)], writes=[("f_w2b", b)])
                for h in range(nh):
                    w = min(512, T - h * 512)
                    pbuf = it % 2
                    it += 1
                    for fo in range(NF):
                        S.op("pe", lambda e, b=b, fo=fo, h=h, w=w, pbuf=pbuf: e.matmul(
                            po[pbuf][:, 0:w], lhsT=w2b[b][:, fo, :], rhs=actT[:, fo, h * 512:h * 512 + w],
                            start=(fo == 0), stop=(fo == NF - 1)),
                            reads=[("f_w2b", b), ("f_actT", fo)], writes=[("f_po", pbuf)])
                    S.op("dve", lambda e, cc=cc, h=h, w=w, pbuf=pbuf: e.scalar_tensor_tensor(
                        out=xo[pbuf][:, 0:w], in0=po[pbuf][:, 0:w], scalar=0.5, in1=xs[:, cc, h * 512:h * 512 + w],
                        op0=ALU.mult, op1=ALU.add),
                        reads=[("f_po", pbuf), "f_xs"], writes=[("f_xo", pbuf)])
                    a0 = t0 + h * 512
                    S.dma("sp", lambda e, cc=cc, a0=a0, w=w, pbuf=pbuf: e.dma_start(
                        out=c.xT[cc * 128:(cc + 1) * 128, a0:a0 + w], in_=xo[pbuf][:, 0:w]),
                        reads=[("f_xo", pbuf)], writes=[("xT", k) for k in range(a0 // 128, (a0 + w + 127) // 128)])


C_Z, C_XBC, C_DT, C_Q, C_K, C_V, C_QI, C_KI, C_WI, C_GS, C_GA = 0, 2048, 5120, 5152, 6176, 6432, 6688, 7200, 7264, 7272, 8296
DPROJ = 9320
IDX_W_SCALE = (8 ** -0.5) * (64 ** -0.5)


def inproj_slabs():
    sl = []
    for i in range(4):
        sl.append((C_Z + 512 * i, 512, [("tm", 0, 512, "z", 512 * i)]))
    for i in range(6):
        sl.append((C_XBC + 512 * i, 512, [("fm", 128 * j, 128, "xbc", 512 * i + 128 * j) for j in range(4)]))
    sl.append((C_DT, 32, [("tm", 0, 32, "dt", 0)]))
    for i in range(2):
        sl.append((C_Q + 512 * i, 512, [("fm", 64 * j, 64, "q", 8 * i + j) for j in range(8)]))
    sl.append((C_K, 512, [("tm", 0, 512, "kv", 0)] + [("fm", 64 * j, 64, "k", j) for j in range(4)]))
    sl.append((C_QI, 512, [("fm", 64 * j, 64, "qi", j) for j in range(8)]))
    sl.append((C_KI, 72, [("tm", 0, 72, "kiw", 0), ("fm", 0, 64, "ki", 0)]))
    for i in range(2):
        sl.append((C_GS + 512 * i, 512, [("fm", 128 * j, 128, "gs", 512 * i + 128 * j) for j in range(4)]))
    for i in range(2):
        sl.append((C_GA + 512 * i, 512, [("fm", 128 * j, 128, "ga", 512 * i + 128 * j) for j in range(4)]))
    return sl


def tkeys(name, a, b):
    return [(name, k) for k in range(a // 128, (b + 127) // 128)]


def phase_inproj(c, layer, tiles):
    nc, S = c.nc, c.S
    S.barrier()
    SEQ = c.SEQ
    wv = c.w_in[layer].rearrange("(ko ki) n -> ki ko n", ki=128)
    gain = c.gains[:, (layer * 3 + 1) * 8:(layer * 3 + 1) * 8 + 8]
    TS = c.TS
    slabs = inproj_slabs()
    with ExitStack() as st:
        xs = sb(nc, st, "a_xs", [128, 8, TS], F32)
        hnT = sb(nc, st, "a_hnT", [128, 8, TS], BF16)
        wS = [sb(nc, st, f"a_wS{i}", [128, 8, 512], F32) for i in range(2)]
        wB = [sb(nc, st, f"a_wB{i}", [128, 8, 512], BF16) for i in range(2)]
        sf = [sb(nc, st, f"a_sf{i}", [128, 512], F32) for i in range(3)]
        sh = [sb(nc, st, f"a_sh{i}", [128, 512], BF16) for i in range(3)]
        pf = [ps(nc, st, f"a_pf{i}", [128, 512], F32) for i in range(3)]
        wi_ = 0
        it = 0
        for (t0, T) in tiles:
            is_s = t0 >= SEQ
            S.dma("sp", lambda e, t0=t0, T=T: e.dma_start(out=xs[:, :, 0:T],
                                                         in_=c.xT[:, t0:t0 + T].rearrange("(dc p) t -> p dc t", p=128)),
                  reads=tkeys("xT", t0, t0 + T), writes=["a_xs"])
            with ExitStack() as st2:
                rmsnorm_fm(c, st2, xs[:, :, 0:T], "a_xs", T, gain, hnT[:, :, 0:T], "a_hnT", "an")
            for (c0, ncol, jobs) in slabs:
                b = wi_ % 2
                wi_ += 1
                S.dma("sp", lambda e, b=b, c0=c0, ncol=ncol: e.dma_start(out=wS[b][:, :, 0:ncol], in_=wv[:, :, c0:c0 + ncol]),
                      writes=[("a_wS", b)])
                S.op("pool", lambda e, b=b, ncol=ncol: e.tensor_copy(out=wB[b][:, :, 0:ncol], in_=wS[b][:, :, 0:ncol]),
                     reads=[("a_wS", b)], writes=[("a_wB", b)])
                for (kind, off, n, name, idx) in jobs:
                    if kind == "fm":
                        for h in range((T + 511) // 512):
                            w = min(512, T - h * 512)
                            a0 = t0 + h * 512
                            pb_ = it % 3
                            it += 1
                            for ko in range(8):
                                S.op("pe", lambda e, b=b, ko=ko, off=off, n=n, h=h, w=w, pb_=pb_: e.matmul(
                                    pf[pb_][0:n, 0:w], lhsT=wB[b][:, ko, off:off + n], rhs=hnT[:, ko, h * 512:h * 512 + w],
                                    start=(ko == 0), stop=(ko == 7)),
                                    reads=[("a_wB", b), "a_hnT"], writes=[("a_pf", pb_)])
                            if name == "xbc":
                                S.op("act", lambda e, n=n, w=w, pb_=pb_: e.copy(out=sf[pb_][0:n, 0:w], in_=pf[pb_][0:n, 0:w]),
                                     reads=[("a_pf", pb_)], writes=[("a_sf", pb_)])
                                S.dma("sp", lambda e, idx=idx, a0=a0, w=w, pb_=pb_: e.dma_start(
                                    out=c.xbcT[idx:idx + 128, a0:a0 + w], in_=sf[pb_][:, 0:w]),
                                    reads=[("a_sf", pb_)], writes=tkeys("xbcT", a0, a0 + w))
                            elif name in ("gs", "ga"):
                                dst = c.gsT if name == "gs" else c.gaT
                                S.op("act", lambda e, n=n, w=w, pb_=pb_: e.activation(out=sf[pb_][0:n, 0:w], in_=pf[pb_][0:n, 0:w],
                                                                                    func=AF.Sigmoid),
                                     reads=[("a_pf", pb_)], writes=[("a_sf", pb_)])
                                S.dma("sp", lambda e, dst=dst, idx=idx, a0=a0, w=w, pb_=pb_: e.dma_start(
                                    out=dst[idx:idx + 128, a0:a0 + w], in_=sf[pb_][:, 0:w]),
                                    reads=[("a_sf", pb_)], writes=tkeys(name + "T", a0, a0 + w))
                            else:
                                dst = {"q": c.qT, "k": c.kT, "qi": c.qiT, "ki": c.kiT}[name]
                                scl = 0.125 if name == "q" else 1.0
                                S.op("act", lambda e, n=n, w=w, pb_=pb_, scl=scl: e.mul(out=sh[pb_][0:n, 0:w], in_=pf[pb_][0:n, 0:w],
                                                                                      mul=scl),
                                     reads=[("a_pf", pb_)], writes=[("a_sh", pb_)])
                                S.dma("sp", lambda e, dst=dst, idx=idx, a0=a0, w=w, pb_=pb_: e.dma_start(
                                    out=dst[idx, :, a0:a0 + w], in_=sh[pb_][0:64, 0:w]),
                                    reads=[("a_sh", pb_)], writes=tkeys(name + "T", a0, a0 + w))
                    else:
                        for i in range((T + 127) // 128):
                            nt_ = min(128, T - i * 128)
                            a0 = t0 + i * 128
                            pb_ = it % 3
                            it += 1
                            for ko in range(8):
                                S.op("pe", lambda e, b=b, ko=ko, off=off, n=n, i=i, nt_=nt_, pb_=pb_: e.matmul(
                                    pf[pb_][0:nt_, 0:n], lhsT=hnT[:, ko, i * 128:i * 128 + nt_], rhs=wB[b][:, ko, off:off + n],
                                    start=(ko == 0), stop=(ko == 7)),
                                    reads=[("a_wB", b), "a_hnT"], writes=[("a_pf", pb_)])
                            S.op("act", lambda e, n=n, nt_=nt_, pb_=pb_: e.copy(out=sf[pb_][0:nt_, 0:n], in_=pf[pb_][0:nt_, 0:n]),
                                 reads=[("a_pf", pb_)], writes=[("a_sf", pb_)])
                            if name == "z":
                                S.dma("sp", lambda e, idx=idx, a0=a0, nt_=nt_, pb_=pb_: e.dma_start(
                                    out=c.zS[a0:a0 + nt_, idx:idx + 512], in_=sf[pb_][0:nt_, 0:512]),
                                    reads=[("a_sf", pb_)], writes=tkeys("zS", a0, a0 + nt_))
                            elif name == "dt":
                                S.dma("sp", lambda e, a0=a0, nt_=nt_, pb_=pb_: e.dma_start(
                                    out=c.dtS[a0:a0 + nt_, :], in_=sf[pb_][0:nt_, 0:32]),
                                    reads=[("a_sf", pb_)], writes=tkeys("dtS", a0, a0 + nt_))
                            elif name == "kv":
                                ko_, vo_ = (c.ok_s, c.ov_s) if is_s else (c.ok_p, c.ov_p)
                                r0 = a0 - SEQ if is_s else a0
                                S.dma("sp", lambda e, ko_=ko_, r0=r0, nt_=nt_, pb_=pb_: e.dma_start(
                                    out=ko_[layer, r0:r0 + nt_, :], in_=sf[pb_][0:nt_, 0:256]),
                                    reads=[("a_sf", pb_)], writes=[("ok", layer, a0)])
                                S.dma("sp", lambda e, vo_=vo_, r0=r0, nt_=nt_, pb_=pb_: e.dma_start(
                                    out=vo_[layer, r0:r0 + nt_, :], in_=sf[pb_][0:nt_, 256:512]),
                                    reads=[("a_sf", pb_)], writes=[("ov", layer, a0)])
                                S.op("dve", lambda e, nt_=nt_, pb_=pb_: e.tensor_copy(out=sh[pb_][0:nt_, 0:256], in_=sf[pb_][0:nt_, 256:512]),
                                     reads=[("a_sf", pb_)], writes=[("a_sh", pb_)])
                                S.dma("sp", lambda e, a0=a0, nt_=nt_, pb_=pb_: e.dma_start(
                                    out=c.vB[a0:a0 + nt_, :], in_=sh[pb_][0:nt_, 0:256]),
                                    reads=[("a_sh", pb_)], writes=tkeys("vB", a0, a0 + nt_))
                            elif name == "kiw":
                                io_ = c.oi_s if is_s else c.oi_p
                                r0 = a0 - SEQ if is_s else a0
                                S.dma("sp", lambda e, io_=io_, r0=r0, nt_=nt_, pb_=pb_: e.dma_start(
                                    out=io_[layer, r0:r0 + nt_, :], in_=sf[pb_][0:nt_, 0:64]),
                                    reads=[("a_sf", pb_)], writes=[("oi", layer, a0)])
                                S.op("dve", lambda e, nt_=nt_, pb_=pb_: e.tensor_scalar(
                                    out=sf[pb_][0:nt_, 64:72], in0=sf[pb_][0:nt_, 64:72], scalar1=IDX_W_SCALE, scalar2=None, op0=ALU.mult),
                                    reads=[("a_sf", pb_)], writes=[("a_sf", pb_)])
                                S.dma("sp", lambda e, a0=a0, nt_=nt_, pb_=pb_: e.dma_start(
                                    out=c.wiS[a0:a0 + nt_, :], in_=sf[pb_][0:nt_, 64:72]),
                                    reads=[("a_sf", pb_)], writes=tkeys("wiS", a0, a0 + nt_))


def load_layer_params(c, layer):
    S = c.S
    for i in range(4):
        S.dma("sp", lambda e, i=i: e.dma_start(out=c.convw[:, :, i], in_=c.conv_w_d[layer][i].rearrange("(k p) -> p k", p=128),
                                               allow_slow_non_contiguous=True), writes=["lay_conv"])
    S.dma("sp", lambda e: e.dma_start(out=c.convb[:, :], in_=c.conv_b_d[layer].rearrange("(k p) -> p k", p=128),
                                      allow_slow_non_contiguous=True), writes=["lay_conv"])
    S.dma("sp", lambda e: e.dma_start(out=c.dtb_rep[:, :], in_=c.dt_bias_d[layer].partition_broadcast(128)), writes=["lay_ssd"])
    S.dma("sp", lambda e: e.dma_start(out=c.a_rep[:, :], in_=c.a_log_d[layer].partition_broadcast(128)), writes=["lay_ssd"])
    S.dma("sp", lambda e: e.dma_start(out=c.dsk_rep[:, :], in_=c.d_skip_d[layer].partition_broadcast(128)), writes=["lay_ssd"])
    S.dma("sp", lambda e: e.dma_start(out=c.normg_rep[:, :], in_=c.ssd_norm_d[layer].partition_broadcast(128)), writes=["lay_ssd"])
    S.op("act", lambda e: e.activation(out=c.a_rep[:, :], in_=c.a_rep[:, :], func=AF.Exp), reads=["lay_ssd"], writes=["lay_ssd"])
    S.op("dve", lambda e: e.tensor_scalar(out=c.a_rep[:, :], in0=c.a_rep[:, :], scalar1=-1.0, scalar2=None, op0=ALU.mult),
         reads=["lay_ssd"], writes=["lay_ssd"])


def phase_conv(c, layer, wins):
    nc, S = c.nc, c.S
    S.barrier()
    SEQ = c.SEQ
    WMAX = max(3 + c.TS, c.NSEQ * 11)
    with ExitStack() as st:
        xw = [sb(nc, st, f"c_xw{i}", [128, 4, WMAX], F32) for i in range(2)]
        acc = sb(nc, st, "c_acc", [128, WMAX], F32)
        xc = sb(nc, st, "c_xc", [128, 4, c.TS], F32)
        xcb = sb(nc, st, "c_xcb", [128, 4, c.TS], BF16)
        of = [sb(nc, st, f"c_of{i}", [128, 512], F32) for i in range(2)]
        ob = [sb(nc, st, f"c_ob{i}", [128, 512], BF16) for i in range(2)]
        tail = sb(nc, st, "c_tail", [128, 24, c.NSEQ, 3], F32)
        ptf = [ps(nc, st, f"c_pf{i}", [128, 512], F32) for i in range(2)]
        ptb = [ps(nc, st, f"c_pb{i}", [128, 512], BF16) for i in range(2)]
        gi = 0
        it = 0
        for (t0, nseg, L) in wins:
            is_s = t0 >= SEQ
            T = nseg * L
            W = 3 + L
            for cg in range(6):
                b = gi % 2
                gi += 1
                rows = slice(cg * 512, (cg + 1) * 512)
                xv = xw[b][:, :, 0:nseg * W].rearrange("p k (s w) -> p k s w", w=W)
                for k in range(4):
                    r0 = cg * 512 + k * 128
                    for s_ in range(nseg):
                        S.dma("sp", lambda e, xv=xv, r0=r0, k=k, t0=t0, L=L, s_=s_: e.dma_start(
                            out=xv[:, k, s_, 3:3 + L], in_=c.xbcT[r0:r0 + 128, t0 + s_ * L:t0 + (s_ + 1) * L]),
                            reads=tkeys("xbcT", t0, t0 + T), writes=[("c_xw", b)])
                        if is_s:
                            S.dma("sp", lambda e, xv=xv, r0=r0, k=k, s_=s_: e.dma_start(
                                out=xv[:, k, s_, 0:3], in_=c.sconv[layer, s_, :, r0:r0 + 128].rearrange("i p -> p i"),
                                allow_slow_non_contiguous=True), writes=[("c_xw", b)])
                if is_s:
                    pass
                elif t0 == 0:
                    S.op("pool", lambda e, xv=xv: e.memset(xv[:, :, :, 0:3], 0.0), writes=[("c_xw", b)])
                else:
                    S.dma("sp", lambda e, xv=xv, rows=rows, t0=t0: e.dma_start(
                        out=xv[:, :, 0, 0:3], in_=c.xbcT[rows, t0 - 3:t0].rearrange("(k p) t -> p k t", p=128)),
                        reads=tkeys("xbcT", t0 - 3, t0), writes=[("c_xw", b)])
                av = acc[:, 0:T].rearrange("p (s t) -> p s t", t=L)
                for k in range(4):
                    cc = cg * 4 + k
                    S.op("dve", lambda e, xv=xv, av=av, k=k, cc=cc, L=L: e.tensor_scalar(
                        out=av, in0=xv[:, k, :, 0:L], scalar1=c.convw[:, cc, 0:1], scalar2=c.convb[:, cc:cc + 1],
                        op0=ALU.mult, op1=ALU.add), reads=[("c_xw", b), "lay_conv"], writes=["c_acc"])
                    for i in range(1, 4):
                        S.op("dve", lambda e, xv=xv, av=av, k=k, cc=cc, L=L, i=i: e.scalar_tensor_tensor(
                            out=av, in0=xv[:, k, :, i:i + L], scalar=c.convw[:, cc, i:i + 1], in1=av,
                            op0=ALU.mult, op1=ALU.add), reads=[("c_xw", b), "lay_conv", "c_acc"], writes=["c_acc"])
                    if cg < 4:
                        S.op("act", lambda e, k=k, T=T: e.activation(out=xc[:, k, 0:T], in_=acc[:, 0:T], func=AF.Silu),
                             reads=["c_acc"], writes=["c_xc"])
                    else:
                        S.op("act", lambda e, k=k, T=T: e.activation(out=xcb[:, k, 0:T], in_=acc[:, 0:T], func=AF.Silu),
                             reads=["c_acc"], writes=["c_xcb"])
                if cg >= 4:
                    dst = c.BT if cg == 4 else c.CT
                    nm = "BT" if cg == 4 else "CT"
                    S.dma("sp", lambda e, dst=dst, t0=t0, T=T: e.dma_start(
                        out=dst[:, t0:t0 + T].rearrange("(k p) t -> p k t", p=128), in_=xcb[:, :, 0:T]),
                        reads=["c_xcb"], writes=tkeys(nm, t0, t0 + T))
                if cg < 5:
                    for j in range((T + 127) // 128):
                        nt_ = min(128, T - j * 128)
                        a0 = t0 + j * 128
                        pb_ = it % 2
                        it += 1
                        if cg < 4:
                            for k in range(4):
                                S.op("pe", lambda e, k=k, j=j, nt_=nt_, pb_=pb_: e.transpose(
                                    ptf[pb_][0:nt_, k * 128:(k + 1) * 128], xc[:, k, j * 128:j * 128 + nt_], c.ident_f[:, :]),
                                    reads=["c_xc", "ident_f"], writes=[("c_pf", pb_)])
                            S.op("act", lambda e, nt_=nt_, pb_=pb_: e.copy(out=of[pb_][0:nt_, :], in_=ptf[pb_][0:nt_, :]),
                                 reads=[("c_pf", pb_)], writes=[("c_of", pb_)])
                            S.dma("sp", lambda e, cg=cg, a0=a0, nt_=nt_, pb_=pb_: e.dma_start(
                                out=c.xS[a0:a0 + nt_, cg * 512:(cg + 1) * 512], in_=of[pb_][0:nt_, :]),
                                reads=[("c_of", pb_)], writes=tkeys("xS", a0, a0 + nt_))
                        else:
                            for k in range(4):
                                S.op("pe", lambda e, k=k, j=j, nt_=nt_, pb_=pb_: e.transpose(
                                    ptb[pb_][0:nt_, k * 128:(k + 1) * 128], xcb[:, k, j * 128:j * 128 + nt_], c.ident_b[:, :]),
                                    reads=["c_xcb", "ident_b"], writes=[("c_pb", pb_)])
                            S.op("act", lambda e, nt_=nt_, pb_=pb_: e.copy(out=ob[pb_][0:nt_, :], in_=ptb[pb_][0:nt_, :]),
                                 reads=[("c_pb", pb_)], writes=[("c_ob", pb_)])
                            S.dma("sp", lambda e, a0=a0, nt_=nt_, pb_=pb_: e.dma_start(
                                out=c.Btm[a0:a0 + nt_, :], in_=ob[pb_][0:nt_, :]),
                                reads=[("c_ob", pb_)], writes=tkeys("Btm", a0, a0 + nt_))
        S.dma("sp", lambda e: e.dma_start(out=tail[:, :, 0, :], in_=c.xbcT[:, SEQ - 3:SEQ].rearrange("(k p) t -> p k t", p=128)),
              reads=tkeys("xbcT", SEQ - 3, SEQ), writes=["c_tail"])
        for i in range(3):
            S.dma("sp", lambda e, i=i: e.dma_start(out=c.oc_p[layer, i].rearrange("(k p) -> p k", p=128), in_=tail[:, :, 0, i],
                                                   allow_slow_non_contiguous=True), reads=["c_tail"], writes=[("oc_p", layer, i)])
        for s_ in range(c.NSEQ):
            a = SEQ + 8 * s_ + 5
            S.dma("sp", lambda e, s_=s_, a=a: e.dma_start(out=tail[:, :, s_, :], in_=c.xbcT[:, a:a + 3].rearrange("(k p) t -> p k t", p=128)),
                  reads=tkeys("xbcT", a, a + 3), writes=["c_tail"])
            for i in range(3):
                S.dma("sp", lambda e, s_=s_, i=i: e.dma_start(out=c.oc_s[layer, s_, i].rearrange("(k p) -> p k", p=128), in_=tail[:, :, s_, i],
                                                              allow_slow_non_contiguous=True), reads=["c_tail"], writes=[("oc_s", layer, s_, i)])


def phase_ssd(c, layer):
    nc, S = c.nc, c.S
    S.barrier()
    SEQ, NSEQ = c.SEQ, c.NSEQ
    with ExitStack() as st:
        x_tm = [sb(nc, st, f"s_x{i}", [128, 2048], F32) for i in range(2)]
        z_tm = [sb(nc, st, f"s_z{i}", [128, 2048], F32) for i in range(2)]
        B_tm = [sb(nc, st, f"s_B{i}", [128, 512], BF16) for i in range(2)]
        BTt = [sb(nc, st, f"s_BT{i}", [128, 4, 128], BF16) for i in range(2)]
        CTt = [sb(nc, st, f"s_CT{i}", [128, 4, 128], BF16) for i in range(2)]
        dtr = [sb(nc, st, f"s_dtr{i}", [128, 32], F32) for i in range(2)]
        xb = sb(nc, st, "s_xb", [128, 32], F32)
        ab = sb(nc, st, "s_ab", [128, 32], F32)
        dt = sb(nc, st, "s_dt", [128, 32], F32)
        la = sb(nc, st, "s_la", [128, 32], F32)
        acs = sb(nc, st, "s_acs", [128, 32], F32)
        acsT = sb(nc, st, "s_acsT", [128, 128], F32)
        Rm = [sb(nc, st, f"s_Rm{i}", [128, 8, 128], BF16) for i in range(3)]
        las = [sb(nc, st, f"s_las{i}", [128, 32], BF16) for i in range(3)]
        lres = sb(nc, st, "s_lres", [128, 32], F32)
        ones_b = sb(nc, st, "s_onesb", [128, 128], BF16)
        diff = sb(nc, st, "s_diff", [128, 8, 128], F32)
        dA = sb(nc, st, "s_dA", [128, 8, 128], F32)
        PAs = sb(nc, st, "s_PAs", [128, 8, 128], F32)
        CBs = sb(nc, st, "s_CBs", [128, 128], F32)
        CBL = sb(nc, st, "s_CBL", [128, 8, 128], BF16)
        Cdec = sb(nc, st, "s_Cdec", [128, 8, 128], BF16)
        xdt = sb(nc, st, "s_xdt", [128, 8, 64], BF16)
        xdec = sb(nc, st, "s_xdec", [128, 8, 64], BF16)
        dte = sb(nc, st, "s_dte", [128, 8], F32)
        t1 = sb(nc, st, "s_t1", [128, 512], F32)
        y1 = sb(nc, st, "s_y1", [128, 512], F32)
        sz = sb(nc, st, "s_sz", [128, 512], F32)
        jk = sb(nc, st, "s_jk", [128, 512], F32)
        ss = sb(nc, st, "s_ss", [128, 1], F32)
        y3 = sb(nc, st, "s_y3", [128, 512], BF16)
        yT = [sb(nc, st, f"s_yT{i}", [128, 4, 128], BF16) for i in range(2)]
        hT = sb(nc, st, "s_hT", [128, 32, 64], F32)
        hTb = sb(nc, st, "s_hTb", [128, 32, 64], BF16)
        h0 = sb(nc, st, "s_h0", [128, 32, 128], F32)
        utri = sb(nc, st, "s_utri", [128, 128], F32)
        negtri = sb(nc, st, "s_negtri", [128, 128], F32)
        p_acs = ps(nc, st, "s_pacs", [128, 4, 128], F32)
        PA0 = ps(nc, st, "s_PA0", [128, 4, 128], F32)
        CBT = ps(nc, st, "s_CBT", [128, 512], F32)
        py = ps(nc, st, "s_py", [128, 512], F32)
        pst = ps(nc, st, "s_pst", [128, 512], F32)
        pT = ps(nc, st, "s_pT", [128, 8, 128], BF16)
        PA1 = ps(nc, st, "s_PA1", [128, 4, 128], F32)
        PAh = [PA0, PA1]

        S.op("pool", lambda e: e.memset(utri[:, :], 1.0), writes=["s_utri"])
        S.op("pool", lambda e: e.affine_select(out=utri[:, :], in_=utri[:, :], pattern=[[1, 128]], base=0, channel_multiplier=-1,
                                               compare_op=ALU.is_ge, fill=0.0), reads=["s_utri"], writes=["s_utri"])
        S.op("pool", lambda e: e.memset(negtri[:, :], 0.0), writes=["s_negtri"])
        S.op("pool", lambda e: e.affine_select(out=negtri[:, :], in_=negtri[:, :], pattern=[[1, 128]], base=0, channel_multiplier=-1,
                                               compare_op=ALU.is_ge, fill=-30000.0), reads=["s_negtri"], writes=["s_negtri"])
        S.op("pool", lambda e: e.memset(ones_b[:, :], 1.0), writes=["s_onesb"])
        S.op("pool", lambda e: e.memset(h0[64:128, :, :], 0.0), writes=["s_h0"])

        def hkeys(name):
            return [(name, g) for g in range(4)]

        def load_state(sidx):
            import os
            if "LD_LIM" in os.environ:
                S.lim = int(os.environ["LD_LIM"])
                S.nrec = 0
            S.dma("sp", lambda e: e.dma_start(out=h0[0:64, :, :], in_=c.sssm[layer, sidx].rearrange("h p n -> p h n")),
                  writes=["s_h0"])
            for g in range(4):
                for j in range(8):
                    S.op("pe", lambda e, g=g, j=j: e.matmul(PAh[j // 4][:, j % 4, :], lhsT=h0[:, 8 * g + j, :], rhs=c.ident_f[:, :],
                                                            start=True, stop=True),
                         reads=["s_h0", "ident_f"], writes=["s_PA"])
                for hh in range(2):
                    S.op("act", lambda e, hh=hh: e.copy(out=PAs[:, 4 * hh:4 * hh + 4, :], in_=PAh[hh][:, :, :]),
                         reads=["s_PA"], writes=["s_PAs"])
                S.op("dve", lambda e, g=g: e.tensor_copy(out=hT[:, 8 * g:8 * g + 8, :], in_=PAs[:, :, 0:64]),
                     reads=["s_PAs"], writes=[("s_hT", g)])
                S.op("pool", lambda e, g=g: e.tensor_copy(out=hTb[:, 8 * g:8 * g + 8, :], in_=PAs[:, :, 0:64]),
                     reads=["s_PAs"], writes=[("s_hTb", g)])

        def store_state(dst):
            import os
            if "SS_LIM" in os.environ:
                S.lim = int(os.environ["SS_LIM"])
                S.nrec = 0
            for g in range(4):
                for j in range(8):
                    S.op("pe", lambda e, g=g, j=j: e.matmul(PAh[j // 4][0:64, j % 4, :], lhsT=hT[:, 8 * g + j, :], rhs=c.ident_f[:, :],
                                                            start=True, stop=True),
                         reads=[("s_hT", g), "ident_f"], writes=["s_PA"])
                for hh in range(2):
                    S.op("act", lambda e, g=g, hh=hh: e.copy(out=h0[0:64, 8 * g + 4 * hh:8 * g + 4 * hh + 4, :], in_=PAh[hh][0:64, :, :]),
                         reads=["s_PA"], writes=["s_h0"])
            S.dma("sp", lambda e: e.dma_start(out=dst.rearrange("h p n -> p h n"), in_=h0[0:64, :, :]),
                  reads=["s_h0"], writes=[("oh", id(dst))], is_output=True)

        ci = [0]

        def chunk(t0, L):
            b = ci[0] % 2
            ci[0] += 1
            S.dma("sp", lambda e: e.dma_start(out=x_tm[b][0:L, :], in_=c.xS[t0:t0 + L, :]), reads=tkeys("xS", t0, t0 + L), writes=[("s_x", b)])
            import os
            NOL = os.environ.get("SSD_NOLOAD", "")
            if "z" not in NOL:
                S.dma("sp", lambda e: e.dma_start(out=z_tm[b][0:L, :], in_=c.zS[t0:t0 + L, :]), reads=tkeys("zS", t0, t0 + L), writes=[("s_z", b)])
            if "B" not in NOL:
                S.dma("sp", lambda e: e.dma_start(out=B_tm[b][0:L, :], in_=c.Btm[t0:t0 + L, :]), reads=tkeys("Btm", t0, t0 + L), writes=[("s_B", b)])
            if "T" not in NOL:
                S.dma("sp", lambda e: e.dma_start(out=BTt[b][:, :, 0:L], in_=c.BT[:, t0:t0 + L].rearrange("(g n) t -> n g t", n=128)),
                      reads=tkeys("BT", t0, t0 + L), writes=[("s_BT", b)])
                S.dma("sp", lambda e: e.dma_start(out=CTt[b][:, :, 0:L], in_=c.CT[:, t0:t0 + L].rearrange("(g n) t -> n g t", n=128)),
                      reads=tkeys("CT", t0, t0 + L), writes=[("s_CT", b)])
            S.dma("sp", lambda e: e.dma_start(out=dtr[b][0:L, :], in_=c.dtS[t0:t0 + L, :]), reads=tkeys("dtS", t0, t0 + L), writes=[("s_dtr", b)])
            S.op("dve", lambda e: e.tensor_tensor(out=xb[0:L, :], in0=dtr[b][0:L, :], in1=c.dtb_rep[0:L, :], op=ALU.add),
                 reads=[("s_dtr", b), "lay_ssd"], writes=["s_xb"])
            S.op("dve", lambda e: e.scalar_tensor_tensor(out=ab[0:L, :], in0=xb[0:L, :], scalar=-1.0, in1=xb[0:L, :], op0=ALU.mult, op1=ALU.max),
                 reads=["s_xb"], writes=["s_ab"])
            S.op("act", lambda e: e.activation(out=ab[0:L, :], in_=ab[0:L, :], func=AF.Exp, scale=-1.0), reads=["s_ab"], writes=["s_ab"])
            S.op("act", lambda e: e.activation(out=ab[0:L, :], in_=ab[0:L, :], func=AF.Ln, bias=c.one_t[0:L, 0:1]),
                 reads=["s_ab", "consts"], writes=["s_ab"])
            S.op("dve", lambda e: e.scalar_tensor_tensor(out=dt[0:L, :], in0=xb[0:L, :], scalar=0.0, in1=ab[0:L, :], op0=ALU.max, op1=ALU.add),
                 reads=["s_xb", "s_ab"], writes=["s_dt"])
            S.op("dve", lambda e: e.tensor_tensor(out=la[0:L, :], in0=dt[0:L, :], in1=c.a_rep[0:L, :], op=ALU.mult),
                 reads=["s_dt", "lay_ssd"], writes=["s_la"])
            S.op("dve", lambda e: e.tensor_copy(out=las[0][0:L, :], in_=la[0:L, :]), reads=["s_la"], writes=["s_las"])
            S.op("dve", lambda e: e.tensor_tensor(out=lres[0:L, :], in0=la[0:L, :], in1=las[0][0:L, :], op=ALU.subtract),
                 reads=["s_la", "s_las"], writes=["s_lres"])
            S.op("dve", lambda e: e.tensor_copy(out=las[1][0:L, :], in_=lres[0:L, :]), reads=["s_lres"], writes=["s_las"])
            S.op("dve", lambda e: e.tensor_tensor(out=lres[0:L, :], in0=lres[0:L, :], in1=las[1][0:L, :], op=ALU.subtract),
                 reads=["s_lres", "s_las"], writes=["s_lres"])
            S.op("dve", lambda e: e.tensor_copy(out=las[2][0:L, :], in_=lres[0:L, :]), reads=["s_lres"], writes=["s_las"])
            S.op("pe", lambda e: e.matmul(p_acs[0:L, 0, 0:32], lhsT=utri[0:L, 0:L], rhs=la[0:L, :], start=True, stop=True),
                 reads=["s_utri", "s_la"], writes=["s_pacs"])
            S.op("act", lambda e: e.copy(out=acs[0:L, :], in_=p_acs[0:L, 0, 0:32]), reads=["s_pacs"], writes=["s_acs"])
            for g in range(4):
                hs = slice(8 * g, 8 * g + 8)
                cs_ = slice(512 * g, 512 * g + 512)
                for k3 in range(3):
                    S.op("dve", lambda e, hs=hs, k3=k3: e.tensor_tensor(out=Rm[k3][0:L, :, 0:L], in0=las[k3][0:L, hs].unsqueeze(2).broadcast_to([L, 8, L]),
                                                                      in1=utri[0:L, 0:L].unsqueeze(1).broadcast_to([L, 8, L]), op=ALU.mult),
                         reads=["s_las", "s_utri"], writes=[("s_Rm", k3)])
                for hh in range(2):
                    for k3 in range(3):
                        S.op("pe", lambda e, hh=hh, k3=k3: e.matmul(PAh[hh][:, :, 0:L], lhsT=ones_b[0:L, :], rhs=Rm[k3][0:L, 4 * hh:4 * hh + 4, 0:L],
                                                                    start=(k3 == 0), stop=(k3 == 2)),
                             reads=[("s_Rm", k3), "s_onesb"], writes=["s_PA"])
                for hh in range(2):
                    S.op("act", lambda e, hh=hh: e.activation(out=dA[:, 4 * hh:4 * hh + 4, 0:L], in_=PAh[hh][:, :, 0:L], func=AF.Exp),
                         reads=["s_PA"], writes=["s_dA"])
                    S.op("act", lambda e, hh=hh: e.copy(out=PAs[:, 4 * hh:4 * hh + 4, 0:L], in_=PAh[hh][:, :, 0:L]),
                         reads=["s_PA"], writes=["s_PAs"])
                    S.op("dve", lambda e, g=g, hh=hh: e.tensor_tensor(out=diff[0:L, 4 * hh:4 * hh + 4, 0:L], in0=PAs[0:L, 4 * hh:4 * hh + 4, 0:L],
                                                                    in1=acs[0:L, 8 * g + 4 * hh:8 * g + 4 * hh + 4].unsqueeze(2).broadcast_to([L, 4, L]),
                                                                    op=ALU.subtract),
                         reads=["s_PAs", "s_acs"], writes=["s_diff"])
                S.op("pool", lambda e: e.tensor_tensor(out=diff[0:L, :, 0:L], in0=diff[0:L, :, 0:L],
                                                       in1=negtri[0:L, 0:L].unsqueeze(1).broadcast_to([L, 8, L]), op=ALU.add),
                     reads=["s_diff", "s_negtri"], writes=["s_diff"])
                S.op("act", lambda e: e.activation(out=diff[0:L, :, 0:L], in_=diff[0:L, :, 0:L], func=AF.Exp), reads=["s_diff"], writes=["s_diff"])
                S.op("pe", lambda e, g=g: e.matmul(CBT[0:L, 0:L], lhsT=BTt[b][:, g, 0:L], rhs=CTt[b][:, g, 0:L], start=True, stop=True),
                     reads=[("s_BT", b), ("s_CT", b)], writes=["s_CBT"])
                S.op("act", lambda e: e.copy(out=CBs[0:L, 0:L], in_=CBT[0:L, 0:L]), reads=["s_CBT"], writes=["s_CBs"])
                S.op("dve", lambda e: e.tensor_tensor(out=CBL[0:L, :, 0:L], in0=diff[0:L, :, 0:L],
                                                      in1=CBs[0:L, 0:L].unsqueeze(1).broadcast_to([L, 8, L]), op=ALU.mult),
                     reads=["s_diff", "s_CBs"], writes=["s_CBL"])
                S.op("pool", lambda e, g=g: e.tensor_tensor(out=Cdec[:, :, 0:L], in0=dA[:, :, 0:L],
                                                            in1=CTt[b][:, g, 0:L].unsqueeze(1).broadcast_to([128, 8, L]), op=ALU.mult),
                     reads=["s_dA", ("s_CT", b)], writes=["s_Cdec"])
                S.op("dve", lambda e, hs=hs, cs_=cs_: e.tensor_tensor(out=xdt[0:L, :, :], in0=x_tm[b][0:L, cs_].rearrange("p (j d) -> p j d", d=64),
                                                                    in1=dt[0:L, hs].unsqueeze(2).broadcast_to([L, 8, 64]), op=ALU.mult),
                     reads=[("s_x", b), "s_dt"], writes=["s_xdt"])
                for j in range(8):
                    S.op("pe", lambda e, j=j: e.matmul(py[0:L, 64 * j:64 * j + 64], lhsT=CBL[0:L, j, 0:L], rhs=xdt[0:L, j, :], start=True, stop=False),
                         reads=["s_CBL", "s_xdt"], writes=["s_py"])
                    S.op("pe", lambda e, j=j, g=g: e.matmul(py[0:L, 64 * j:64 * j + 64], lhsT=Cdec[:, j, 0:L], rhs=hTb[:, 8 * g + j, :], start=False, stop=True),
                         reads=["s_Cdec", ("s_hTb", g)], writes=["s_py"])
                S.op("pool", lambda e, hs=hs, cs_=cs_: e.tensor_tensor(out=t1[0:L, :].rearrange("p (j d) -> p j d", d=64),
                                                                     in0=x_tm[b][0:L, cs_].rearrange("p (j d) -> p j d", d=64),
                                                                     in1=c.dsk_rep[0:L, hs].unsqueeze(2).broadcast_to([L, 8, 64]), op=ALU.mult),
                     reads=[("s_x", b), "lay_ssd"], writes=["s_t1"])
                S.op("dve", lambda e: e.tensor_tensor(out=y1[0:L, :], in0=t1[0:L, :], in1=py[0:L, :], op=ALU.add),
                     reads=["s_t1", "s_py"], writes=["s_y1"])
                S.op("act", lambda e, cs_=cs_: e.activation(out=sz[0:L, :], in_=z_tm[b][0:L, cs_], func=AF.Silu), reads=[("s_z", b)], writes=["s_sz"])
                S.op("dve", lambda e: e.tensor_tensor(out=y1[0:L, :], in0=y1[0:L, :], in1=sz[0:L, :], op=ALU.mult),
                     reads=["s_y1", "s_sz"], writes=["s_y1"])
                S.op("act", lambda e: e.activation(out=jk[0:L, :], in_=y1[0:L, :], func=AF.Square, accum_out=ss[0:L, 0:1]),
                     reads=["s_y1"], writes=["s_jk", "s_ss"])
                S.op("act", lambda e: e.activation(out=ss[0:L, :], in_=ss[0:L, :], func=AF.Sqrt, scale=1.0 / 512, bias=c.eps_t[0:L, 0:1]),
                     reads=["s_ss", "consts"], writes=["s_ss"])
                S.op("dve", lambda e: e.reciprocal(out=ss[0:L, :], in_=ss[0:L, :]), reads=["s_ss"], writes=["s_ss"])
                S.op("dve", lambda e, cs_=cs_: e.scalar_tensor_tensor(out=y3[0:L, :], in0=y1[0:L, :], scalar=ss[0:L, 0:1], in1=c.normg_rep[0:L, cs_],
                                                                    op0=ALU.mult, op1=ALU.mult),
                     reads=["s_y1", "s_ss", "lay_ssd"], writes=["s_y3"])
                yb = (ci[0] * 4 + g) % 2
                for k in range(4):
                    S.op("pe", lambda e, k=k: e.transpose(pT[:, k, 0:L], y3[0:L, 128 * k:128 * k + 128], c.ident_b[0:L, 0:L]),
                         reads=["s_y3", "ident_b"], writes=["s_pT"])
                S.op("act", lambda e, yb=yb: e.copy(out=yT[yb][:, :, 0:L], in_=pT[:, 0:4, 0:L]), reads=["s_pT"], writes=[("s_yT", yb)])
                S.dma("sp", lambda e, g=g, yb=yb: e.dma_start(out=c.ynT[512 * g:512 * g + 512, t0:t0 + L].rearrange("(k p) t -> p k t", p=128),
                                                             in_=yT[yb][:, :, 0:L]),
                      reads=[("s_yT", yb)], writes=tkeys("ynT", t0, t0 + L))
                for hh in range(2):
                    S.op("dve", lambda e, g=g, hh=hh: e.tensor_tensor(out=dte[0:L, 4 * hh:4 * hh + 4], in0=PAs[0:L, 4 * hh:4 * hh + 4, L - 1],
                                                                    in1=acs[0:L, 8 * g + 4 * hh:8 * g + 4 * hh + 4], op=ALU.subtract),
                         reads=["s_PAs", "s_acs"], writes=["s_dte"])
                S.op("act", lambda e: e.activation(out=dte[0:L, :], in_=dte[0:L, :], func=AF.Exp), reads=["s_dte"], writes=["s_dte"])
                S.op("dve", lambda e: e.tensor_tensor(out=xdec[0:L, :, :], in0=xdt[0:L, :, :],
                                                      in1=dte[0:L, :].unsqueeze(2).broadcast_to([L, 8, 64]), op=ALU.mult),
                     reads=["s_xdt", "s_dte"], writes=["s_xdec"])
                S.op("pe", lambda e, g=g: e.matmul(pst[:, :], lhsT=B_tm[b][0:L, 128 * g:128 * g + 128],
                                                   rhs=xdec[0:L, :, :].rearrange("p j d -> p (j d)"), start=True, stop=True),
                     reads=[("s_B", b), "s_xdec"], writes=["s_pst"])
                S.op("dve", lambda e, hs=hs: e.tensor_tensor(out=hT[:, hs, :], in0=hT[:, hs, :],
                                                           in1=dA[:, :, L - 1:L].broadcast_to([128, 8, 64]), op=ALU.mult),
                     reads=[("s_hT", g), "s_dA"], writes=[("s_hT", g)])
                S.op("dve", lambda e, hs=hs: e.tensor_tensor(out=hT[:, hs, :], in0=hT[:, hs, :],
                                                           in1=pst[:, :].rearrange("p (j d) -> p j d", d=64), op=ALU.add),
                     reads=[("s_hT", g), "s_pst"], writes=[("s_hT", g)])
                S.op("pool", lambda e, hs=hs: e.tensor_copy(out=hTb[:, hs, :], in_=hT[:, hs, :]),
                     reads=[("s_hT", g)], writes=[("s_hTb", g)])

        S.op("pool", lambda e: e.memset(hT[:, :, :], 0.0), writes=hkeys("s_hT"))
        S.op("pool", lambda e: e.memset(hTb[:, :, :], 0.0), writes=hkeys("s_hTb"))
        import os
        MODE = int(os.environ.get("SSD_MODE", "9"))
        if "SSD_LIM" in os.environ:
            S.lim = int(os.environ["SSD_LIM"])
            S.nrec = 0
        for t0 in range(0, SEQ, 128):
            if MODE >= 1:
                chunk(t0, 128)
        if MODE >= 2:
            store_state(c.oh_p[layer])
        for s_ in range(NSEQ):
            if MODE >= 3:
                load_state(s_)
            if MODE >= 4:
                chunk(SEQ + 8 * s_, 8)
            if MODE >= 5:
                store_state(c.oh_s[layer, s_])
        S.lim = None


def phase_dsa_prompt(c, layer):
    nc, S = c.nc, c.S
    S.barrier()
    SEQ = c.SEQ
    TOPK = min(256, SEQ // 4)
    NIT = 26
    NKT = SEQ // 128
    with ExitStack() as st:
        kiT = sb(nc, st, "d_kiT", [64, SEQ], BF16)
        kTa = sb(nc, st, "d_kT", [64, 4, SEQ], BF16)
        Va = sb(nc, st, "d_V", [128, NKT, 4, 65], BF16)
        sc = sb(nc, st, "d_sc", [128, SEQ], F32)
        mneg = sb(nc, st, "d_mneg", [128, SEQ], BF16)
        qi8 = sb(nc, st, "d_qi8", [64, 8, 128], BF16)
        wi = sb(nc, st, "d_wi", [128, 8], F32)
        rl = [sb(nc, st, f"d_rl{i}", [128, 512], F32) for i in range(2)]
        sm = sb(nc, st, "d_sm", [128, 8], F32)
        QT4 = [sb(nc, st, f"d_QT4{i}", [64, 4, 128], BF16) for i in range(2)]
        PT = [sb(nc, st, f"d_PT{i}", [128, 512], BF16) for i in range(2)]
        I4 = sb(nc, st, "d_I4", [128, 4, 128], BF16)
        rc = sb(nc, st, "d_rc", [128, 512], F32)
        bcs = sb(nc, st, "d_bcs", [64, 512], F32)
        OT = [sb(nc, st, f"d_OT{i}", [64, 512], BF16) for i in range(2)]
        p_i = [ps(nc, st, f"d_pi{i}", [128, 512], F32) for i in range(2)]
        p_s = [ps(nc, st, f"d_ps{i}", [128, 512], F32) for i in range(2)]
        p_o = [ps(nc, st, f"d_po{i}", [128, 512], F32) for i in range(2)]
        p_b = ps(nc, st, "d_pb", [128, 512], F32)

        S.dma("sp", lambda e: e.dma_start(out=kiT[:, :], in_=c.kiT[0, :, 0:SEQ]), reads=tkeys("kiT", 0, SEQ), writes=["d_kiT"])
        for kvh in range(4):
            S.dma("sp", lambda e, kvh=kvh: e.dma_start(out=kTa[:, kvh, :], in_=c.kT[kvh, :, 0:SEQ]), reads=tkeys("kT", 0, SEQ), writes=["d_kT"])
        S.op("pool", lambda e: e.memset(Va[:, :, :, 64:65], 1.0), writes=["d_V"])
        for kt in range(NKT):
            S.dma("sp", lambda e, kt=kt: e.dma_start(out=Va[:, kt, :, 0:64], in_=c.vB[kt * 128:(kt + 1) * 128, :].rearrange("p (h d) -> p h d", d=64)),
                  reads=tkeys("vB", kt * 128, kt * 128 + 128), writes=["d_V"])
        for g in range(4):
            S.op("dve", lambda e, g=g: e.tensor_copy(out=I4[:, g, :], in_=c.ident_b[:, :]), reads=["ident_b"], writes=["d_I4"])

        it = 0
        fillreg = {}
        for qt in range(NKT):
            t0 = qt * 128
            Sk = (qt + 1) * 128
            S.dma("sp", lambda e, t0=t0: e.dma_start(out=qi8[:, :, :], in_=c.qiT[:, :, t0:t0 + 128].rearrange("h d t -> d h t")),
                  reads=tkeys("qiT", t0, t0 + 128), writes=["d_qi8"])
            S.dma("sp", lambda e, t0=t0: e.dma_start(out=wi[:, :], in_=c.wiS[t0:t0 + 128, :]), reads=tkeys("wiS", t0, t0 + 128), writes=["d_wi"])
            for kb in range((Sk + 511) // 512):
                w = min(512, Sk - kb * 512)
                k0 = kb * 512
                for hh in range(8):
                    pb_ = it % 2
                    it += 1
                    S.op("pe", lambda e, hh=hh, k0=k0, w=w, pb_=pb_: e.matmul(p_i[pb_][:, 0:w], lhsT=qi8[:, hh, :], rhs=kiT[:, k0:k0 + w],
                                                                             start=True, stop=True),
                         reads=["d_qi8", "d_kiT"], writes=[("d_pi", pb_)])
                    S.op("act", lambda e, w=w, pb_=pb_: e.activation(out=rl[pb_][:, 0:w], in_=p_i[pb_][:, 0:w], func=AF.Relu),
                         reads=[("d_pi", pb_)], writes=[("d_rl", pb_)])
                    if hh == 0:
                        S.op("dve", lambda e, k0=k0, w=w, pb_=pb_: e.tensor_scalar(out=sc[:, k0:k0 + w], in0=rl[pb_][:, 0:w], scalar1=wi[:, 0:1],
                                                                                 scalar2=None, op0=ALU.mult),
                             reads=[("d_rl", pb_), "d_wi"], writes=["d_sc"])
                    else:
                        S.op("dve", lambda e, hh=hh, k0=k0, w=w, pb_=pb_: e.scalar_tensor_tensor(
                            out=sc[:, k0:k0 + w], in0=rl[pb_][:, 0:w], scalar=wi[:, hh:hh + 1], in1=sc[:, k0:k0 + w], op0=ALU.mult, op1=ALU.add),
                            reads=[("d_rl", pb_), "d_wi", "d_sc"], writes=["d_sc"])
            S.op("dve", lambda e, Sk=Sk: e.tensor_reduce(out=sm[:, 5:6], in_=sc[:, 0:Sk], axis=AX.X, op=ALU.max), reads=["d_sc"], writes=["d_sm"])
            S.op("dve", lambda e, Sk=Sk: e.tensor_reduce(out=sm[:, 0:1], in_=sc[:, 0:Sk], axis=AX.X, op=ALU.min), reads=["d_sc"], writes=["d_sm"])
            S.op("dve", lambda e: e.tensor_tensor(out=sm[:, 1:2], in0=sm[:, 5:6], in1=sm[:, 0:1], op=ALU.subtract), reads=["d_sm"], writes=["d_sm"])
            def _asel(e, Sk=Sk):
                if "r" not in fillreg:
                    fillreg["r"] = e.to_reg(-1e30)
                return e.affine_select(out=sc[:, Sk - 128:Sk], in_=sc[:, Sk - 128:Sk], pattern=[[-1, 128]], base=0,
                                       channel_multiplier=1, compare_op=ALU.is_ge, fill=fillreg["r"])
            S.op("pool", _asel,
                 reads=["d_sc", "d_sm"], writes=["d_sc"])
            for i_ in range(NIT):
                ci_ = 0.5 ** (i_ + 1)
                S.op("dve", lambda e, ci_=ci_: e.scalar_tensor_tensor(out=sm[:, 2:3], in0=sm[:, 1:2], scalar=ci_, in1=sm[:, 0:1],
                                                                      op0=ALU.mult, op1=ALU.add), reads=["d_sm"], writes=["d_sm"])
                S.op("dve", lambda e, Sk=Sk: e.tensor_scalar(out=mneg[:, 0:Sk], in0=sc[:, 0:Sk], scalar1=sm[:, 2:3], scalar2=None,
                                                           op0=ALU.is_ge, op1=ALU.add, accum_out=sm[:, 3:4]),
                     reads=["d_sc", "d_sm"], writes=["d_mneg", "d_sm"])
                S.op("dve", lambda e: e.scalar_tensor_tensor(out=sm[:, 4:5], in0=sm[:, 3:4], scalar=float(TOPK), in1=sm[:, 1:2],
                                                             op0=ALU.is_ge, op1=ALU.mult), reads=["d_sm"], writes=["d_sm"])
                S.op("dve", lambda e, ci_=ci_: e.scalar_tensor_tensor(out=sm[:, 0:1], in0=sm[:, 4:5], scalar=ci_, in1=sm[:, 0:1],
                                                                      op0=ALU.mult, op1=ALU.add), reads=["d_sm"], writes=["d_sm"])
            S.op("dve", lambda e: e.tensor_scalar(out=sm[:, 6:7], in0=sm[:, 0:1], scalar1=-1e29, scalar2=None, op0=ALU.max),
                 reads=["d_sm"], writes=["d_sm"])
            S.op("dve", lambda e, Sk=Sk: e.tensor_scalar(out=mneg[:, 0:Sk], in0=sc[:, 0:Sk], scalar1=sm[:, 6:7], scalar2=-30000.0,
                                                       op0=ALU.is_lt, op1=ALU.mult), reads=["d_sc", "d_sm"], writes=["d_mneg"])
            for kvh in range(4):
                qb = (qt * 4 + kvh) % 2
                S.dma("sp", lambda e, kvh=kvh, t0=t0, qb=qb: e.dma_start(out=QT4[qb][:, :, :],
                                                                        in_=c.qT[4 * kvh:4 * kvh + 4, :, t0:t0 + 128].rearrange("h d t -> d h t")),
                      reads=tkeys("qT", t0, t0 + 128), writes=[("d_QT4", qb)])
                for kt in range(qt + 1):
                    pb_ = it % 2
                    it += 1
                    S.op("pe", lambda e, kvh=kvh, kt=kt, qb=qb, pb_=pb_: e.matmul(p_s[pb_][:, :], lhsT=kTa[:, kvh, kt * 128:(kt + 1) * 128],
                                                                                 rhs=QT4[qb][:, :, :].rearrange("d h t -> d (h t)"),
                                                                                 start=True, stop=False),
                         reads=["d_kT", ("d_QT4", qb)], writes=[("d_ps", pb_)])
                    S.op("pe", lambda e, kt=kt, pb_=pb_: e.matmul(p_s[pb_][:, :], lhsT=mneg[:, kt * 128:(kt + 1) * 128],
                                                                 rhs=I4[:, :, :].rearrange("p h t -> p (h t)"), start=False, stop=True),
                         reads=["d_mneg", "d_I4"], writes=[("d_ps", pb_)])
                    S.op("act", lambda e, pb_=pb_: e.activation(out=PT[pb_][:, :], in_=p_s[pb_][:, :], func=AF.Exp),
                         reads=[("d_ps", pb_)], writes=[("d_PT", pb_)])
                    S.op("pe", lambda e, kvh=kvh, kt=kt, qb=qb, pb_=pb_, qt=qt: e.matmul(p_o[qb][0:65, :], lhsT=Va[:, kt, kvh, :], rhs=PT[pb_][:, :],
                                                                                        start=(kt == 0), stop=(kt == qt)),
                         reads=["d_V", ("d_PT", pb_)], writes=[("d_po", qb)])
                S.op("dve", lambda e, qb=qb: e.reciprocal(out=rc[64:65, :], in_=p_o[qb][64:65, :]), reads=[("d_po", qb)], writes=["d_rc"])
                S.op("pe", lambda e: e.matmul(p_b[0:64, :], lhsT=c.ones_f[64:65, 0:64], rhs=rc[64:65, :], start=True, stop=True),
                     reads=["d_rc", "ones_f"], writes=["d_pb"])
                S.op("act", lambda e: e.copy(out=bcs[:, :], in_=p_b[0:64, :]), reads=["d_pb"], writes=["d_bcs"])
                S.op("dve", lambda e, qb=qb: e.tensor_tensor(out=OT[qb][:, :], in0=p_o[qb][0:64, :], in1=bcs[:, :], op=ALU.mult),
                     reads=[("d_po", qb), "d_bcs"], writes=[("d_OT", qb)])
                S.dma("sp", lambda e, kvh=kvh, t0=t0, qb=qb: e.dma_start(
                    out=c.yaT[256 * kvh:256 * kvh + 256, t0:t0 + 128].rearrange("(g d) t -> d g t", d=64),
                    in_=OT[qb][:, :].rearrange("d (g t) -> d g t", t=128)),
                    reads=[("d_OT", qb)], writes=tkeys("yaT", t0, t0 + 128))


def phase_dsa_sample(c, layer):
    nc, S = c.nc, c.S
    S.barrier()
    SEQ, NSEQ, NPG = c.SEQ, c.NSEQ, c.NPAGES
    PAST = NPG * 128
    LK = PAST + 8
    TOPK = min(256, LK // 4)
    NIT = 26
    fillreg = {}
    with ExitStack() as st:
        pt_i = sb(nc, st, "e_pt", [128, 1], I32)
        idxq = [sb(nc, st, f"e_idxq{i}", [128, 1], I32) for i in range(4)]
        sc = sb(nc, st, "e_sc", [8, LK], F32)
        mneg = sb(nc, st, "e_mneg", [8, LK], BF16)
        qi8 = sb(nc, st, "e_qi8", [64, 8, 8], BF16)
        kin = sb(nc, st, "e_kin", [64, 8], BF16)
        wi = sb(nc, st, "e_wi", [8, 8], F32)
        rl = [sb(nc, st, f"e_rl{i}", [8, 512], F32) for i in range(2)]
        sm = sb(nc, st, "e_sm", [8, 8], F32)
        I4s = sb(nc, st, "e_I4s", [8, 4, 8], BF16)
        QT4 = sb(nc, st, "e_QT4", [64, 16, 8], BF16)
        kTn = sb(nc, st, "e_kTn", [64, 4, 8], BF16)
        Vn = sb(nc, st, "e_Vn", [8, 4, 65], BF16)
        kTj = [sb(nc, st, f"e_kTj{i}", [64, 4, 128], BF16) for i in range(2)]
        PT = [sb(nc, st, f"e_PT{i}", [128, 32], BF16) for i in range(2)]
        rc = sb(nc, st, "e_rc", [128, 32], F32)
        bcs = sb(nc, st, "e_bcs", [64, 32], F32)
        OT = sb(nc, st, "e_OT", [64, 4, 32], BF16)
        p_t = [ps(nc, st, f"e_pt{i}", [128, 512], F32) for i in range(2)]
        p_s = [ps(nc, st, f"e_ps{i}", [128, 512], F32) for i in range(2)]
        p_o = [ps(nc, st, f"e_po{i}", [128, 512], F32) for i in range(4)]
        for g in range(4):
            S.op("dve", lambda e, g=g: e.tensor_copy(out=I4s[:, g, :], in_=c.ident_b[0:8, 0:8]), reads=["ident_b"], writes=["e_I4s"])
        it = 0
        for si in range(NSEQ):
            t0 = SEQ + 8 * si
            S.dma("sp", lambda e, si=si: e.dma_start(out=pt_i[0:NPG, :], in_=c.ptab[si].rearrange("(p o) -> p o", o=1)), writes=["e_pt"])
            for q in range(4):
                S.op("dve", lambda e, q=q: e.tensor_scalar(out=idxq[q][0:NPG, :], in0=pt_i[0:NPG, :], scalar1=4.0, scalar2=float(q),
                                                         op0=ALU.mult, op1=ALU.add), reads=["e_pt"], writes=[("e_idxq", q)])
            S.dma("sp", lambda e, t0=t0: e.dma_start(out=qi8[:, :, :], in_=c.qiT[:, :, t0:t0 + 8].rearrange("h d t -> d h t")),
                  reads=tkeys("qiT", t0, t0 + 8), writes=["e_qi8"])
            S.dma("sp", lambda e, t0=t0: e.dma_start(out=kin[:, :], in_=c.kiT[0, :, t0:t0 + 8]), reads=tkeys("kiT", t0, t0 + 8), writes=["e_kin"])
            S.dma("sp", lambda e, t0=t0: e.dma_start(out=wi[:, :], in_=c.wiS[t0:t0 + 8, :]), reads=tkeys("wiS", t0, t0 + 8), writes=["e_wi"])
            S.dma("sp", lambda e, t0=t0: e.dma_start(out=QT4[:, :, :], in_=c.qT[:, :, t0:t0 + 8].rearrange("h d t -> d h t")),
                  reads=tkeys("qT", t0, t0 + 8), writes=["e_QT4"])
            S.dma("sp", lambda e, t0=t0: e.dma_start(out=kTn[:, :, :], in_=c.kT[:, :, t0:t0 + 8].rearrange("h d t -> d h t")),
                  reads=tkeys("kT", t0, t0 + 8), writes=["e_kTn"])
            S.op("pool", lambda e: e.memset(Vn[:, :, 64:65], 1.0), writes=["e_Vn"])
            S.dma("sp", lambda e, t0=t0: e.dma_start(out=Vn[:, :, 0:64], in_=c.vB[t0:t0 + 8, :].rearrange("p (h d) -> p h d", d=64)),
                  reads=tkeys("vB", t0, t0 + 8), writes=["e_Vn"])

            def accum(pb_, hh, cols):
                if hh == 0:
                    S.op("dve", lambda e: e.tensor_scalar(out=sc[:, cols], in0=rl[pb_][:, 0:cols.stop - cols.start], scalar1=wi[:, 0:1],
                                                          scalar2=None, op0=ALU.mult), reads=[("e_rl", pb_), "e_wi"], writes=["e_sc"])
                else:
                    S.op("dve", lambda e: e.scalar_tensor_tensor(out=sc[:, cols], in0=rl[pb_][:, 0:cols.stop - cols.start], scalar=wi[:, hh:hh + 1],
                                                                 in1=sc[:, cols], op0=ALU.mult, op1=ALU.add),
                         reads=[("e_rl", pb_), "e_wi", "e_sc"], writes=["e_sc"])

            with ExitStack() as st2:
                KI = sb(nc, st2, "e_KI", [128, 8192], F32)
                kiTs = sb(nc, st2, "e_kiTs", [64, 128, 128], BF16)
                S.dma("pool", lambda e: e.indirect_dma_start(out=KI[0:NPG, :], out_offset=None, in_=c.cache_i[layer],
                                                             in_offset=bass.IndirectOffsetOnAxis(ap=pt_i[0:NPG, 0:1], axis=0)),
                      reads=["e_pt"], writes=["e_KI"])
                for jb in range(32):
                    pb_ = it % 2
                    it += 1
                    for jj in range(4):
                        j = 4 * jb + jj
                        S.op("pe", lambda e, j=j, jj=jj, pb_=pb_: e.matmul(p_t[pb_][0:64, jj * 128:jj * 128 + NPG], lhsT=KI[0:NPG, j * 64:(j + 1) * 64],
                                                                          rhs=c.ident_f[0:NPG, 0:NPG], start=True, stop=True),
                             reads=["e_KI", "ident_f"], writes=[("e_pt", pb_)])
                    S.op("act", lambda e, jb=jb, pb_=pb_: e.copy(out=kiTs[:, 4 * jb:4 * jb + 4, 0:NPG],
                                                                in_=p_t[pb_][0:64, :].rearrange("p (a b) -> p a b", b=128)[:, :, 0:NPG]),
                         reads=[("e_pt", pb_)], writes=["e_kiTs"])
                for jb in range(32):
                    for hh in range(8):
                        pb_ = it % 2
                        it += 1
                        for jj in range(4):
                            S.op("pe", lambda e, hh=hh, jb=jb, jj=jj, pb_=pb_: e.matmul(p_s[pb_][0:8, jj * NPG:(jj + 1) * NPG], lhsT=qi8[:, hh, :],
                                                                                       rhs=kiTs[:, 4 * jb + jj, 0:NPG], start=True, stop=True),
                                 reads=["e_qi8", "e_kiTs"], writes=[("e_ps", pb_)])
                        S.op("act", lambda e, pb_=pb_: e.activation(out=rl[pb_][:, 0:4 * NPG], in_=p_s[pb_][0:8, 0:4 * NPG], func=AF.Relu),
                             reads=[("e_ps", pb_)], writes=[("e_rl", pb_)])
                        accum(pb_, hh, slice(4 * jb * NPG, 4 * (jb + 1) * NPG))
            for hh in range(8):
                pb_ = it % 2
                it += 1
                S.op("pe", lambda e, hh=hh, pb_=pb_: e.matmul(p_s[pb_][0:8, 0:8], lhsT=qi8[:, hh, :], rhs=kin[:, :], start=True, stop=True),
                     reads=["e_qi8", "e_kin"], writes=[("e_ps", pb_)])
                S.op("act", lambda e, pb_=pb_: e.activation(out=rl[pb_][:, 0:8], in_=p_s[pb_][0:8, 0:8], func=AF.Relu),
                     reads=[("e_ps", pb_)], writes=[("e_rl", pb_)])
                accum(pb_, hh, slice(PAST, PAST + 8))
            S.op("dve", lambda e: e.tensor_reduce(out=sm[:, 5:6], in_=sc[:, :], axis=AX.X, op=ALU.max), reads=["e_sc"], writes=["e_sm"])
            S.op("dve", lambda e: e.tensor_reduce(out=sm[:, 0:1], in_=sc[:, :], axis=AX.X, op=ALU.min), reads=["e_sc"], writes=["e_sm"])
            S.op("dve", lambda e: e.tensor_tensor(out=sm[:, 1:2], in0=sm[:, 5:6], in1=sm[:, 0:1], op=ALU.subtract), reads=["e_sm"], writes=["e_sm"])

            def _asel(e):
                if "r" not in fillreg:
                    fillreg["r"] = e.to_reg(-1e30)
                return e.affine_select(out=sc[:, PAST:PAST + 8], in_=sc[:, PAST:PAST + 8], pattern=[[-1, 8]], base=0,
                                       channel_multiplier=1, compare_op=ALU.is_ge, fill=fillreg["r"])
            S.op("pool", _asel, reads=["e_sc", "e_sm"], writes=["e_sc"])
            for i_ in range(NIT):
                ci_ = 0.5 ** (i_ + 1)
                S.op("dve", lambda e, ci_=ci_: e.scalar_tensor_tensor(out=sm[:, 2:3], in0=sm[:, 1:2], scalar=ci_, in1=sm[:, 0:1],
                                                                      op0=ALU.mult, op1=ALU.add), reads=["e_sm"], writes=["e_sm"])
                S.op("dve", lambda e: e.tensor_scalar(out=mneg[:, :], in0=sc[:, :], scalar1=sm[:, 2:3], scalar2=None,
                                                      op0=ALU.is_ge, op1=ALU.add, accum_out=sm[:, 3:4]),
                     reads=["e_sc", "e_sm"], writes=["e_mneg", "e_sm"])
                S.op("dve", lambda e: e.scalar_tensor_tensor(out=sm[:, 4:5], in0=sm[:, 3:4], scalar=float(TOPK), in1=sm[:, 1:2],
                                                             op0=ALU.is_ge, op1=ALU.mult), reads=["e_sm"], writes=["e_sm"])
                S.op("dve", lambda e, ci_=ci_: e.scalar_tensor_tensor(out=sm[:, 0:1], in0=sm[:, 4:5], scalar=ci_, in1=sm[:, 0:1],
                                                                      op0=ALU.mult, op1=ALU.add), reads=["e_sm"], writes=["e_sm"])
            S.op("dve", lambda e: e.tensor_scalar(out=sm[:, 6:7], in0=sm[:, 0:1], scalar1=-1e29, scalar2=None, op0=ALU.max),
                 reads=["e_sm"], writes=["e_sm"])
            S.op("dve", lambda e: e.tensor_scalar(out=mneg[:, :], in0=sc[:, :], scalar1=sm[:, 6:7], scalar2=-30000.0,
                                                  op0=ALU.is_lt, op1=ALU.mult), reads=["e_sc", "e_sm"], writes=["e_mneg"])
            with ExitStack() as st3:
                Kq = sb(nc, st3, "e_Kq", [128, 32, 256], F32)
                Vq = sb(nc, st3, "e_Vq", [128, 32, 256], F32)
                Vb = sb(nc, st3, "e_Vb", [128, 32, 4, 65], BF16)
                S.op("pool", lambda e: e.memset(Vb[:, :, :, 64:65], 1.0), writes=["e_Vb"])
                first = [True] * 4
                for q in range(4):
                    S.dma("pool", lambda e, q=q: e.indirect_dma_start(out=Kq[0:NPG, :, :].rearrange("p a b -> p (a b)"), out_offset=None,
                                                                     in_=c.cache_k[layer],
                                                                     in_offset=bass.IndirectOffsetOnAxis(ap=idxq[q][0:NPG, 0:1], axis=0)),
                          reads=[("e_idxq", q)], writes=["e_Kq"])
                    S.dma("pool", lambda e, q=q: e.indirect_dma_start(out=Vq[0:NPG, :, :].rearrange("p a b -> p (a b)"), out_offset=None,
                                                                     in_=c.cache_v[layer],
                                                                     in_offset=bass.IndirectOffsetOnAxis(ap=idxq[q][0:NPG, 0:1], axis=0)),
                          reads=[("e_idxq", q)], writes=["e_Vq"])
                    S.op("dve", lambda e: e.tensor_copy(out=Vb[0:NPG, :, :, 0:64], in_=Vq[0:NPG, :, :].rearrange("p a (h d) -> p a h d", d=64)),
                         reads=["e_Vq"], writes=["e_Vb"])
                    for kvh in range(4):
                        for jb in range(8):
                            tb = it % 2
                            it += 1
                            for jj in range(4):
                                jl = 4 * jb + jj
                                S.op("pe", lambda e, jl=jl, jj=jj, kvh=kvh, tb=tb: e.matmul(
                                    p_t[tb][0:64, jj * 128:jj * 128 + NPG], lhsT=Kq[0:NPG, jl, kvh * 64:(kvh + 1) * 64],
                                    rhs=c.ident_f[0:NPG, 0:NPG], start=True, stop=True),
                                    reads=["e_Kq", "ident_f"], writes=[("e_pt", tb)])
                            S.op("act", lambda e, tb=tb: e.copy(out=kTj[tb][:, :, 0:NPG],
                                                                in_=p_t[tb][0:64, :].rearrange("p (a b) -> p a b", b=128)[:, :, 0:NPG]),
                                 reads=[("e_pt", tb)], writes=[("e_kTj", tb)])
                            for jj in range(4):
                                jl = 4 * jb + jj
                                jg = 32 * q + jl
                                pb_ = it % 2
                                it += 1
                                S.op("pe", lambda e, jj=jj, kvh=kvh, tb=tb, pb_=pb_: e.matmul(
                                    p_s[pb_][0:NPG, 0:32], lhsT=kTj[tb][:, jj, 0:NPG], rhs=QT4[:, 4 * kvh:4 * kvh + 4, :].rearrange("d h t -> d (h t)"),
                                    start=True, stop=False), reads=[("e_kTj", tb), "e_QT4"], writes=[("e_ps", pb_)])
                                S.op("pe", lambda e, jg=jg, pb_=pb_: e.matmul(
                                    p_s[pb_][0:NPG, 0:32], lhsT=mneg[0:8, jg * NPG:(jg + 1) * NPG], rhs=I4s[:, :, :].rearrange("p h t -> p (h t)"),
                                    start=False, stop=True), reads=["e_mneg", "e_I4s"], writes=[("e_ps", pb_)])
                                S.op("act", lambda e, pb_=pb_: e.activation(out=PT[pb_][0:NPG, :], in_=p_s[pb_][0:NPG, 0:32], func=AF.Exp),
                                     reads=[("e_ps", pb_)], writes=[("e_PT", pb_)])
                                S.op("pe", lambda e, jl=jl, kvh=kvh, pb_=pb_, fst=first[kvh]: e.matmul(
                                    p_o[kvh][0:65, 0:32], lhsT=Vb[0:NPG, jl, kvh, :], rhs=PT[pb_][0:NPG, :], start=fst, stop=False),
                                    reads=["e_Vb", ("e_PT", pb_)], writes=[("e_po", kvh)])
                                first[kvh] = False
                for kvh in range(4):
                    pb_ = it % 2
                    it += 1
                    S.op("pe", lambda e, kvh=kvh, pb_=pb_: e.matmul(p_s[pb_][0:8, 0:32], lhsT=kTn[:, kvh, :],
                                                                   rhs=QT4[:, 4 * kvh:4 * kvh + 4, :].rearrange("d h t -> d (h t)"), start=True, stop=False),
                         reads=["e_kTn", "e_QT4"], writes=[("e_ps", pb_)])
                    S.op("pe", lambda e, pb_=pb_: e.matmul(p_s[pb_][0:8, 0:32], lhsT=mneg[0:8, PAST:PAST + 8],
                                                          rhs=I4s[:, :, :].rearrange("p h t -> p (h t)"), start=False, stop=True),
                         reads=["e_mneg", "e_I4s"], writes=[("e_ps", pb_)])
                    S.op("act", lambda e, pb_=pb_: e.activation(out=PT[pb_][0:8, :], in_=p_s[pb_][0:8, 0:32], func=AF.Exp),
                         reads=[("e_ps", pb_)], writes=[("e_PT", pb_)])
                    S.op("pe", lambda e, kvh=kvh, pb_=pb_: e.matmul(p_o[kvh][0:65, 0:32], lhsT=Vn[0:8, kvh, :], rhs=PT[pb_][0:8, :], start=False, stop=True),
                         reads=["e_Vn", ("e_PT", pb_)], writes=[("e_po", kvh)])
                    S.op("dve", lambda e, kvh=kvh: e.reciprocal(out=rc[64:65, :], in_=p_o[kvh][64:65, 0:32]), reads=[("e_po", kvh)], writes=["e_rc"])
                    tb = it % 2
                    it += 1
                    S.op("pe", lambda e, tb=tb: e.matmul(p_t[tb][0:64, 0:32], lhsT=c.ones_f[64:65, 0:64], rhs=rc[64:65, :], start=True, stop=True),
                         reads=["e_rc", "ones_f"], writes=[("e_pt", tb)])
                    S.op("act", lambda e, tb=tb: e.copy(out=bcs[:, :], in_=p_t[tb][0:64, 0:32]), reads=[("e_pt", tb)], writes=["e_bcs"])
                    S.op("dve", lambda e, kvh=kvh: e.tensor_tensor(out=OT[:, kvh, :], in0=p_o[kvh][0:64, 0:32], in1=bcs[:, :], op=ALU.mult),
                         reads=[("e_po", kvh), "e_bcs"], writes=["e_OT"])
                    S.dma("sp", lambda e, kvh=kvh, t0=t0: e.dma_start(
                        out=c.yaT[256 * kvh:256 * kvh + 256, t0:t0 + 8].rearrange("(g d) t -> d g t", d=64),
                        in_=OT[:, kvh, :].rearrange("d (g t) -> d g t", t=8)),
                        reads=["e_OT"], writes=tkeys("yaT", t0, t0 + 8))

def phase_merge(c, layer, tiles):
    nc, S = c.nc, c.S
    S.barrier()
    TS = c.TS
    wbs = c.w_bs[layer].rearrange("(ko ki) n -> ki ko n", ki=128)
    wba = c.w_ba[layer].rearrange("(ko ki) n -> ki ko n", ki=128)
    wo = c.w_o[layer].rearrange("(ko ki) n -> ki ko n", ki=128)
    with ExitStack() as st:
        yn = sb(nc, st, "m_yn", [128, 16, TS], BF16)
        ya = sb(nc, st, "m_ya", [128, 8, TS], BF16)
        mT = sb(nc, st, "m_mT", [128, 8, TS], BF16)
        wS = [sb(nc, st, f"m_wS{i}", [128, 24, 128], F32) for i in range(2)]
        wB = [sb(nc, st, f"m_wB{i}", [128, 24, 128], BF16) for i in range(2)]
        gsg = [sb(nc, st, f"m_g{i}", [128, 2, TS], F32) for i in range(2)]
        xs = [sb(nc, st, f"m_xs{i}", [128, TS], F32) for i in range(2)]
        m1 = [sb(nc, st, f"m_m1{i}", [128, 512], F32) for i in range(2)]
        m2 = [sb(nc, st, f"m_m2{i}", [128, 512], F32) for i in range(2)]
        xo = [sb(nc, st, f"m_xo{i}", [128, 512], F32) for i in range(2)]
        p1 = [ps(nc, st, f"m_p1{i}", [128, 512], F32) for i in range(2)]
        p2 = [ps(nc, st, f"m_p2{i}", [128, 512], F32) for i in range(2)]
        p3 = [ps(nc, st, f"m_p3{i}", [128, 512], F32) for i in range(2)]
        wi_ = 0
        it = 0
        for (t0, T) in tiles:
            nh = (T + 511) // 512
            S.dma("sp", lambda e, t0=t0, T=T: e.dma_start(out=yn[:, :, 0:T], in_=c.ynT[:, t0:t0 + T].rearrange("(k p) t -> p k t", p=128)),
                  reads=tkeys("ynT", t0, t0 + T), writes=["m_yn"])
            S.dma("sp", lambda e, t0=t0, T=T: e.dma_start(out=ya[:, :, 0:T], in_=c.yaT[:, t0:t0 + T].rearrange("(k p) t -> p k t", p=128)),
                  reads=tkeys("yaT", t0, t0 + T), writes=["m_ya"])
            for cc in range(8):
                b = wi_ % 2
                wi_ += 1
                cs_ = slice(cc * 128, cc * 128 + 128)
                S.dma("sp", lambda e, b=b, cs_=cs_: e.dma_start(out=wS[b][:, 0:16, :], in_=wbs[:, :, cs_]), writes=[("m_wS", b)])
                S.dma("sp", lambda e, b=b, cs_=cs_: e.dma_start(out=wS[b][:, 16:24, :], in_=wba[:, :, cs_]), writes=[("m_wS", b)])
                S.op("pool", lambda e, b=b: e.tensor_copy(out=wB[b][:, :, :], in_=wS[b][:, :, :]), reads=[("m_wS", b)], writes=[("m_wB", b)])
                S.dma("sp", lambda e, b=b, cs_=cs_, t0=t0, T=T: e.dma_start(out=gsg[b][:, 0, 0:T], in_=c.gsT[cs_, t0:t0 + T]),
                      reads=tkeys("gsT", t0, t0 + T), writes=[("m_g", b)])
                S.dma("sp", lambda e, b=b, cs_=cs_, t0=t0, T=T: e.dma_start(out=gsg[b][:, 1, 0:T], in_=c.gaT[cs_, t0:t0 + T]),
                      reads=tkeys("gaT", t0, t0 + T), writes=[("m_g", b)])
                for h in range(nh):
                    w = min(512, T - h * 512)
                    hs = slice(h * 512, h * 512 + w)
                    pb_ = it % 2
                    it += 1
                    for ko in range(16):
                        S.op("pe", lambda e, b=b, ko=ko, hs=hs, w=w, pb_=pb_: e.matmul(p1[pb_][:, 0:w], lhsT=wB[b][:, ko, :], rhs=yn[:, ko, hs],
                                                                                      start=(ko == 0), stop=(ko == 15)),
                             reads=[("m_wB", b), "m_yn"], writes=[("m_p1", pb_)])
                    for ko in range(8):
                        S.op("pe", lambda e, b=b, ko=ko, hs=hs, w=w, pb_=pb_: e.matmul(p2[pb_][:, 0:w], lhsT=wB[b][:, 16 + ko, :], rhs=ya[:, ko, hs],
                                                                                      start=(ko == 0), stop=(ko == 7)),
                             reads=[("m_wB", b), "m_ya"], writes=[("m_p2", pb_)])
                    S.op("dve", lambda e, b=b, hs=hs, w=w, pb_=pb_: e.tensor_tensor(out=m1[pb_][:, 0:w], in0=p1[pb_][:, 0:w], in1=gsg[b][:, 0, hs], op=ALU.mult),
                         reads=[("m_p1", pb_), ("m_g", b)], writes=[("m_m1", pb_)])
                    S.op("dve", lambda e, b=b, hs=hs, w=w, pb_=pb_: e.tensor_tensor(out=m2[pb_][:, 0:w], in0=p2[pb_][:, 0:w], in1=gsg[b][:, 1, hs], op=ALU.mult),
                         reads=[("m_p2", pb_), ("m_g", b)], writes=[("m_m2", pb_)])
                    S.op("pool", lambda e, cc=cc, hs=hs, w=w, pb_=pb_: e.tensor_tensor(out=mT[:, cc, hs], in0=m1[pb_][:, 0:w], in1=m2[pb_][:, 0:w], op=ALU.add),
                         reads=[("m_m1", pb_), ("m_m2", pb_)], writes=[("m_mT", cc)])
            for c2 in range(8):
                b = wi_ % 2
                wi_ += 1
                cs_ = slice(c2 * 128, c2 * 128 + 128)
                S.dma("sp", lambda e, b=b, cs_=cs_: e.dma_start(out=wS[b][:, 0:8, :], in_=wo[:, :, cs_]), writes=[("m_wS", b)])
                S.op("pool", lambda e, b=b: e.tensor_copy(out=wB[b][:, 0:8, :], in_=wS[b][:, 0:8, :]), reads=[("m_wS", b)], writes=[("m_wB", b)])
                S.dma("sp", lambda e, b=b, cs_=cs_, t0=t0, T=T: e.dma_start(out=xs[b][:, 0:T], in_=c.xT[cs_, t0:t0 + T]),
                      reads=tkeys("xT", t0, t0 + T), writes=[("m_xs", b)])
                for h in range(nh):
                    w = min(512, T - h * 512)
                    hs = slice(h * 512, h * 512 + w)
                    pb_ = it % 2
                    it += 1
                    for cc in range(8):
                        S.op("pe", lambda e, b=b, cc=cc, hs=hs, w=w, pb_=pb_: e.matmul(p3[pb_][:, 0:w], lhsT=wB[b][:, cc, :], rhs=mT[:, cc, hs],
                                                                                      start=(cc == 0), stop=(cc == 7)),
                             reads=[("m_wB", b), ("m_mT", cc)], writes=[("m_p3", pb_)])
                    S.op("dve", lambda e, b=b, hs=hs, w=w, pb_=pb_: e.tensor_tensor(out=xo[pb_][:, 0:w], in0=p3[pb_][:, 0:w], in1=xs[b][:, hs], op=ALU.add),
                         reads=[("m_p3", pb_), ("m_xs", b)], writes=[("m_xo", pb_)])
                    a0 = t0 + h * 512
                    S.dma("sp", lambda e, cs_=cs_, a0=a0, w=w, pb_=pb_: e.dma_start(out=c.xT[cs_, a0:a0 + w], in_=xo[pb_][:, 0:w]),
                          reads=[("m_xo", pb_)], writes=tkeys("xT", a0, a0 + w))

def phase_final(c, tiles_out):
    nc, S = c.nc, c.S
    S.barrier()
    gain = c.gains[:, 6 * 8:6 * 8 + 8]
    with ExitStack() as st:
        xs = sb(nc, st, "y_xs", [128, 8, 128], F32)
        hn = sb(nc, st, "y_hn", [128, 8, 128], F32)
        yo = [sb(nc, st, f"y_o{i}", [128, D], F32) for i in range(2)]
        pt = [ps(nc, st, f"y_p{i}", [128, D], F32) for i in range(2)]
        with ExitStack() as st2:
            sq = [sb(nc, st2, f"y_sq{i}", [128, 128], F32) for i in range(2)]
            rstd = sb(nc, st2, "y_rstd", [128, 128], F32)
            pss = ps(nc, st2, "y_pss", [128, 128], F32)
            for i, (t0, n, dst) in enumerate(tiles_out):
                b = i % 2
                S.dma("sp", lambda e, t0=t0, n=n: e.dma_start(out=xs[:, :, 0:n],
                                                             in_=c.xT[:, t0:t0 + n].rearrange("(dc p) t -> p dc t", p=128)),
                      reads=[("xT", t0 // 128)], writes=["y_xs"])
                for dc in range(8):
                    sb_ = dc % 2
                    S.op("act", lambda e, sb_=sb_, dc=dc, n=n: e.activation(out=sq[sb_][:, 0:n], in_=xs[:, dc, 0:n], func=AF.Square),
                         reads=["y_xs"], writes=[("y_sq", sb_)])
                    S.op("pe", lambda e, sb_=sb_, dc=dc, n=n: e.matmul(pss[:, 0:n], lhsT=c.ones_f[:, :], rhs=sq[sb_][:, 0:n],
                                                                       start=(dc == 0), stop=(dc == 7)),
                         reads=[("y_sq", sb_), "ones_f"], writes=["y_pss"])
                S.op("act", lambda e, n=n: e.activation(out=rstd[:, 0:n], in_=pss[:, 0:n], func=AF.Sqrt, scale=1.0 / D,
                                                        bias=c.eps_t[:, 0:1]),
                     reads=["y_pss", "consts"], writes=["y_rstd"])
                S.op("dve", lambda e, n=n: e.reciprocal(out=rstd[:, 0:n], in_=rstd[:, 0:n]), reads=["y_rstd"], writes=["y_rstd"])
                for dc in range(8):
                    S.op("dve", lambda e, dc=dc, n=n: e.scalar_tensor_tensor(out=hn[:, dc, 0:n], in0=xs[:, dc, 0:n],
                                                                           scalar=gain[:, dc:dc + 1], in1=rstd[:, 0:n],
                                                                           op0=ALU.mult, op1=ALU.mult),
                         reads=["y_xs", "y_rstd", "gains"], writes=["y_hn"])
                for dc in range(8):
                    S.op("pe", lambda e, b=b, dc=dc, n=n: e.transpose(pt[b][0:n, dc * 128:(dc + 1) * 128], hn[:, dc, 0:n],
                                                                     c.ident_f[:, :]),
                         reads=["y_hn", "ident_f"], writes=[("y_p", b)])
                S.op("act", lambda e, b=b, n=n: e.copy(out=yo[b][0:n, :], in_=pt[b][0:n, :]),
                     reads=[("y_p", b)], writes=[("y_o", b)])
                S.dma("sp", lambda e, b=b, n=n, dst=dst: e.dma_start(out=dst, in_=yo[b][0:n, :]),
                      reads=[("y_o", b)], writes=[("yout", i)], is_output=True)


def setup_consts(c, st):
    nc, S = c.nc, c.S
    c.ident_f = sb(nc, st, "ident_f", [128, 128], F32)
    c.ones_f = sb(nc, st, "ones_f", [128, 128], F32)
    c.eps_t = sb(nc, st, "eps_t", [128, 1], F32)
    c.one_t = sb(nc, st, "one_t", [128, 1], F32)
    c.gains = sb(nc, st, "gains_sb", [128, 7 * 8], F32)
    c.ident_b = sb(nc, st, "ident_b", [128, 128], BF16)
    c.convw = sb(nc, st, "convw", [128, 24, 4], F32)
    c.convb = sb(nc, st, "convb", [128, 24], F32)
    c.dtb_rep = sb(nc, st, "dtb_rep", [128, 32], F32)
    c.a_rep = sb(nc, st, "a_rep", [128, 32], F32)
    c.dsk_rep = sb(nc, st, "dsk_rep", [128, 32], F32)
    c.normg_rep = sb(nc, st, "normg_rep", [128, 2048], F32)
    S.op("pool", lambda e: e.memset(c.ones_f[:, :], 1.0), writes=["ones_f"])
    S.op("pool", lambda e: e.memset(c.eps_t[:, :], EPS), writes=["consts"])
    S.op("pool", lambda e: e.memset(c.one_t[:, :], 1.0), writes=["consts"])
    S.op("pool", lambda e: e.memset(c.ident_f[:, :], 1.0), writes=["ident_f"])
    S.op("pool", lambda e: e.affine_select(out=c.ident_f[:, :], in_=c.ident_f[:, :], pattern=[[-1, 128]], base=0,
                                           channel_multiplier=1, compare_op=ALU.is_equal, fill=0.0),
         reads=["ident_f"], writes=["ident_f"])
    S.op("dve", lambda e: e.tensor_copy(out=c.ident_b[:, :], in_=c.ident_f[:, :]), reads=["ident_f"], writes=["ident_b"])
    S.dma("sp", lambda e: e.dma_start(out=c.gains[:, :].rearrange("p (g dc) -> p g dc", dc=8),
                                      in_=c.gains_d.rearrange("g (dc p) -> p g dc", p=128),
                                      allow_slow_non_contiguous=True),
          writes=["gains"])


def build(cfg):
    nc = bass.Bass("TRN2", target_bir_lowering=False)
    c = Ctx()
    c.nc = nc
    SEQ, NSEQ, DEPTH = cfg["SEQ"], cfg["NSEQ"], cfg["DEPTH"]
    NS = NSEQ * 8
    NT = SEQ + NS
    c.SEQ, c.NSEQ, c.NS, c.NT, c.DEPTH = SEQ, NSEQ, NS, NT, DEPTH
    c.NPAGES, c.NPOOL = cfg["NPAGES"], cfg["NPOOL"]
    c.TS = cfg["TS"]
    upto = cfg.get("upto", 99)

    def din(name, shape, dt=F32):
        return nc.dram_tensor(name, list(shape), dt, kind="ExternalInput").ap()

    def dout(name, shape, dt=F32):
        return nc.dram_tensor(name, list(shape), dt, kind="ExternalOutput").ap()

    def dscr(name, shape, dt=F32):
        return nc.dram_tensor(name, list(shape), dt, kind="Internal").ap()

    c.xp = din("x_prompt", [SEQ, D])
    c.xsm = din("x_sample", [NS, D])
    c.gains_d = din("gains", [7, D])
    w1d = [din(f"ffn{i + 1}_w1", [DEPTH, D, 2 * DFF]) for i in range(2)]
    w2d = [din(f"ffn{i + 1}_w2", [DEPTH, DFF, D]) for i in range(2)]
    c.w1 = [[w1d[i][l] for l in range(DEPTH)] for i in range(2)]
    c.w2 = [[w2d[i][l] for l in range(DEPTH)] for i in range(2)]
    c.w_in = din("w_in", [DEPTH, D, DPROJ])
    c.conv_w_d = din("conv_w", [DEPTH, 4, 3072])
    c.conv_b_d = din("conv_b", [DEPTH, 3072])
    c.dt_bias_d = din("dt_bias", [DEPTH, 32])
    c.a_log_d = din("a_log", [DEPTH, 32])
    c.d_skip_d = din("d_skip", [DEPTH, 32])
    c.ssd_norm_d = din("ssd_norm", [DEPTH, 2048])
    c.w_bs = din("w_branch_ssd", [DEPTH, 2048, D])
    c.w_ba = din("w_branch_attn", [DEPTH, D, D])
    c.w_o = din("w_out", [DEPTH, D, D])
    c.sconv = din("state_conv", [DEPTH, NSEQ, 3, 3072])
    c.sssm = din("state_ssm", [DEPTH, NSEQ, 32, 64, 128])
    if cfg.get("sample_dsa", False):
        c.cache_k = [din(f"cache_k{l}", [c.NPOOL * 4, 8192]) for l in range(DEPTH)]
        c.cache_v = [din(f"cache_v{l}", [c.NPOOL * 4, 8192]) for l in range(DEPTH)]
        c.cache_i = [din(f"cache_idx_k{l}", [c.NPOOL, 128 * 64]) for l in range(DEPTH)]
        c.ptab = din("page_table", [NSEQ, c.NPAGES], I32)

    c.yp = dout("y_prompt", [SEQ, D])
    c.ys = dout("y_sample", [NS, D])
    c.ok_p = dout("ok_p", [DEPTH, SEQ, 256])
    c.ov_p = dout("ov_p", [DEPTH, SEQ, 256])
    c.oi_p = dout("oi_p", [DEPTH, SEQ, 64])
    c.oh_p = dout("oh_p", [DEPTH, 32, 64, 128])
    c.oc_p = dout("oc_p", [DEPTH, 3, 3072])
    c.ok_s = dout("ok_s", [DEPTH, NS, 256])
    c.ov_s = dout("ov_s", [DEPTH, NS, 256])
    c.oi_s = dout("oi_s", [DEPTH, NS, 64])
    c.oh_s = dout("oh_s", [DEPTH, NSEQ, 32, 64, 128])
    c.oc_s = dout("oc_s", [DEPTH, NSEQ, 3, 3072])

    c.xT = dscr("xT_scratch", [D, NT])
    c.zS = dscr("zS", [NT, 2048])
    c.xbcT = dscr("xbcT", [3072, NT])
    c.dtS = dscr("dtS", [NT, 32])
    c.qT = dscr("qT", [16, 64, NT], BF16)
    c.kT = dscr("kT", [4, 64, NT], BF16)
    c.qiT = dscr("qiT", [8, 64, NT], BF16)
    c.kiT = dscr("kiT", [1, 64, NT], BF16)
    c.vB = dscr("vB", [NT, 256], BF16)
    c.wiS = dscr("wiS", [NT, 8])
    c.gsT = dscr("gsT", [D, NT])
    c.gaT = dscr("gaT", [D, NT])
    c.xS = dscr("xS", [NT, 2048])
    c.Btm = dscr("Btm", [NT, 512], BF16)
    c.BT = dscr("BT", [512, NT], BF16)
    c.CT = dscr("CT", [512, NT], BF16)
    c.ynT = dscr("ynT", [2048, NT], BF16)
    c.yaT = dscr("yaT", [D, NT], BF16)

    tiles = [(t0, min(c.TS, SEQ - t0)) for t0 in range(0, SEQ, c.TS)] + [(SEQ, NS)]
    wins = [(t0, 1, min(c.TS, SEQ - t0)) for t0 in range(0, SEQ, c.TS)] + [(SEQ, NSEQ, 8)]
    with ExitStack() as st:
        c.S = Sched(nc, st)
        setup_consts(c, st)
        phase_to_fm(c, c.xp, 0, SEQ, 0)
        phase_to_fm(c, c.xsm, 0, NS, SEQ)
        for l in range(DEPTH):
            load_layer_params(c, l)
            if upto >= 1:
                phase_ffn(c, l, 0, tiles)
            if upto >= 2:
                phase_inproj(c, l, tiles)
                phase_conv(c, l, wins)
            if upto >= 3:
                phase_ssd(c, l)
            if upto >= 4:
                phase_dsa_prompt(c, l)
            if upto >= 5 and cfg.get("sample_dsa", False):
                phase_dsa_sample(c, l)
            if upto >= 5:
                phase_merge(c, l, tiles)
                phase_ffn(c, l, 1, tiles)
        outs = [(t0, min(128, SEQ - t0), c.yp[t0:t0 + min(128, SEQ - t0), :]) for t0 in range(0, SEQ, 128)]
        outs.append((SEQ, NS, c.ys[0:NS, :]))
        phase_final(c, outs)
        c.S.finish()
        c.S.emit()
    return nc


def make_in_maps(inputs, cfg):
    SEQ, NSEQ, DEPTH = cfg["SEQ"], cfg["NSEQ"], cfg["DEPTH"]
    NS = NSEQ * 8
    f = lambda k: np.ascontiguousarray(np.asarray(inputs[k]))
    xp, xs = f("x_prompt"), f("x_sample")
    gains = np.ascontiguousarray(np.concatenate(
        [np.stack([f("ffn1_norm")[l], f("mix_norm")[l], f("ffn2_norm")[l]]) for l in range(DEPTH)]
        + [np.zeros((3, D), np.float32)] * (2 - DEPTH) + [f("final_norm")[None]], 0))
    shared = {"gains": gains}
    for k in ("ffn1_w1", "ffn1_w2", "ffn2_w1", "ffn2_w2", "w_in", "conv_w", "conv_b", "dt_bias", "a_log", "d_skip", "ssd_norm",
              "w_branch_ssd", "w_branch_attn", "w_out"):
        shared[k] = f(k)
    if cfg.get("sample_dsa", False):
        ck, cv, ci = f("cache_k"), f("cache_v"), f("cache_idx_k")
        npool = ck.shape[1]
        for l in range(DEPTH):
            shared[f"cache_k{l}"] = np.ascontiguousarray(ck[l].reshape(npool * 4, 8192))
            shared[f"cache_v{l}"] = np.ascontiguousarray(cv[l].reshape(npool * 4, 8192))
            shared[f"cache_idx_k{l}"] = np.ascontiguousarray(ci[l].reshape(npool, 128 * 64))
    sconv, sssm, pt = f("state_conv"), f("state_ssm"), f("page_table")
    in_maps = []
    for cid in range(8):
        m = dict(shared)
        m["x_prompt"] = xp[cid % xp.shape[0]]
        sl = slice(cid * NSEQ, (cid + 1) * NSEQ)
        m["x_sample"] = np.ascontiguousarray(xs[sl].reshape(NS, D))
        m["state_conv"] = np.ascontiguousarray(sconv[:, sl])
        m["state_ssm"] = np.ascontiguousarray(sssm[:, sl])
        if cfg.get("sample_dsa", False):
            m["page_table"] = np.ascontiguousarray(pt[sl]).astype(np.int32)
        in_maps.append(m)
    return in_maps


def gather_outputs(r, cfg, nb):
    SEQ, NSEQ, DEPTH = cfg["SEQ"], cfg["NSEQ"], cfg["DEPTH"]
    cat_p = lambda k, sh: np.stack([r[b][k] for b in range(nb)], 1).reshape(sh).astype(np.float32)
    cat_s = lambda k, sh: np.concatenate([r[cid][k].reshape((DEPTH, NSEQ) + r[cid][k].shape[1:][1:] if False else r[cid][k].shape) for cid in range(8)], 1)
    y_prompt = np.stack([r[b]["y_prompt"] for b in range(nb)]).astype(np.float32)
    y_sample = np.concatenate([r[cid]["y_sample"].reshape(NSEQ, 8, D) for cid in range(8)], 0).astype(np.float32)
    kp = cat_p("ok_p", (DEPTH, nb, SEQ, 4, 64))
    vp = cat_p("ov_p", (DEPTH, nb, SEQ, 4, 64))
    ip = cat_p("oi_p", (DEPTH, nb, SEQ, 64))
    hp = cat_p("oh_p", (DEPTH, nb, 32, 64, 128))
    cp = cat_p("oc_p", (DEPTH, nb, 3, 3072))
    ks = np.concatenate([r[cid]["ok_s"].reshape(DEPTH, NSEQ, 8, 4, 64) for cid in range(8)], 1).astype(np.float32)
    vs = np.concatenate([r[cid]["ov_s"].reshape(DEPTH, NSEQ, 8, 4, 64) for cid in range(8)], 1).astype(np.float32)
    is_ = np.concatenate([r[cid]["oi_s"].reshape(DEPTH, NSEQ, 8, 64) for cid in range(8)], 1).astype(np.float32)
    hs = np.concatenate([r[cid]["oh_s"] for cid in range(8)], 1).astype(np.float32)
    cs = np.concatenate([r[cid]["oc_s"] for cid in range(8)], 1).astype(np.float32)
    return (y_prompt, y_sample, kp, vp, ip, hp, cp, ks, vs, is_, hs, cs)


def kernel(**inputs):
    cfg = dict(SEQ=8192, NSEQ=4, NPAGES=128, NPOOL=5120, TS=1024, DEPTH=2, upto=5, sample_dsa=True)
    nc = build(cfg)
    in_maps = make_in_maps(inputs, cfg)
    res = run_bass_kernel_spmd(nc, in_maps, core_ids=list(range(8)))
    return gather_outputs(res.results, cfg, 2)
```
